# Optimizing a Trainium2 kernel written in Bass

```python
import math
import jax, jax.numpy as jnp
from jax import lax
import numpy as np

D_MODEL = 2048
BATCH = 1
SEQ = 8192
DEPTH = 4

HEAD_DIM = 128
MOBA_HEADS = 6
DIFF_HEADS = 6
SGU_GROUPS = 4
SGU_GROUP_DIM = HEAD_DIM
MOBA_WIDTH = MOBA_HEADS * HEAD_DIM
DIFF_WIDTH = DIFF_HEADS * HEAD_DIM
SGU_WIDTH = SGU_GROUPS * SGU_GROUP_DIM
DIFF_QK_DIM = HEAD_DIM // 2
MOBA_BLOCK = 256
MOBA_TOPK = 3
MOBA_Q_CHUNK = 64
ATTN_Q_BLOCK = 128
SGU_CHUNK = 128
D_FF = 4 * D_MODEL
ROPE_THETA = 10000.0
EPS = 1e-6
IN_WIDTH = 3 * MOBA_WIDTH + 3 * DIFF_WIDTH + 2 * SGU_WIDTH
IN_SPLITS = [MOBA_WIDTH, 2 * MOBA_WIDTH, 3 * MOBA_WIDTH,
             3 * MOBA_WIDTH + DIFF_WIDTH, 3 * MOBA_WIDTH + 2 * DIFF_WIDTH,
             3 * MOBA_WIDTH + 3 * DIFF_WIDTH, 3 * MOBA_WIDTH + 3 * DIFF_WIDTH + SGU_WIDTH]

kernel_name = 'hybrid_moba_diffattn_gmlp_trunk'


def rms_norm(x, g):
    xf = x.astype(jnp.float32)
    y = xf * lax.rsqrt(jnp.mean(xf * xf, axis=-1, keepdims=True) + EPS)
    return (y * g.astype(jnp.float32)).astype(x.dtype)


def rope_tables(seq, dim):
    inv = 1.0 / (ROPE_THETA ** (jnp.arange(0, dim, 2, dtype=jnp.float32) / dim))
    ang = jnp.arange(seq, dtype=jnp.float32)[:, None] * inv[None, :]
    return jnp.cos(ang), jnp.sin(ang)


def apply_rope(x, cos, sin):
    half = x.shape[-1] // 2
    bshape = (1, x.shape[1]) + (1,) * (x.ndim - 3) + (half,)
    c = cos.reshape(bshape)
    s = sin.reshape(bshape)
    xf = x.astype(jnp.float32)
    x1, x2 = xf[..., :half], xf[..., half:]
    return jnp.concatenate([x1 * c - x2 * s, x2 * c + x1 * s], axis=-1).astype(x.dtype)


def moba_attention(q, k, v, cos, sin):
    B, S, H, Dh = q.shape
    q = apply_rope(q, cos, sin)
    k = apply_rope(k, cos, sin)
    n_blk = -(-S // MOBA_BLOCK)
    s_pad = n_blk * MOBA_BLOCK
    pad = ((0, 0), (0, 0), (0, s_pad - S), (0, 0))
    q = jnp.pad(q.transpose(0, 2, 1, 3), pad)
    k = jnp.pad(k.transpose(0, 2, 1, 3), pad)
    v = jnp.pad(v.transpose(0, 2, 1, 3), pad)
    kb = k.reshape(B, H, n_blk, MOBA_BLOCK, Dh)
    vb = v.reshape(B, H, n_blk, MOBA_BLOCK, Dh)
    k_mean = jnp.mean(kb.astype(jnp.float32), axis=3)
    scale = Dh ** -0.5
    k_sel = min(MOBA_TOPK, n_blk)
    n_chunks = s_pad // MOBA_Q_CHUNK
    qc = q.reshape(B, H, n_chunks, MOBA_Q_CHUNK, Dh).transpose(2, 0, 1, 3, 4)
    bi = jnp.arange(B)[:, None, None, None]
    hi = jnp.arange(H)[None, :, None, None]
    blk_ids = jnp.arange(n_blk)

    def chunk_fn(args):
        q_c, c = args
        q_start = c * MOBA_Q_CHUNK
        own = q_start // MOBA_BLOCK
        gate = jnp.einsum('bhqd,bhnd->bhqn', q_c.astype(jnp.float32), k_mean)
        gate = jnp.where(blk_ids[None, None, None, :] < own, gate, -jnp.inf)
        _, sel = lax.top_k(gate, k_sel)
        sel_valid = jnp.arange(k_sel) < own
        k_g = kb[bi, hi, sel]
        v_g = vb[bi, hi, sel]
        s_sel = jnp.einsum('bhqd,bhqkld->bhqkl', q_c, k_g).astype(jnp.float32) * scale
        s_sel = jnp.where(sel_valid[None, None, None, :, None], s_sel, -jnp.inf)
        s_sel = s_sel.reshape(B, H, MOBA_Q_CHUNK, k_sel * MOBA_BLOCK)
        k_own = lax.dynamic_slice_in_dim(kb, own, 1, axis=2)[:, :, 0]
        v_own = lax.dynamic_slice_in_dim(vb, own, 1, axis=2)[:, :, 0]
        s_own = jnp.einsum('bhqd,bhld->bhql', q_c, k_own).astype(jnp.float32) * scale
        q_pos = q_start + jnp.arange(MOBA_Q_CHUNK)
        k_pos = own * MOBA_BLOCK + jnp.arange(MOBA_BLOCK)
        s_own = jnp.where(k_pos[None, :] <= q_pos[:, None], s_own, -jnp.inf)
        p = jax.nn.softmax(jnp.concatenate([s_sel, s_own], axis=-1), axis=-1).astype(v.dtype)
        p_sel = p[..., :k_sel * MOBA_BLOCK].reshape(B, H, MOBA_Q_CHUNK, k_sel, MOBA_BLOCK)
        p_own = p[..., k_sel * MOBA_BLOCK:]
        return (jnp.einsum('bhqkl,bhqkld->bhqd', p_sel, v_g)
                + jnp.einsum('bhql,bhld->bhqd', p_own, v_own))

    out = lax.map(chunk_fn, (qc, jnp.arange(n_chunks)))
    out = out.transpose(1, 2, 0, 3, 4).reshape(B, H, s_pad, Dh)[:, :, :S]
    return out.transpose(0, 2, 1, 3)


def diff_attention(q, k, v, lam_params, subln_g, lambda_init, cos, sin):
    B, S, H, _, dq = q.shape
    q = apply_rope(q, cos, sin)
    k = apply_rope(k, cos, sin)
    lp = lam_params.astype(jnp.float32)
    lam = jnp.exp(jnp.sum(lp[0] * lp[1])) - jnp.exp(jnp.sum(lp[2] * lp[3])) + lambda_init
    scale = dq ** -0.5
    nqb = S // ATTN_Q_BLOCK
    qb = q.reshape(B, nqb, ATTN_Q_BLOCK, H, 2, dq).transpose(1, 0, 2, 3, 4, 5)
    k_pos = jnp.arange(S)

    def block_fn(args):
        q_blk, i = args
        s = jnp.einsum('bqhcd,bkhcd->bhcqk', q_blk, k).astype(jnp.float32) * scale
        q_pos = i * ATTN_Q_BLOCK + jnp.arange(ATTN_Q_BLOCK)
        s = jnp.where(k_pos[None, :] <= q_pos[:, None], s, -jnp.inf)
        p = jax.nn.softmax(s, axis=-1)
        a = p[:, :, 0] - lam * p[:, :, 1]
        return jnp.einsum('bhqk,bkhd->bqhd', a.astype(v.dtype), v)

    o = lax.map(block_fn, (qb, jnp.arange(nqb)))
    o = o.transpose(1, 0, 2, 3, 4).reshape(B, S, H, v.shape[-1])
    return rms_norm(o, subln_g) * (1.0 - lambda_init)


def spatial_gating(u, v, ln_g, ln_b, w_s, b_s):
    B, S, G, C = v.shape
    vf = v.astype(jnp.float32)
    mu = jnp.mean(vf, axis=-1, keepdims=True)
    var = jnp.mean(jnp.square(vf - mu), axis=-1, keepdims=True)
    vn = ((vf - mu) * lax.rsqrt(var + EPS) * ln_g.astype(jnp.float32) + ln_b.astype(jnp.float32)).astype(v.dtype)
    nc = S // SGU_CHUNK
    vn = vn.reshape(B, nc, SGU_CHUNK, G, C)
    tri = jnp.tril(jnp.ones((SGU_CHUNK, SGU_CHUNK), dtype=bool))
    w = jnp.where(tri[None], w_s, jnp.zeros_like(w_s))
    mixed = jnp.einsum('gts,bnsgc->bntgc', w, vn) + b_s.T[None, None, :, :, None]
    return u * mixed.reshape(B, S, G, C)


def setup_inputs(seed: int = 0) -> dict:
    key = jax.random.key(seed)
    ks = jax.random.split(key, 14)
    f32 = jnp.float32

    def nrm(k, shape, scale):
        return jax.random.normal(k, shape, f32) * scale

    return {
        'x': nrm(ks[0], (BATCH, SEQ, D_MODEL), 1.0),
        'attn_norm_g': 1.0 + nrm(ks[1], (DEPTH, D_MODEL), 0.02),
        'w_in': nrm(ks[2], (DEPTH, D_MODEL, IN_WIDTH), D_MODEL ** -0.5),
        'diff_lambda': nrm(ks[3], (DEPTH, 4, DIFF_QK_DIM), 0.1),
        'diff_subln_g': 1.0 + nrm(ks[4], (DEPTH, HEAD_DIM), 0.02),
        'sgu_ln_g': 1.0 + nrm(ks[5], (DEPTH, SGU_GROUPS, SGU_GROUP_DIM), 0.02),
        'sgu_ln_b': nrm(ks[6], (DEPTH, SGU_GROUPS, SGU_GROUP_DIM), 0.02),
        'sgu_w': nrm(ks[7], (DEPTH, SGU_GROUPS, SGU_CHUNK, SGU_CHUNK), SGU_CHUNK ** -0.5),
        'sgu_b': 1.0 + nrm(ks[8], (DEPTH, SGU_GROUPS, SGU_CHUNK), 0.1),
        'w_out': nrm(ks[9], (DEPTH, D_MODEL, D_MODEL), D_MODEL ** -0.5),
        'mlp_norm_g': 1.0 + nrm(ks[10], (DEPTH, D_MODEL), 0.02),
        'w_mlp_in': nrm(ks[11], (DEPTH, D_MODEL, D_FF), D_MODEL ** -0.5),
        'w_mlp_out': nrm(ks[12], (DEPTH, D_FF, D_MODEL), D_FF ** -0.5),
        'final_norm_g': 1.0 + nrm(ks[13], (D_MODEL,), 0.02),
    }


def reference(x, attn_norm_g, w_in, diff_lambda, diff_subln_g, sgu_ln_g, sgu_ln_b,
              sgu_w, sgu_b, w_out, mlp_norm_g, w_mlp_in, w_mlp_out, final_norm_g):
    B, S, _ = x.shape
    cos_m, sin_m = rope_tables(S, HEAD_DIM)
    cos_d, sin_d = rope_tables(S, DIFF_QK_DIM)
    for l in range(DEPTH):
        lambda_init = 0.8 - 0.6 * math.exp(-0.3 * l)
        h = rms_norm(x, attn_norm_g[l])
        proj = h @ w_in[l]
        mq, mk, mv, dq, dk, dv, su, sv = jnp.split(proj, IN_SPLITS, axis=-1)
        moba_o = moba_attention(mq.reshape(B, S, MOBA_HEADS, HEAD_DIM),
                                mk.reshape(B, S, MOBA_HEADS, HEAD_DIM),
                                mv.reshape(B, S, MOBA_HEADS, HEAD_DIM), cos_m, sin_m)
        diff_o = diff_attention(dq.reshape(B, S, DIFF_HEADS, 2, DIFF_QK_DIM),
                                dk.reshape(B, S, DIFF_HEADS, 2, DIFF_QK_DIM),
                                dv.reshape(B, S, DIFF_HEADS, HEAD_DIM),
                                diff_lambda[l], diff_subln_g[l], lambda_init, cos_d, sin_d)
        sgu_o = spatial_gating(jax.nn.gelu(su).reshape(B, S, SGU_GROUPS, SGU_GROUP_DIM),
                               jax.nn.gelu(sv).reshape(B, S, SGU_GROUPS, SGU_GROUP_DIM),
                               sgu_ln_g[l], sgu_ln_b[l], sgu_w[l], sgu_b[l])
        mix = jnp.concatenate([moba_o.reshape(B, S, MOBA_WIDTH),
                               diff_o.reshape(B, S, DIFF_WIDTH),
                               sgu_o.reshape(B, S, SGU_WIDTH)], axis=-1)
        x = x + mix @ w_out[l]
        h = rms_norm(x, mlp_norm_g[l])
        x = x + jnp.square(jax.nn.relu(h @ w_mlp_in[l])) @ w_mlp_out[l]
    return rms_norm(x, final_norm_g)
```

```python
import contextlib
import math

import numpy as np

import concourse.bass as bass
import concourse.mybir as mybir
from concourse.bass_utils import run_bass_kernel_spmd

F32 = mybir.dt.float32
BF16 = mybir.dt.bfloat16
AF = mybir.ActivationFunctionType
ALU = mybir.AluOpType
AX = mybir.AxisListType

D = 2048
KD = 16
HD = 128
IN_W = 5632
OFF_MQ, OFF_MK, OFF_MV, OFF_DQ, OFF_DK, OFF_DV, OFF_SU, OFF_SV = 0, 768, 1536, 2304, 3072, 3840, 4608, 5120
EPS = 1e-6
NEG = -30000.0
NCORES = 8
NWB = 3

ENGS = ("pe", "act", "dve", "pool", "sp")


class Op:
    __slots__ = ("eng", "fn", "deps", "is_dma", "sem", "val", "inc", "sig", "gidx")

    def __init__(self, eng, fn, is_dma):
        self.eng = eng
        self.fn = fn
        self.deps = []
        self.is_dma = is_dma
        self.sem = None
        self.val = 0
        self.inc = 1
        self.sig = None
        self.gidx = -1


class Sched:
    def __init__(self, nc, stack, dummy_fns):
        self.nc = nc
        self.stack = stack
        self.order = []
        self.last_w = {}
        self.readers = {}
        self.pair_sem = {}
        self.dma_sems = {}
        self.dma_cnt = {}
        self.dummy_fns = dummy_fns

    def _pair(self, p, c):
        if (p, c) not in self.pair_sem:
            self.pair_sem[(p, c)] = self.stack.enter_context(self.nc.semaphore("s_%s_%s" % (p, c)))
        return self.pair_sem[(p, c)]

    def _dma_sem(self, key):
        if key not in self.dma_sems:
            self.dma_sems[key] = self.stack.enter_context(
                self.nc.semaphore("ds_%d" % len(self.dma_sems)))
            self.dma_cnt[key] = 0
        return self.dma_sems[key]

    def add(self, eng, fn, reads=(), writes=(), dma=None, inc=None):
        op = Op(eng, fn, dma is not None)
        op.gidx = len(self.order)
        deps = []
        for r in reads:
            w = self.last_w.get(r)
            if w is not None:
                deps.append(w)
        for k in writes:
            rd = self.readers.get(k)
            if rd and (rd[0] or rd[1]):
                deps.extend(rd[0].values())
                deps.extend(rd[1])
            else:
                w = self.last_w.get(k)
                if w is not None:
                    deps.append(w)
        best = {}
        seen = set()
        for d in deps:
            if id(d) in seen:
                continue
            seen.add(id(d))
            if d.is_dma:
                op.deps.append(d)
                continue
            if d.eng == "pe" and eng == "pe" and dma is None:
                continue
            b = best.get(d.eng)
            if b is None or b.gidx < d.gidx:
                best[d.eng] = d
        op.deps.extend(best.values())
        if dma is not None:
            op.sem = self._dma_sem(dma)
            op.inc = 16 if inc is None else inc
            self.dma_cnt[dma] += op.inc
            op.val = self.dma_cnt[dma]
        for r in reads:
            rd = self.readers.setdefault(r, [{}, []])
            if op.is_dma:
                rd[1].append(op)
            else:
                rd[0][eng] = op
        for k in writes:
            self.last_w[k] = op
            self.readers[k] = [{}, []]
        self.order.append(op)
        return op

    def emit(self, final_waits=()):
        nc = self.nc
        dependents = {}
        for op in self.order:
            for d in op.deps:
                dependents.setdefault(id(d), []).append(op)
        post = {}
        cnt = {}

        def newsig(p, c):
            cnt[(p, c)] = cnt.get((p, c), 0) + 1
            return (c, self._pair(p, c), cnt[(p, c)])

        n_dummy = 0
        for X in self.order:
            ds = dependents.get(id(X))
            if not ds:
                continue
            ds.sort(key=lambda o: o.gidx)
            by_eng = {}
            for Y in ds:
                by_eng.setdefault(Y.eng, []).append(Y)
            engs = list(by_eng)
            c1 = engs[0]
            if not X.is_dma:
                X.sig = newsig(X.eng, c1)
            if len(engs) == 1:
                continue
            if (not X.is_dma) and X.eng in ("act", "dve", "pool"):
                for c2 in engs[1:]:
                    R = Op(X.eng, self.dummy_fns[X.eng], False)
                    R.gidx = X.gidx
                    R.sig = newsig(X.eng, c2)
                    post.setdefault(id(X), []).append(R)
                    n_dummy += 1
                    for Y in by_eng[c2]:
                        Y.deps = [R if d is X else d for d in Y.deps]
            else:
                Y1 = by_eng[c1][0]
                for c2 in engs[1:]:
                    for Y in by_eng[c2]:
                        Y.deps = [Y1 if d is X else d for d in Y.deps]
                        dependents.setdefault(id(Y1), []).append(Y)
        self.n_dummy = n_dummy
        queues = {e: [] for e in ENGS}
        for op in self.order:
            queues[op.eng].append(op)
            queues[op.eng].extend(post.get(id(op), ()))
        names = {"pe": "tensor", "act": "scalar", "dve": "vector", "pool": "gpsimd", "sp": "sync"}
        with nc.Block() as block:
            for e in ENGS:
                ops = queues[e]
                fw = final_waits if e == "sp" else ()

                def body(engine, ops=ops, fw=fw, e=e):
                    waited = {}
                    for op in ops:
                        for d in op.deps:
                            if d.is_dma:
                                sem, val = d.sem, d.val
                            else:
                                assert d.sig is not None and d.sig[0] == e, (d.eng, e, d.sig)
                                sem, val = d.sig[1], d.sig[2]
                            key = id(sem)
                            if waited.get(key, 0) >= val:
                                continue
                            engine.wait_ge(sem, val)
                            waited[key] = val
                        ins = op.fn(engine)
                        if op.is_dma:
                            ins.then_inc(op.sem, op.inc)
                        elif op.sig is not None:
                            ins.then_inc(op.sig[1], 1)
                    if fw:
                        for k, sem in self.dma_sems.items():
                            engine.wait_ge(sem, self.dma_cnt[k])

                getattr(block, names[e])(body)


def host_tables(c, S_seq):
    BLK = S_seq // 16
    T = 2 * BLK
    NT = T // 128
    MB = BLK // 256
    NBL = 2 * MB
    NMAIN = 8 * NBL
    NBX = NMAIN + NBL
    gblk = [c, 15 - c]
    pos = np.concatenate([np.arange(BLK) + gblk[0] * BLK, np.arange(BLK) + gblk[1] * BLK]).astype(np.float32)
    rope = np.zeros((4, 128, T), np.float32)
    inv64 = (1.0 / (10000.0 ** (np.arange(0, 128, 2, dtype=np.float32) / 128))).astype(np.float32)
    inv32 = (1.0 / (10000.0 ** (np.arange(0, 64, 2, dtype=np.float32) / 64))).astype(np.float32)
    angM = pos[None, :] * inv64[:, None]
    angD = pos[None, :] * inv32[:, None]
    rope[0, 0:64] = np.cos(angM)
    rope[0, 64:128] = np.cos(angM)
    rope[1, 0:64] = -np.sin(angM)
    rope[1, 64:128] = np.sin(angM)
    for c2 in range(2):
        b = c2 * 64
        rope[2, b:b + 32] = np.cos(angD)
        rope[2, b + 32:b + 64] = np.cos(angD)
        rope[3, b:b + 32] = -np.sin(angD)
        rope[3, b + 32:b + 64] = np.sin(angD)
    slotbias = np.zeros((128, 16), np.float32)
    for r in range(8):
        slotbias[:, r] = 0.0 if r < c else NEG
        slotbias[:, 8 + r] = 0.0 if r > c else NEG
    gm_add = np.zeros((128, NT, NBX), np.float32)
    Pt = np.zeros((128, NT, NBX), np.float32)
    Qt = np.full((128, NT, NBX), NEG, np.float32)
    for i in range(NT):
        hq = (i * 128) // BLK
        tl = i * 128 + np.arange(128)
        gpos = gblk[hq] * BLK + (tl % BLK)
        qm = gpos // 256
        for r in range(8):
            for ab in range(2):
                for m in range(MB):
                    n = r * NBL + ab * MB + m
                    g = (r if ab == 0 else 15 - r) * MB + m
                    past = g < qm
                    gm_add[:, i, n] = np.where(past, 0.0, -1e30)
                    if r == c and ab == hq:
                        continue
                    Pt[:, i, n] = np.where(past, -NEG, 0.0)
        for ab in range(2):
            for m in range(MB):
                n = NMAIN + ab * MB + m
                if ab != hq:
                    continue
                g = gblk[ab] * MB + m
                Pt[:, i, n] = np.where(g < qm, -NEG, 0.0)
                Qt[:, i, n] = np.where(g == qm, 0.0, NEG)
    mobatab = np.stack([gm_add.reshape(128, -1), Pt.reshape(128, -1), Qt.reshape(128, -1)])
    return rope, slotbias, mobatab


def const_mats():
    ident = np.eye(128, dtype=np.float32)
    permM = np.zeros((128, 128), np.float32)
    permD = np.zeros((128, 128), np.float32)
    for m in range(128):
        permM[(m + 64) % 128, m] = 1.0
        permD[(m // 64) * 64 + ((m % 64) + 32) % 64, m] = 1.0
    tri = np.triu(np.ones((128, 128), np.float32))
    return np.stack([ident, permM, permD, tri])


def build_program(S_seq, depth, d_ff, do_final=True, lam_base=0, stop=99):
    BLK = S_seq // 16
    T = 2 * BLK
    NT = T // 128
    KTB = BLK // 128
    MB = BLK // 256
    NBL = 2 * MB
    NMAIN = 8 * NBL
    NBX = NMAIN + NBL
    NFF = d_ff // 128
    FFG = 16
    NG = NFF // FFG
    assert NFF % FFG == 0

    import os as _os
    KSUB = int(_os.environ.get("KSUB", "0"))
    nc = bass.Bass("TRN2", target_bir_lowering=False)

    def din(name, shape, dt=F32):
        return nc.dram_tensor(name, list(shape), dt, kind="ExternalInput").ap()

    x_d = din("x", [T, D])
    ang_d = din("attn_norm_g", [depth, D])
    win_d = din("w_in", [depth, D, IN_W])
    dl_d = din("diff_lambda", [depth, 4, 64])
    dsg_d = din("diff_subln_g", [depth, 128])
    slg_d = din("sgu_ln_g", [depth, 4, 128])
    slb_d = din("sgu_ln_b", [depth, 4, 128])
    sw_d = din("sgu_w", [depth, 4, 128, 128])
    sb_d = din("sgu_b", [depth, 4, 128])
    wout_d = din("w_out", [depth, D, D])
    mng_d = din("mlp_norm_g", [depth, D])
    wmi_d = din("w_mlp_in", [depth, D, d_ff])
    wmo_d = din("w_mlp_out", [depth, d_ff, D])
    fng_d = din("final_norm_g", [D])
    rope_d = din("rope", [4, 128, T])
    cmat_d = din("cmat", [4, 128, 128])
    sbias_d = din("slotbias", [128, 16])
    mtab_d = din("mobatab", [3, 128, NT * NBX])
    out_d = nc.dram_tensor("out", [T, D], F32, kind="ExternalOutput").ap()

    KMR = 6
    AGR = 3072 + KMR
    assert KMR * T // 2 == 128 * 6 * NBL
    ag_loc = nc.dram_tensor("ag_loc", [AGR, T], BF16)
    ag_all = nc.dram_tensor("ag_all", [8 * AGR, T], BF16)
    q_loc = nc.dram_tensor("q_loc", [1536, T], BF16)
    kt_loc = ag_loc
    v_loc3 = ag_loc[1536:3072, :].rearrange("r c -> (r c)").rearrange("(h t d) -> h t d", h=12, t=T)
    km_loc2 = ag_loc[3072:AGR, :].rearrange("r c -> (r c)").bitcast(F32).rearrange("(p f) -> p f", p=128)
    ag_all3 = ag_all.ap().rearrange("(r a) c -> r a c", r=8)

    with contextlib.ExitStack() as st:
        dumt = st.enter_context(nc.sbuf_tensor("dumt", [128, 392], F32))
        dctr = {"act": 0, "dve": 0, "pool": 0}

        def dcol(e):
            dctr[e] += 1
            base = {"act": 0, "dve": 128, "pool": 256}[e]
            k = base + dctr[e] % 128
            return dumt[0:1, k:k + 1]

        S = Sched(nc, st, {
            "act": lambda a: a.activation(out=dcol("act"), in_=dumt[0:1, 388:389], func=AF.Copy),
            "dve": lambda v: v.memset(dcol("dve"), 0.0),
            "pool": lambda g: g.memset(dcol("pool"), 0.0),
        })

        def sb(name, shape, dt):
            return st.enter_context(nc.sbuf_tensor(name, list(shape), dt))

        def ps(name, shape, dt):
            return st.enter_context(nc.psum_tensor(name, list(shape), dt))

        xres = sb("xres", [128, NT, D], F32)
        A32 = sb("A32", [128, KD * T], BF16)
        M32 = sb("M32", [128, NT * D], BF16)
        Wb = [sb("Wb%d" % b, [128, KD * 256], BF16) for b in range(NWB)]
        ropeT = sb("ropeT", [128, 4, T], BF16)
        vn = sb("vn", [128, NT, 512], BF16)
        SCRB = 19456
        SCR = sb("SCR", [128, SCRB // 2], BF16)
        cmat = sb("cmat_sb", [128, 4, 128], BF16)
        sbias = sb("sbias_sb", [128, 16], F32)
        mtab = sb("mtab_sb", [128, 3, NT * NBX], F32)
        gcols = sb("gcols", [128, 2 * depth, KD], F32)
        lam = sb("lam", [128, 8 * depth], F32)
        gsub = sb("gsub", [128, depth, 128], F32)
        lng = sb("lng", [128, 512], F32)
        lnb = sb("lnb", [128, 512], F32)
        wsT = sb("wsT", [128, 4, 128], BF16)
        bs_sb = sb("bs_sb", [128, 4], F32)
        biasT = sb("biasT", [64, T], BF16)
        kmg = sb("kmg", [128, 8, 6 * NBL], F32)
        kmb = sb("kmb", [128, 6, NBX], BF16)
        kmst = sb("kmst", [128, 6, NBL], F32)
        stat = sb("stat", [128, 192], F32)
        zt = sb("zt", [128, 130], BF16)
        ssq = sb("ssq", [128, NT], F32)
        rstd = sb("rstd", [128, NT], F32)

        ident = cmat[:, 0, :]
        permM = cmat[:, 1, :]
        permD = cmat[:, 2, :]
        tri = cmat[:, 3, :]

        PS = [ps("PS%d" % b, [128, 512], F32) for b in range(8)]

        def pk(bank, c0=0, n=512):
            return [("P", bank)]

        def scr(off, shape, dt):
            esz = 4 if dt == F32 else 2
            n = 1
            for s_ in shape[1:]:
                n *= s_
            nb = n * esz
            assert off % 4 == 0 and off + nb <= SCRB, (off, nb)
            flat = SCR[0:shape[0], off // 2:(off + nb) // 2]
            if dt == F32:
                flat = flat.bitcast(F32)
            if len(shape) == 3:
                flat = flat.rearrange("p (a b) -> p a b", a=shape[1])
            keys = [("SCR", k) for k in range(off // 512, (off + nb - 1) // 512 + 1)]
            return flat, keys

        def akeys(kt, t0, n):
            return [("A", kt, i) for i in range(t0 // 128, (t0 + n - 1) // 128 + 1)]

        def mkeys(e0, n):
            return [("M", e) for e in range(e0 // 1024, (e0 + n - 1) // 1024 + 1)]

        def Aap(kt, t0, n):
            return A32[:, kt * T + t0: kt * T + t0 + n]

        wplan = []
        for l in range(depth):
            for c0 in (list(range(OFF_MK, OFF_MK + 768, 256)) + list(range(OFF_DK, OFF_DK + 768, 256))
                       + list(range(OFF_MV, OFF_MV + 768, 256)) + list(range(OFF_DV, OFF_DV + 768, 256))
                       + list(range(OFF_SV, OFF_SV + 512, 256)) + list(range(OFF_SU, OFF_SU + 512, 256))
                       + list(range(OFF_DQ, OFF_DQ + 768, 256)) + list(range(OFF_MQ, OFF_MQ + 768, 256))):
                wplan.append(("in", l, c0))
            for c0 in range(0, D, 256):
                wplan.append(("out", l, c0))
            for g in range(NG):
                for j in range(FFG // 2):
                    wplan.append(("mi", l, (g * FFG + 2 * j) * 128))
                for c0 in range(0, D, 256):
                    wplan.append(("mo", l, g, c0))
        wstate = {"dma": 0, "use": 0}

        def w_src(desc):
            kind = desc[0]
            if kind == "in":
                return win_d[desc[1]].rearrange("(k p) n -> p k n", p=128)[:, :, desc[2]:desc[2] + 256]
            if kind == "out":
                return wout_d[desc[1]].rearrange("(k p) n -> p k n", p=128)[:, :, desc[2]:desc[2] + 256]
            if kind == "mi":
                return wmi_d[desc[1]].rearrange("(k p) n -> p k n", p=128)[:, :, desc[2]:desc[2] + 256]
            g, c0 = desc[2], desc[3]
            return wmo_d[desc[1]].rearrange("(k p) n -> p k n", p=128)[:, g * FFG:(g + 1) * FFG, c0:c0 + 256]

        def w_issue():
            n = wstate["dma"]
            if n >= len(wplan):
                return
            b = n % NWB
            src = w_src(wplan[n])
            dst = Wb[b][:].rearrange("p (k n) -> p k n", k=KD)
            S.add("pool", lambda g, dst=dst, src=src: g.dma_start(out=dst, in_=src),
                  writes=[("W", b)], dma=("W", b))
            wstate["dma"] = n + 1

        def w_acquire(desc):
            n = wstate["use"]
            assert wplan[n] == desc, (wplan[n], desc)
            while wstate["dma"] < min(len(wplan), n + NWB):
                w_issue()
            wstate["use"] = n + 1
            b = n % NWB
            return Wb[b][:].rearrange("p (k n) -> p k n", k=KD), ("W", b)

        lp, lpk = scr(0, [128, depth * 256], F32)
        S.add("dve", lambda v: v.memset(dumt[:], 0.0), writes=["dumt"])
        for i in range(NT):
            S.add("sp", lambda q, i=i: q.dma_start(out=xres[:, i, :], in_=x_d[i * 128:(i + 1) * 128, :]),
                  writes=[("x", i)], dma=("xld", i))
        S.add("pool", lambda g: g.dma_start(out=cmat[:], in_=cmat_d.rearrange("a p n -> p a n")),
              writes=["cmat"], dma="cmat")
        S.add("pool", lambda g: g.dma_start(out=ropeT[:], in_=rope_d.rearrange("a p n -> p a n")),
              writes=["rope"], dma="rope")
        S.add("sp", lambda q: q.dma_start(out=sbias[:], in_=sbias_d), writes=["sbias"], dma="sbias")
        S.add("sp", lambda q: q.dma_start(out=mtab[:], in_=mtab_d.rearrange("a p n -> p a n")),
              writes=["mtab"], dma="mtab")

        def ld_gcols(q):
            with nc.allow_non_contiguous_dma(reason="tiny strided gain load"):
                return q.dma_start(out=gcols[:, 0:depth, :], in_=ang_d.rearrange("l (k p) -> p l k", p=128))

        def ld_gcols2(q):
            with nc.allow_non_contiguous_dma(reason="tiny strided gain load"):
                return q.dma_start(out=gcols[:, depth:2 * depth, :], in_=mng_d.rearrange("l (k p) -> p l k", p=128))

        S.add("sp", ld_gcols, writes=["gcols0"], dma="gcols0")
        S.add("sp", ld_gcols2, writes=["gcols1"], dma="gcols1")
        S.add("sp", lambda q: q.dma_start(out=lp, in_=dl_d.rearrange("l a b -> (l a b)").partition_broadcast(128)),
              writes=lpk, dma="lp")
        S.add("sp", lambda q: q.dma_start(out=gsub[:].rearrange("p l n -> p (l n)"),
                                          in_=dsg_d.rearrange("l n -> (l n)").partition_broadcast(128)),
              writes=["gsub"], dma="gsub")
        for l in range(depth):
            lam_init = 0.8 - 0.6 * math.exp(-0.3 * (l + lam_base))
            for pr in range(2):
                a0 = l * 256 + pr * 128
                S.add("dve", lambda v, a0=a0, l=l, pr=pr: v.scalar_tensor_tensor(
                    out=stat[:, 128:192], in0=lp[:, a0:a0 + 64], scalar=1.0, in1=lp[:, a0 + 64:a0 + 128],
                    op0=ALU.mult, op1=ALU.mult, accum_out=lam[:, 4 * depth + 2 * l + pr:4 * depth + 2 * l + pr + 1]),
                    reads=lpk, writes=["stat", ("lam", l)])
            S.add("act", lambda a, l=l: a.activation(out=lam[:, 6 * depth + 2 * l:6 * depth + 2 * l + 2],
                                                     in_=lam[:, 4 * depth + 2 * l:4 * depth + 2 * l + 2], func=AF.Exp),
                  reads=[("lam", l)], writes=[("lam", l)])
            S.add("dve", lambda v, l=l, li=lam_init: v.scalar_tensor_tensor(
                out=lam[:, l:l + 1], in0=lam[:, 6 * depth + 2 * l + 1:6 * depth + 2 * l + 2], scalar=-li,
                in1=lam[:, 6 * depth + 2 * l:6 * depth + 2 * l + 1], op0=ALU.add, op1=ALU.subtract),
                reads=[("lam", l)], writes=[("lam", l)])
            S.add("dve", lambda v, l=l, li=lam_init: v.tensor_scalar(
                out=gsub[:, l, :], in0=gsub[:, l, :], scalar1=1.0 - li, scalar2=None, op0=ALU.mult),
                reads=["gsub"], writes=["gsub"])

        tp_state = {"g": 0}

        def rmsnorm_to_A(gi):
            for i in range(NT):
                hn, hnk = scr(0, [128, D], BF16)
                S.add("act", lambda a, i=i, hn=hn: a.activation(out=hn, in_=xres[:, i, :], func=AF.Square,
                                                                 accum_out=ssq[:, i:i + 1]),
                      reads=[("x", i)], writes=hnk + [("ssq", i)])
                S.add("act", lambda a, i=i: a.activation(out=rstd[:, i:i + 1], in_=ssq[:, i:i + 1], func=AF.Ln,
                                                         scale=1.0 / D, bias=EPS),
                      reads=[("ssq", i)], writes=[("rstd", i)])
                S.add("act", lambda a, i=i: a.activation(out=rstd[:, i:i + 1], in_=rstd[:, i:i + 1], func=AF.Exp,
                                                         scale=-0.5),
                      reads=[("rstd", i)], writes=[("rstd", i)])
                S.add("dve", lambda v, i=i, hn=hn: v.tensor_scalar(out=hn, in0=xres[:, i, :], scalar1=rstd[:, i:i + 1],
                                                                   scalar2=None, op0=ALU.mult),
                      reads=[("x", i), ("rstd", i)], writes=hnk)
                for k0 in range(0, KD, 4):
                    g = tp_state["g"] % 2
                    tp_state["g"] += 1
                    pv = PS[6 + g][:].bitcast(BF16)
                    for kk in range(4):
                        kt = k0 + kk
                        S.add("pe", lambda t, kt=kt, kk=kk, hn=hn, pv=pv: t.transpose(
                            out=pv[:, kk * 128:(kk + 1) * 128],
                            in_=hn[:, kt * 128:(kt + 1) * 128], identity=ident),
                            reads=hnk + ["cmat"], writes=pk(6 + g))
                    for kk in range(4):
                        kt = k0 + kk
                        src = pv[:, kk * 128:(kk + 1) * 128]
                        dst = Aap(kt, i * 128, 128)
                        if g == 0:
                            S.add("act", lambda a, src=src, dst=dst, kt=kt: a.activation(
                                out=dst, in_=src, func=AF.Copy, scale=gcols[:, gi, kt:kt + 1]),
                                reads=pk(6 + g) + ["gcols0", "gcols1"],
                                writes=akeys(kt, i * 128, 128))
                        else:
                            S.add("dve", lambda v, src=src, dst=dst, kt=kt: v.tensor_scalar(
                                out=dst, in0=src, scalar1=gcols[:, gi, kt:kt + 1], scalar2=None, op0=ALU.mult),
                                reads=pk(6 + g) + ["gcols0", "gcols1"],
                                writes=akeys(kt, i * 128, 128))

        def mix_to_A():
            for i in range(NT):
                for k0 in range(0, KD, 4):
                    g = tp_state["g"] % 2
                    tp_state["g"] += 1
                    pv = PS[6 + g][:].bitcast(BF16)
                    for kk in range(4):
                        kt = k0 + kk
                        e0 = i * D + kt * 128
                        S.add("pe", lambda t, kk=kk, e0=e0, pv=pv: t.transpose(
                            out=pv[:, kk * 128:(kk + 1) * 128],
                            in_=M32[:, e0:e0 + 128], identity=ident),
                            reads=mkeys(e0, 128) + ["cmat"], writes=pk(6 + g))
                    for kk in range(4):
                        kt = k0 + kk
                        src = pv[:, kk * 128:(kk + 1) * 128]
                        dst = Aap(kt, i * 128, 128)
                        eng = "act" if g == 0 else "dve"
                        if eng == "act":
                            S.add("act", lambda a, src=src, dst=dst: a.activation(out=dst, in_=src, func=AF.Copy),
                                  reads=pk(6 + g), writes=akeys(kt, i * 128, 128))
                        else:
                            S.add("dve", lambda v, src=src, dst=dst: v.tensor_copy(out=dst, in_=src),
                                  reads=pk(6 + g), writes=akeys(kt, i * 128, 128))

        fm_state = {"n": 0, "sw": 0, "tm": 0, "stg": 0}

        def fm_matmuls(W, wkey, ch, half):
            bank = fm_state["n"] % 4
            fm_state["n"] += 1
            for kt in range(KD):
                S.add("pe", lambda t, kt=kt, bank=bank: t.matmul(
                    PS[bank][:, 0:BLK], lhsT=W[:, kt, ch * 128:(ch + 1) * 128], rhs=Aap(kt, half * BLK, BLK),
                    start=(kt == 0), stop=(kt == KD - 1)),
                    reads=[wkey] + akeys(kt, half * BLK, BLK), writes=pk(bank, 0, BLK))
            return bank

        def tm_matmuls(W, wkey, i, lhs_fn, lhs_keys_fn, nk=KD):
            r = fm_state["tm"] % 3
            fm_state["tm"] += 1
            bank, c0 = 4 + r, 0
            for kt in range(nk):
                S.add("pe", lambda t, kt=kt, bank=bank, c0=c0: t.matmul(
                    PS[bank][:, c0:c0 + 256], lhsT=lhs_fn(kt), rhs=W[:, kt, :],
                    start=(kt == 0), stop=(kt == nk - 1)),
                    reads=[wkey] + lhs_keys_fn(kt), writes=pk(bank, c0, 256))
            return bank, c0

        def rope_chunk(bank, half, is_moba, want_f32):
            j = fm_state["sw"] % 2
            fm_state["sw"] += 1
            xb, xbk = scr(j * 1024, [128, BLK], BF16)
            t1, t1k = scr(2048 + j * 2048, [128, BLK], F32)
            t2, t2k = scr(6144 + j * 2048, [128, BLK], F32)
            rb, rbk = scr(10240 + j * 1024, [128, BLK], BF16)
            swb = 4 + j
            perm = permM if is_moba else permD
            ci, si = (0, 1) if is_moba else (2, 3)
            t0 = half * BLK
            S.add("dve", lambda v: v.tensor_copy(out=xb, in_=PS[bank][:, 0:BLK]),
                  reads=pk(bank, 0, BLK), writes=xbk)
            S.add("pe", lambda t: t.matmul(PS[swb][:, 0:BLK], lhsT=perm, rhs=xb, start=True, stop=True),
                  reads=xbk + ["cmat"], writes=pk(swb, 0, BLK))
            S.add("dve", lambda v: v.tensor_tensor(out=t1, in0=PS[bank][:, 0:BLK], in1=ropeT[:, ci, t0:t0 + BLK],
                                                   op=ALU.mult),
                  reads=pk(bank, 0, BLK) + ["rope"], writes=t1k)
            S.add("dve", lambda v: v.tensor_tensor(out=t2, in0=PS[swb][:, 0:BLK], in1=ropeT[:, si, t0:t0 + BLK],
                                                   op=ALU.mult),
                  reads=pk(swb, 0, BLK) + ["rope"], writes=t2k)
            if want_f32:
                S.add("pool", lambda g: g.tensor_tensor(out=t1, in0=t1, in1=t2, op=ALU.add),
                      reads=t1k + t2k, writes=t1k)
                S.add("pool", lambda g: g.tensor_copy(out=rb, in_=t1), reads=t1k, writes=rbk)
            else:
                S.add("pool", lambda g: g.tensor_tensor(out=rb, in0=t1, in1=t2, op=ALU.add),
                      reads=t1k + t2k, writes=rbk)
            return rb, rbk, t1, t1k, j

        def layer(l):
            S.add("sp", lambda q: q.dma_start(out=lng[:], in_=slg_d[l].rearrange("g n -> (g n)").partition_broadcast(128)),
                  writes=["lng"], dma="lng")
            S.add("sp", lambda q: q.dma_start(out=lnb[:], in_=slb_d[l].rearrange("g n -> (g n)").partition_broadcast(128)),
                  writes=["lnb"], dma="lnb")

            def ld_bs(q):
                with nc.allow_non_contiguous_dma(reason="tiny"):
                    return q.dma_start(out=bs_sb[:], in_=sb_d[l].rearrange("g t -> t g"))
            S.add("sp", ld_bs, writes=["bs"], dma="bs")
            wst, wstk = scr(16384, [128, 4, 128], BF16)
            S.add("pool", lambda g: g.dma_start(out=wst, in_=sw_d[l].rearrange("g t s -> t g s")),
                  writes=wstk, dma="wst")
            pv = PS[7][:].bitcast(BF16)
            for g4 in range(4):
                S.add("pe", lambda t, g4=g4: t.transpose(out=pv[:, g4 * 128:(g4 + 1) * 128], in_=wst[:, g4, :],
                                                          identity=ident),
                      reads=wstk + ["cmat"], writes=pk(7))
            for g4 in range(4):
                S.add("dve", lambda v, g4=g4: v.tensor_tensor(out=wsT[:, g4, :], in0=pv[:, g4 * 128:(g4 + 1) * 128],
                                                             in1=tri, op=ALU.mult),
                      reads=pk(7) + ["cmat"], writes=["wsT"])

            if stop <= 0:
                return
            rmsnorm_to_A(l)
            if stop <= 1:
                return

            def store_fm(rb, rbk, j, dram, dname, H, half):
                S.add("sp", lambda q: q.dma_start(out=dram[H * 128:(H + 1) * 128, half * BLK:(half + 1) * BLK], in_=rb),
                      reads=rbk, writes=[(dname, H, half)], dma=("rb", j))

            for base, is_moba in ((OFF_MK, True), (OFF_DK, False)):
                for c0 in range(base, base + 768, 256):
                    W, wkey = w_acquire(("in", l, c0))
                    for ch in range(2):
                        hh = (c0 - base) // 128 + ch
                        H = hh if is_moba else 6 + hh
                        for half in range(2):
                            bank = fm_matmuls(W, wkey, ch, half)
                            if KSUB == 1:
                                continue
                            rb, rbk, t1, t1k, j = rope_chunk(bank, half, is_moba, want_f32=is_moba)
                            if KSUB == 2:
                                continue
                            if is_moba:
                                S.add("dve", lambda v, t1=t1, hh=hh, half=half: v.tensor_reduce(
                                    out=kmst[:, hh, half * MB:(half + 1) * MB],
                                    in_=t1.rearrange("p (a b) -> p a b", a=MB), axis=AX.X, op=ALU.add),
                                    reads=t1k, writes=["kmst"])
                            store_fm(rb, rbk, j, kt_loc, "kt_loc", H, half)
                if is_moba and KSUB in (0, 4):
                    S.add("dve", lambda v: v.tensor_scalar(out=kmst[:], in0=kmst[:], scalar1=1.0 / 256, scalar2=None,
                                                           op0=ALU.mult),
                          reads=["kmst"], writes=["kmst"])
                    S.add("sp", lambda q: q.dma_start(out=km_loc2, in_=kmst[:].rearrange("p h b -> p (h b)")),
                          reads=["kmst"], writes=["km_loc"], dma="km_loc")
            if stop <= 2:
                return
            def lhsA(i):
                return (lambda kt: Aap(kt, i * 128, 128)), (lambda kt: akeys(kt, i * 128, 128))

            for base, hoff in ((OFF_MV, 0), (OFF_DV, 6)):
                for c0 in range(base, base + 768, 256):
                    W, wkey = w_acquire(("in", l, c0))
                    h0 = hoff + (c0 - base) // 128
                    for i in range(NT):
                        lf, lk = lhsA(i)
                        bank, pc = tm_matmuls(W, wkey, i, lf, lk)
                        j = fm_state["stg"] % 2
                        fm_state["stg"] += 1
                        vst, vstk = scr(12288 + j * 512, [128, 256], BF16)
                        S.add("act", lambda a, bank=bank, pc=pc, vst=vst: a.activation(
                            out=vst, in_=PS[bank][:, pc:pc + 256], func=AF.Copy),
                            reads=pk(bank, pc, 256), writes=vstk)
                        dst = v_loc3[h0:h0 + 2, i * 128:(i + 1) * 128, :].rearrange("h t d -> t h d")
                        S.add("sp", lambda q, dst=dst, vst=vst: q.dma_start(
                            out=dst, in_=vst.rearrange("p (h d) -> p h d", h=2)),
                            reads=vstk, writes=[("v_loc", h0, i), ("v_loc", h0 + 1, i)], dma=("vst", j))
            S.add("pool", lambda g: g.collective_compute(
                "AllGather", ALU.bypass, replica_groups=[list(range(NCORES))],
                ins=[ag_loc.ap().opt()], outs=[ag_all.ap().opt()]),
                reads=[("v_loc", H, i) for H in range(12) for i in range(NT)] + ["km_loc"]
                + [("kt_loc", H, hf) for H in range(12) for hf in range(2)], writes=["ag_all"], dma="cc_ag", inc=1)
            km_src = ag_all3[:, 3072:AGR, :].rearrange("r a c -> r (a c)").bitcast(F32).rearrange("r (p f) -> p r f", p=128)
            S.add("sp", lambda q: q.dma_start(out=kmg[:], in_=km_src),
                  reads=["ag_all"], writes=["kmg"], dma="kmg")
            for h in range(6):
                S.add("dve", lambda v, h=h: v.tensor_copy(
                    out=kmb[:, h, 0:NMAIN].rearrange("p (r b) -> p r b", b=NBL),
                    in_=kmg[:, :, h * NBL:(h + 1) * NBL]),
                    reads=["kmg"], writes=["kmb"])
            S.add("dve", lambda v: v.tensor_copy(out=kmb[:, :, NMAIN:NBX], in_=kmst[:]),
                  reads=["kmst"], writes=["kmb"])

            if stop <= 3:
                return
            for c0 in range(OFF_SV, OFF_SV + 512, 256):
                W, wkey = w_acquire(("in", l, c0))
                g0 = (c0 - OFF_SV) // 128
                for i in range(NT):
                    lf, lk = lhsA(i)
                    bank, pc = tm_matmuls(W, wkey, i, lf, lk)
                    j = fm_state["stg"] % 2
                    fm_state["stg"] += 1
                    gv, gvk = scr(13312 + j * 1024, [128, 256], F32)
                    S.add("act", lambda a, bank=bank, pc=pc, gv=gv: a.activation(
                        out=gv, in_=PS[bank][:, pc:pc + 256], func=AF.Gelu_apprx_tanh),
                        reads=pk(bank, pc, 256), writes=gvk)
                    sk = ("stat", j)
                    so = j * 32
                    for gg in range(2):
                        S.add("dve", lambda v, gv=gv, gg=gg, so=so: v.bn_stats(
                            out=stat[:, so + gg * 6: so + gg * 6 + 6], in_=gv[:, gg * 128:(gg + 1) * 128]),
                            reads=gvk, writes=[sk])
                        S.add("dve", lambda v, gg=gg, so=so: v.bn_aggr(
                            out=stat[:, so + 12 + gg * 2: so + 14 + gg * 2], in_=stat[:, so + gg * 6: so + gg * 6 + 6]),
                            reads=[sk], writes=[sk])
                    varv = stat[:, so + 12: so + 16].rearrange("p (g t) -> p g t", t=2)[:, :, 1:2]
                    rsv = stat[:, so + 16: so + 18].rearrange("p (g t) -> p g t", t=1)
                    S.add("act", lambda a, varv=varv, rsv=rsv: a.activation(out=rsv, in_=varv, func=AF.Ln, bias=EPS),
                          reads=[sk], writes=[sk])
                    S.add("act", lambda a, rsv=rsv: a.activation(out=rsv, in_=rsv, func=AF.Exp, scale=-0.5),
                          reads=[sk], writes=[sk])
                    for gg in range(2):
                        S.add("dve", lambda v, gv=gv, gg=gg, so=so: v.tensor_scalar(
                            out=gv[:, gg * 128:(gg + 1) * 128], in0=gv[:, gg * 128:(gg + 1) * 128],
                            scalar1=stat[:, so + 12 + gg * 2: so + 13 + gg * 2],
                            scalar2=stat[:, so + 16 + gg: so + 17 + gg], op0=ALU.subtract, op1=ALU.mult),
                            reads=gvk + [sk], writes=gvk)
                    S.add("dve", lambda v, gv=gv, g0=g0: v.tensor_tensor(
                        out=gv, in0=gv, in1=lng[:, g0 * 128: g0 * 128 + 256], op=ALU.mult),
                        reads=gvk + ["lng"], writes=gvk)
                    S.add("pool", lambda g, gv=gv, g0=g0, i=i: g.tensor_tensor(
                        out=vn[:, i, g0 * 128: g0 * 128 + 256], in0=gv, in1=lnb[:, g0 * 128: g0 * 128 + 256], op=ALU.add),
                        reads=gvk + ["lnb"], writes=[("vn", i, g0 // 2)])

            for c0 in range(OFF_SU, OFF_SU + 512, 256):
                W, wkey = w_acquire(("in", l, c0))
                g0 = (c0 - OFF_SU) // 128
                for i in range(NT):
                    lf, lk = lhsA(i)
                    bank, pc = tm_matmuls(W, wkey, i, lf, lk)
                    j = fm_state["stg"] % 2
                    fm_state["stg"] += 1
                    u, uk = scr(15360 + j * 512, [128, 256], BF16)
                    S.add("act", lambda a, bank=bank, pc=pc, u=u: a.activation(
                        out=u, in_=PS[bank][:, pc:pc + 256], func=AF.Gelu_apprx_tanh),
                        reads=pk(bank, pc, 256), writes=uk)
                    for gg in range(2):
                        g4 = g0 + gg
                        qd = gg
                        S.add("pe", lambda t, g4=g4, i=i, qd=qd: t.matmul(
                            PS[7][:, qd * 128:(qd + 1) * 128], lhsT=wsT[:, g4, :], rhs=vn[:, i, g4 * 128:(g4 + 1) * 128],
                            start=True, stop=True),
                            reads=["wsT", ("vn", i, g0 // 2)], writes=pk(7))
                        e0 = i * D + 1536 + g4 * 128
                        S.add("dve", lambda v, g4=g4, gg=gg, qd=qd, e0=e0, u=u: v.scalar_tensor_tensor(
                            out=M32[:, e0:e0 + 128], in0=PS[7][:, qd * 128:(qd + 1) * 128], scalar=bs_sb[:, g4:g4 + 1],
                            in1=u[:, gg * 128:(gg + 1) * 128], op0=ALU.add, op1=ALU.mult),
                            reads=pk(7) + ["bs"] + uk, writes=mkeys(e0, 128))

            if stop <= 4:
                return
            for base, is_moba in ((OFF_DQ, False), (OFF_MQ, True)):
                for c0 in range(base, base + 768, 256):
                    W, wkey = w_acquire(("in", l, c0))
                    for ch in range(2):
                        hh = (c0 - base) // 128 + ch
                        H = hh if is_moba else 6 + hh
                        for half in range(2):
                            bank = fm_matmuls(W, wkey, ch, half)
                            rb, rbk, t1, t1k, j = rope_chunk(bank, half, is_moba, want_f32=False)
                            store_fm(rb, rbk, j, q_loc, "q_loc", H, half)

            if stop <= 5:
                return
            attention(l)
            if stop <= 7:
                return

            mix_to_A()
            for c0 in range(0, D, 256):
                W, wkey = w_acquire(("out", l, c0))
                for i in range(NT):
                    lf, lk = lhsA(i)
                    bank, pc = tm_matmuls(W, wkey, i, lf, lk)
                    S.add("dve", lambda v, bank=bank, pc=pc, i=i, c0=c0: v.tensor_tensor(
                        out=xres[:, i, c0:c0 + 256], in0=PS[bank][:, pc:pc + 256], in1=xres[:, i, c0:c0 + 256], op=ALU.add),
                        reads=pk(bank, pc, 256) + [("x", i)], writes=[("x", i)])

            if stop <= 8:
                return
            rmsnorm_to_A(depth + l)
            rl_state = 0
            for g in range(NG):
                for jj in range(FFG // 2):
                    W, wkey = w_acquire(("mi", l, (g * FFG + 2 * jj) * 128))
                    for ch in range(2):
                        f = 2 * jj + ch
                        for half in range(2):
                            bank = fm_matmuls(W, wkey, ch, half)
                            rj = rl_state % 2
                            rl_state += 1
                            rl, rlk = scr(4096 + rj * 2048, [128, BLK], F32)
                            S.add("act", lambda a, bank=bank, rl=rl: a.activation(out=rl, in_=PS[bank][:, 0:BLK], func=AF.Relu),
                                  reads=pk(bank, 0, BLK), writes=rlk)
                            e0 = f * T + half * BLK
                            S.add("pool", lambda gq, rl=rl, e0=e0: gq.tensor_tensor(
                                out=M32[:, e0:e0 + BLK], in0=rl, in1=rl, op=ALU.mult),
                                reads=rlk, writes=mkeys(e0, BLK))
                for c0 in range(0, D, 256):
                    W, wkey = w_acquire(("mo", l, g, c0))
                    for i in range(NT):
                        lf = lambda f, i=i: M32[:, f * T + i * 128: f * T + (i + 1) * 128]
                        lk = lambda f, i=i: mkeys(f * T + i * 128, 128)
                        bank, pc = tm_matmuls(W, wkey, i, lf, lk, nk=FFG)
                        S.add("dve", lambda v, bank=bank, pc=pc, i=i, c0=c0: v.tensor_tensor(
                            out=xres[:, i, c0:c0 + 256], in0=PS[bank][:, pc:pc + 256], in1=xres[:, i, c0:c0 + 256],
                            op=ALU.add),
                            reads=pk(bank, pc, 256) + [("x", i)], writes=[("x", i)])

        def attention(l):
            kvs = {"n": 0, "q": 0, "pt": 0}
            v_all4 = ag_all3[:, 1536:3072, :].rearrange("r a c -> r (a c)").rearrange("r (h t d) -> r h t d", h=12, t=T)

            def load_kv(H, slot):
                bi = kvs["n"] % 3
                kvs["n"] += 1
                Kb, Kbk = scr(bi * 1024, [128, BLK], BF16)
                Vb, Vbk = scr(3072 + bi * 1040, [128, KTB, 130], BF16)
                kind, r, ab = slot[0], slot[1], slot[2]
                if kind == "all":
                    ksrc = ag_all3[r, H * 128:(H + 1) * 128, ab * BLK:(ab + 1) * BLK]
                    vsrc = v_all4[r, H, ab * BLK:(ab + 1) * BLK, :].rearrange("(j p) d -> p j d", p=128)
                    rk, rv = ["ag_all"], ["ag_all"]
                else:
                    ksrc = kt_loc[H * 128:(H + 1) * 128, ab * BLK:(ab + 1) * BLK]
                    vsrc = v_loc3[H, ab * BLK:(ab + 1) * BLK, :].rearrange("(j p) d -> p j d", p=128)
                    rk = [("kt_loc", H, ab)]
                    rv = [("v_loc", H, i) for i in range(ab * KTB, (ab + 1) * KTB)]
                S.add("sp", lambda q: q.dma_start(out=Kb, in_=ksrc), reads=rk, writes=Kbk, dma=("Kb", bi))
                S.add("sp", lambda q: q.dma_start(out=Vb[:, :, 0:128], in_=vsrc), reads=rv, writes=Vbk, dma=("Vb", bi))
                return Kb, Kbk, Vb, Vbk

            def load_q(H):
                qi = kvs["q"] % 2
                kvs["q"] += 1
                Qb, Qbk = scr(6656 + qi * 2 * T, [128, T], BF16)
                S.add("sp", lambda q: q.dma_start(out=Qb, in_=q_loc[H * 128:(H + 1) * 128, :]),
                      reads=[("q_loc", H, 0), ("q_loc", H, 1)], writes=Qbk, dma=("Qb", qi))
                return Qb, Qbk

            def new_pt():
                pi = kvs["pt"] % 6
                kvs["pt"] += 1
                return scr(6656 + 4 * T + pi * 2 * BLK, [128, BLK], BF16)

            for bi in range(3):
                Vb, Vbk = scr(3072 + bi * 1040, [128, KTB, 130], BF16)
                S.add("dve", lambda v, Vb=Vb: v.memset(Vb[:, :, 128:129], 1.0), writes=Vbk)

            def slots_for(qh):
                if qh == 0:
                    sl = [("all", r, 0, r) for r in range(8)] + [("loc", -1, 0, None)]
                else:
                    sl = ([("all", r, 0, None) for r in range(8)] + [("all", r, 1, 8 + r) for r in range(8)]
                          + [("loc", -1, 1, None)])
                return sl

            def acc_region(idx):
                return idx // 3, (idx % 3) * 129

            for hd in range(6):
                H = 6 + hd
                Qb, Qbk = load_q(H)
                for qh in range(2):
                    sl = slots_for(qh)
                    for ridx in range(2 * KTB):
                        bankZ, offZ = acc_region(ridx)
                        S.add("pe", lambda t, bankZ=bankZ, offZ=offZ: t.matmul(
                            PS[bankZ][:, offZ:offZ + 129], lhsT=zt[:, 0:128], rhs=zt[:, 0:129], start=True, stop=False),
                            reads=["zt"], writes=pk(bankZ))
                    for si, slot in enumerate(sl):
                        Kb, Kbk, Vb, Vbk = load_kv(H, slot)
                        diag = slot[0] == "loc"
                        bcol = slot[3]
                        for j in range(KTB):
                            q0 = j * 128 if diag else 0
                            nq = BLK - q0
                            pts = []
                            for sm in range(2):
                                bank = 3 + (j % 2) * 2 + sm
                                S.add("pe", lambda t, sm=sm, bank=bank, j=j, q0=q0, nq=nq, Kb=Kb, Qb=Qb, qh=qh: t.matmul(
                                    PS[bank][:, q0:q0 + nq], lhsT=Kb[64 * sm:64 * sm + 64, j * 128:(j + 1) * 128],
                                    rhs=Qb[64 * sm:64 * sm + 64, qh * BLK + q0: qh * BLK + BLK],
                                    start=True, stop=True, tile_position=(64 * sm, 0)),
                                    reads=Kbk + Qbk, writes=pk(bank, q0, nq))
                            for sm in range(2):
                                bank = 3 + (j % 2) * 2 + sm
                                PT, PTk = new_pt()
                                pts.append((PT, PTk))
                                if bcol is None:
                                    S.add("act", lambda a, bank=bank, PT=PT, q0=q0, nq=nq: a.activation(
                                        out=PT[:, q0:q0 + nq], in_=PS[bank][:, q0:q0 + nq], func=AF.Exp, scale=0.125),
                                        reads=pk(bank, q0, nq), writes=PTk)
                                else:
                                    S.add("act", lambda a, bank=bank, PT=PT, q0=q0, nq=nq, bcol=bcol: a.activation(
                                        out=PT[:, q0:q0 + nq], in_=PS[bank][:, q0:q0 + nq], func=AF.Exp, scale=0.125,
                                        bias=sbias[:, bcol:bcol + 1]),
                                        reads=pk(bank, q0, nq) + ["sbias"], writes=PTk)
                                if diag:
                                    S.add("pool", lambda g, PT=PT, q0=q0: g.tensor_tensor(
                                        out=PT[:, q0:q0 + 128], in0=PT[:, q0:q0 + 128], in1=tri, op=ALU.mult),
                                        reads=PTk + ["cmat"], writes=PTk)
                            for qi in range(q0 // 128, KTB):
                                for sm in range(2):
                                    PT, PTk = pts[sm]
                                    bank, off = acc_region(qi * 2 + sm)
                                    first = False
                                    last = (diag and j == qi)
                                    S.add("pe", lambda t, PT=PT, bank=bank, off=off, qi=qi, j=j, first=first, last=last, Vb=Vb: t.matmul(
                                        PS[bank][:, off:off + 129], lhsT=PT[:, qi * 128:(qi + 1) * 128], rhs=Vb[:, j, 0:129],
                                        start=first, stop=last),
                                        reads=PTk + Vbk, writes=pk(bank, off, 129))
                    for qi in range(KTB):
                        it = qh * KTB + qi
                        b0, o0 = acc_region(qi * 2)
                        b1, o1 = acc_region(qi * 2 + 1)
                        fj = qi % 2
                        o32, o32k = scr(16896 + fj * 512, [128, 128], F32)
                        fs = 80 + fj * 8
                        fk = ("fstat", fj)
                        S.add("dve", lambda v, b0=b0, o0=o0, fs=fs: v.reciprocal(out=stat[:, fs:fs + 1], in_=PS[b0][:, o0 + 128:o0 + 129]),
                              reads=pk(b0, o0, 129), writes=[fk])
                        S.add("dve", lambda v, b1=b1, o1=o1, fs=fs: v.reciprocal(out=stat[:, fs + 1:fs + 2], in_=PS[b1][:, o1 + 128:o1 + 129]),
                              reads=pk(b1, o1, 129), writes=[fk])
                        S.add("dve", lambda v, fs=fs: v.tensor_tensor(out=stat[:, fs + 1:fs + 2], in0=stat[:, fs + 1:fs + 2],
                                                                      in1=lam[:, l:l + 1], op=ALU.mult),
                              reads=[fk, ("lam", l)], writes=[fk])
                        S.add("dve", lambda v, b0=b0, o0=o0, fs=fs, o32=o32: v.tensor_scalar(
                            out=o32, in0=PS[b0][:, o0:o0 + 128], scalar1=stat[:, fs:fs + 1], scalar2=None, op0=ALU.mult),
                            reads=pk(b0, o0, 129) + [fk], writes=o32k)
                        S.add("dve", lambda v, b1=b1, o1=o1, fs=fs, o32=o32: v.scalar_tensor_tensor(
                            out=o32, in0=PS[b1][:, o1:o1 + 128], scalar=stat[:, fs + 1:fs + 2], in1=o32,
                            op0=ALU.mult, op1=ALU.add),
                            reads=pk(b1, o1, 129) + [fk] + o32k, writes=o32k)
                        jk, jkk = scr(17920 + fj * 512, [128, 128], F32)
                        S.add("dve", lambda v, o32=o32, jk=jk, fs=fs: v.scalar_tensor_tensor(
                            out=jk, in0=o32, scalar=1.0, in1=o32, op0=ALU.mult, op1=ALU.mult,
                            accum_out=stat[:, fs + 2:fs + 3]),
                            reads=o32k, writes=jkk + [fk])
                        S.add("act", lambda a, fs=fs: a.activation(out=stat[:, fs + 3:fs + 4], in_=stat[:, fs + 2:fs + 3],
                                                                   func=AF.Ln, scale=1.0 / 128, bias=EPS),
                              reads=[fk], writes=[fk])
                        S.add("act", lambda a, fs=fs: a.activation(out=stat[:, fs + 3:fs + 4], in_=stat[:, fs + 3:fs + 4],
                                                                   func=AF.Exp, scale=-0.5),
                              reads=[fk], writes=[fk])
                        e0 = it * D + 768 + hd * 128
                        S.add("dve", lambda v, o32=o32, fs=fs, e0=e0: v.scalar_tensor_tensor(
                            out=M32[:, e0:e0 + 128], in0=o32, scalar=stat[:, fs + 3:fs + 4], in1=gsub[:, l, :],
                            op0=ALU.mult, op1=ALU.mult),
                            reads=o32k + [fk, "gsub"], writes=mkeys(e0, 128))

            if stop <= 6:
                return
            ident64 = cmat[0:64, 0, 0:64]
            for hm in range(6):
                H = hm
                Qb, Qbk = load_q(H)
                for i in range(NT):
                    S.add("pe", lambda t, i=i, Qb=Qb, hm=hm: t.matmul(
                        PS[6][:, i * NBX:(i + 1) * NBX], lhsT=Qb[:, i * 128:(i + 1) * 128], rhs=kmb[:, hm, :],
                        start=True, stop=True),
                        reads=Qbk + ["kmb"], writes=pk(6, 0, 512))
                gm, gmk = scr(16896, [128, NT * NBX], F32)
                bq, bqk = scr(16896 + NT * NBX * 4, [128, NT, NBX], BF16)
                S.add("dve", lambda v, gm=gm: v.tensor_tensor(out=gm, in0=PS[6][:, 0:NT * NBX], in1=mtab[:, 0, :], op=ALU.add),
                      reads=pk(6, 0, 512) + ["mtab"], writes=gmk)
                gm3 = gm.rearrange("p (i n) -> p i n", n=NBX)
                for i in range(NT):
                    S.add("dve", lambda v, i=i, gm3=gm3: v.max(out=stat[:, 64:72], in_=gm3[:, i, 0:NMAIN]),
                          reads=gmk, writes=["top8"])
                    S.add("dve", lambda v, i=i, gm3=gm3: v.tensor_scalar(out=gm3[:, i, :], in0=gm3[:, i, :],
                                                                         scalar1=stat[:, 66:67], scalar2=None, op0=ALU.is_ge),
                          reads=gmk + ["top8"], writes=gmk)
                S.add("dve", lambda v, gm=gm: v.tensor_tensor(out=gm, in0=gm, in1=mtab[:, 1, :], op=ALU.mult),
                      reads=gmk + ["mtab"], writes=gmk)
                S.add("dve", lambda v, gm=gm, bq=bq: v.tensor_tensor(out=bq.rearrange("p i n -> p (i n)"), in0=gm,
                                                                       in1=mtab[:, 2, :], op=ALU.add),
                      reads=gmk + ["mtab"], writes=bqk)
                pv = PS[7][:].bitcast(BF16)
                for i in range(NT):
                    S.add("pe", lambda t, i=i, bq=bq: t.transpose(out=pv[0:NBX, i * 128:(i + 1) * 128], in_=bq[:, i, :],
                                                                   identity=ident),
                          reads=bqk + ["cmat"], writes=pk(7, 0, 512))
                S.add("act", lambda a: a.activation(out=biasT[0:NBX, :], in_=pv[0:NBX, 0:T], func=AF.Copy),
                      reads=pk(7, 0, 512), writes=["biasT"])

                for qh in range(2):
                    sl = slots_for(qh)
                    for ridx in range(KTB):
                        bankZ, offZ = acc_region(ridx)
                        S.add("pe", lambda t, bankZ=bankZ, offZ=offZ: t.matmul(
                            PS[bankZ][:, offZ:offZ + 129], lhsT=zt[:, 0:128], rhs=zt[:, 0:129], start=True, stop=False),
                            reads=["zt"], writes=pk(bankZ))
                    for si, slot in enumerate(sl):
                        Kb, Kbk, Vb, Vbk = load_kv(H, slot)
                        diag = slot[0] == "loc"
                        r, ab = slot[1], slot[2]
                        for j in range(KTB):
                            q0 = j * 128 if diag else 0
                            nq = BLK - q0
                            nidx = (NMAIN + ab * MB + j // 2) if diag else (r * NBL + ab * MB + j // 2)
                            bank = 2 + (kvs["pt"] % 4)
                            S.add("pe", lambda t, bank=bank, j=j, q0=q0, nq=nq, Kb=Kb, Qb=Qb, qh=qh: t.matmul(
                                PS[bank][:, q0:q0 + nq], lhsT=Kb[:, j * 128:(j + 1) * 128],
                                rhs=Qb[:, qh * BLK + q0: qh * BLK + BLK], start=True, stop=False),
                                reads=Kbk + Qbk, writes=pk(bank, q0, nq))
                            S.add("pe", lambda t, bank=bank, q0=q0, nq=nq, nidx=nidx, qh=qh: t.matmul(
                                PS[bank][:, q0:q0 + nq], lhsT=ident64[:, nidx:nidx + 1].to_broadcast([64, 128]),
                                rhs=biasT[0:64, qh * BLK + q0: qh * BLK + BLK], start=False, stop=True),
                                reads=["biasT", "cmat"], writes=pk(bank, q0, nq))
                            PT, PTk = new_pt()
                            S.add("act", lambda a, bank=bank, PT=PT, q0=q0, nq=nq: a.activation(
                                out=PT[:, q0:q0 + nq], in_=PS[bank][:, q0:q0 + nq], func=AF.Exp, scale=HD ** -0.5),
                                reads=pk(bank, q0, nq), writes=PTk)
                            if diag:
                                S.add("pool", lambda g, PT=PT, q0=q0: g.tensor_tensor(
                                    out=PT[:, q0:q0 + 128], in0=PT[:, q0:q0 + 128], in1=tri, op=ALU.mult),
                                    reads=PTk + ["cmat"], writes=PTk)
                            for qi in range(q0 // 128, KTB):
                                bankA, off = acc_region(qi)
                                first = False
                                last = (diag and j == qi)
                                S.add("pe", lambda t, PT=PT, bankA=bankA, off=off, qi=qi, j=j, first=first, last=last, Vb=Vb: t.matmul(
                                    PS[bankA][:, off:off + 129], lhsT=PT[:, qi * 128:(qi + 1) * 128], rhs=Vb[:, j, 0:129],
                                    start=first, stop=last),
                                    reads=PTk + Vbk, writes=pk(bankA, off, 129))
                    for qi in range(KTB):
                        it = qh * KTB + qi
                        b0, o0 = acc_region(qi)
                        fj = qi % 2
                        fs = 80 + fj * 8
                        fk = ("fstat", fj)
                        S.add("dve", lambda v, b0=b0, o0=o0, fs=fs: v.reciprocal(out=stat[:, fs:fs + 1], in_=PS[b0][:, o0 + 128:o0 + 129]),
                              reads=pk(b0, o0, 129), writes=[fk])
                        e0 = it * D + hm * 128
                        S.add("dve", lambda v, b0=b0, o0=o0, fs=fs, e0=e0: v.tensor_scalar(
                            out=M32[:, e0:e0 + 128], in0=PS[b0][:, o0:o0 + 128], scalar1=stat[:, fs:fs + 1], scalar2=None,
                            op0=ALU.mult),
                            reads=pk(b0, o0, 129) + [fk], writes=mkeys(e0, 128))

        S.add("dve", lambda v: v.memset(biasT[:], 0.0), writes=["biasT"])
        S.add("dve", lambda v: v.memset(kmst[:], 0.0), writes=["kmst"])
        S.add("dve", lambda v: v.memset(zt[:], 0.0), writes=["zt"])

        for l in range(depth):
            layer(l)

        finals = []
        if do_final:
            gfin = Wb[0][:].bitcast(F32)
            S.add("sp", lambda q: q.dma_start(out=gfin, in_=fng_d.partition_broadcast(128)),
                  writes=[("W", 0)], dma=("W", 0))
        for i in range(NT):
            if do_final:
                hn, hnk = scr(0, [128, D], BF16)
                S.add("act", lambda a, i=i, hn=hn: a.activation(out=hn, in_=xres[:, i, :], func=AF.Square,
                                                                 accum_out=ssq[:, i:i + 1]),
                      reads=[("x", i)], writes=hnk + [("ssq", i)])
                S.add("act", lambda a, i=i: a.activation(out=rstd[:, i:i + 1], in_=ssq[:, i:i + 1], func=AF.Ln,
                                                         scale=1.0 / D, bias=EPS),
                      reads=[("ssq", i)], writes=[("rstd", i)])
                S.add("act", lambda a, i=i: a.activation(out=rstd[:, i:i + 1], in_=rstd[:, i:i + 1], func=AF.Exp,
                                                         scale=-0.5),
                      reads=[("rstd", i)], writes=[("rstd", i)])
                S.add("dve", lambda v, i=i: v.scalar_tensor_tensor(
                    out=xres[:, i, :], in0=xres[:, i, :], scalar=rstd[:, i:i + 1], in1=gfin, op0=ALU.mult, op1=ALU.mult),
                    reads=[("x", i), ("rstd", i), ("W", 0)], writes=[("x", i)])
            finals.append(S.add("sp", lambda q, i=i: q.dma_start(out=out_d[i * 128:(i + 1) * 128, :], in_=xres[:, i, :]),
                                reads=[("x", i)], writes=[("out", i)], dma="out"))
        S.emit(final_waits=finals)
    return nc


_PROG_CACHE = {}


STOP = 99


def _get_prog(S_seq, depth, d_ff, do_final, lam_base):
    key = (S_seq, depth, d_ff, do_final, lam_base, STOP)
    if key not in _PROG_CACHE:
        _PROG_CACHE[key] = build_program(S_seq, depth, d_ff, do_final, lam_base, STOP)
    return _PROG_CACHE[key]


def run_model(inputs, S_seq, depth, d_ff, layers_per_launch=None):
    x = np.asarray(inputs["x"], np.float32)[0]
    BLK = S_seq // 16
    cm = const_mats()
    per_core_tabs = [host_tables(c, S_seq) for c in range(NCORES)]
    xs = [np.ascontiguousarray(np.concatenate([x[c * BLK:(c + 1) * BLK], x[(15 - c) * BLK:(16 - c) * BLK]], 0))
          for c in range(NCORES)]
    names = ["attn_norm_g", "w_in", "diff_lambda", "diff_subln_g", "sgu_ln_g", "sgu_ln_b", "sgu_w", "sgu_b",
             "w_out", "mlp_norm_g", "w_mlp_in", "w_mlp_out"]
    lpl = depth if layers_per_launch is None else layers_per_launch
    l0 = 0
    while l0 < depth:
        nl = min(lpl, depth - l0)
        last = (l0 + nl == depth)
        nc = _get_prog(S_seq, nl, d_ff, last, l0)
        shared = {n: np.ascontiguousarray(np.asarray(inputs[n], np.float32)[l0:l0 + nl]) for n in names}
        shared["final_norm_g"] = np.ascontiguousarray(np.asarray(inputs["final_norm_g"], np.float32))
        shared["cmat"] = cm
        in_maps = []
        for c in range(NCORES):
            rope, sbias, mtab = per_core_tabs[c]
            m = dict(shared)
            m["x"] = xs[c]
            m["rope"] = rope
            m["slotbias"] = sbias
            m["mobatab"] = mtab
            in_maps.append(m)
        import os as _os
        if _os.environ.get("KTRACE"):
            res = run_bass_kernel_spmd(nc, in_maps, core_ids=list(range(NCORES)), trace=True)
            print("EXEC_NS", res.exec_time_ns)
        else:
            res = run_bass_kernel_spmd(nc, in_maps, core_ids=list(range(NCORES)))
        xs = [np.asarray(res.results[c]["out"], np.float32) for c in range(NCORES)]
        l0 += nl
    out = np.zeros((S_seq, D), np.float32)
    for c in range(NCORES):
        out[c * BLK:(c + 1) * BLK] = xs[c][:BLK]
        out[(15 - c) * BLK:(16 - c) * BLK] = xs[c][BLK:]
    return out[None]


def kernel(**inputs):
    return run_model(inputs, 8192, 4, 8192)
```

```python
import contextlib
import math

import numpy as np

import concourse.bass as bass
import concourse.mybir as mybir
from concourse.bass_utils import run_bass_kernel_spmd

F32 = mybir.dt.float32
BF16 = mybir.dt.bfloat16
AF = mybir.ActivationFunctionType
ALU = mybir.AluOpType
AX = mybir.AxisListType

D = 2048
KD = 16
HD = 128
IN_W = 5632
OFF_MQ, OFF_MK, OFF_MV, OFF_DQ, OFF_DK, OFF_DV, OFF_SU, OFF_SV = 0, 768, 1536, 2304, 3072, 3840, 4608, 5120
EPS = 1e-6
NEG = -30000.0
NCORES = 8
NWB = 3

ENGS = ("pe", "act", "dve", "pool", "sp")


class Op:
    __slots__ = ("eng", "fn", "deps", "is_dma", "sem", "val", "inc", "sig", "gidx")

    def __init__(self, eng, fn, is_dma):
        self.eng = eng
        self.fn = fn
        self.deps = []
        self.is_dma = is_dma
        self.sem = None
        self.val = 0
        self.inc = 1
        self.sig = None
        self.gidx = -1


class Sched:
    def __init__(self, nc, stack, dummy_fns):
        self.nc = nc
        self.stack = stack
        self.order = []
        self.last_w = {}
        self.readers = {}
        self.pair_sem = {}
        self.dma_sems = {}
        self.dma_cnt = {}
        self.dummy_fns = dummy_fns

    def _pair(self, p, c):
        if (p, c) not in self.pair_sem:
            self.pair_sem[(p, c)] = self.stack.enter_context(self.nc.semaphore("s_%s_%s" % (p, c)))
        return self.pair_sem[(p, c)]

    def _dma_sem(self, key):
        if key not in self.dma_sems:
            self.dma_sems[key] = self.stack.enter_context(
                self.nc.semaphore("ds_%d" % len(self.dma_sems)))
            self.dma_cnt[key] = 0
        return self.dma_sems[key]

    def add(self, eng, fn, reads=(), writes=(), dma=None, inc=None):
        op = Op(eng, fn, dma is not None)
        op.gidx = len(self.order)
        deps = []
        for r in reads:
            w = self.last_w.get(r)
            if w is not None:
                deps.append(w)
        for k in writes:
            rd = self.readers.get(k)
            if rd and (rd[0] or rd[1]):
                deps.extend(rd[0].values())
                deps.extend(rd[1])
            else:
                w = self.last_w.get(k)
                if w is not None:
                    deps.append(w)
        best = {}
        seen = set()
        for d in deps:
            if id(d) in seen:
                continue
            seen.add(id(d))
            if d.is_dma:
                op.deps.append(d)
                continue
            if d.eng == "pe" and eng == "pe" and dma is None:
                continue
            b = best.get(d.eng)
            if b is None or b.gidx < d.gidx:
                best[d.eng] = d
        op.deps.extend(best.values())
        if dma is not None:
            op.sem = self._dma_sem(dma)
            op.inc = 16 if inc is None else inc
            self.dma_cnt[dma] += op.inc
            op.val = self.dma_cnt[dma]
        for r in reads:
            rd = self.readers.setdefault(r, [{}, []])
            if op.is_dma:
                rd[1].append(op)
            else:
                rd[0][eng] = op
        for k in writes:
            self.last_w[k] = op
            self.readers[k] = [{}, []]
        self.order.append(op)
        return op

    def emit(self, final_waits=()):
        nc = self.nc
        dependents = {}
        for op in self.order:
            for d in op.deps:
                dependents.setdefault(id(d), []).append(op)
        post = {}
        cnt = {}

        def newsig(p, c):
            cnt[(p, c)] = cnt.get((p, c), 0) + 1
            return (c, self._pair(p, c), cnt[(p, c)])

        n_dummy = 0
        for X in self.order:
            ds = dependents.get(id(X))
            if not ds:
                continue
            ds.sort(key=lambda o: o.gidx)
            by_eng = {}
            for Y in ds:
                by_eng.setdefault(Y.eng, []).append(Y)
            engs = list(by_eng)
            c1 = engs[0]
            if not X.is_dma:
                X.sig = newsig(X.eng, c1)
            if len(engs) == 1:
                continue
            if (not X.is_dma) and X.eng in ("act", "dve", "pool"):
                for c2 in engs[1:]:
                    R = Op(X.eng, self.dummy_fns[X.eng], False)
                    R.gidx = X.gidx
                    R.sig = newsig(X.eng, c2)
                    post.setdefault(id(X), []).append(R)
                    n_dummy += 1
                    for Y in by_eng[c2]:
                        Y.deps = [R if d is X else d for d in Y.deps]
            else:
                Y1 = by_eng[c1][0]
                for c2 in engs[1:]:
                    for Y in by_eng[c2]:
                        Y.deps = [Y1 if d is X else d for d in Y.deps]
                        dependents.setdefault(id(Y1), []).append(Y)
        self.n_dummy = n_dummy
        queues = {e: [] for e in ENGS}
        for op in self.order:
            queues[op.eng].append(op)
            queues[op.eng].extend(post.get(id(op), ()))
        names = {"pe": "tensor", "act": "scalar", "dve": "vector", "pool": "gpsimd", "sp": "sync"}
        with nc.Block() as block:
            for e in ENGS:
                ops = queues[e]
                fw = final_waits if e == "sp" else ()

                def body(engine, ops=ops, fw=fw, e=e):
                    waited = {}
                    for op in ops:
                        for d in op.deps:
                            if d.is_dma:
                                sem, val = d.sem, d.val
                            else:
                                assert d.sig is not None and d.sig[0] == e, (d.eng, e, d.sig)
                                sem, val = d.sig[1], d.sig[2]
                            key = id(sem)
                            if waited.get(key, 0) >= val:
                                continue
                            engine.wait_ge(sem, val)
                            waited[key] = val
                        ins = op.fn(engine)
                        if op.is_dma:
                            ins.then_inc(op.sem, op.inc)
                        elif op.sig is not None:
                            ins.then_inc(op.sig[1], 1)
                    if fw:
                        for k, sem in self.dma_sems.items():
                            engine.wait_ge(sem, self.dma_cnt[k])

                getattr(block, names[e])(body)


def host_tables(c, S_seq):
    BLK = S_seq // 16
    T = 2 * BLK
    NT = T // 128
    MB = BLK // 256
    NBL = 2 * MB
    NMAIN = 8 * NBL
    NBX = NMAIN + NBL
    gblk = [c, 15 - c]
    pos = np.concatenate([np.arange(BLK) + gblk[0] * BLK, np.arange(BLK) + gblk[1] * BLK]).astype(np.float32)
    rope = np.zeros((4, 128, T), np.float32)
    inv64 = (1.0 / (10000.0 ** (np.arange(0, 128, 2, dtype=np.float32) / 128))).astype(np.float32)
    inv32 = (1.0 / (10000.0 ** (np.arange(0, 64, 2, dtype=np.float32) / 64))).astype(np.float32)
    angM = pos[None, :] * inv64[:, None]
    angD = pos[None, :] * inv32[:, None]
    rope[0, 0:64] = np.cos(angM)
    rope[0, 64:128] = np.cos(angM)
    rope[1, 0:64] = -np.sin(angM)
    rope[1, 64:128] = np.sin(angM)
    for c2 in range(2):
        b = c2 * 64
        rope[2, b:b + 32] = np.cos(angD)
        rope[2, b + 32:b + 64] = np.cos(angD)
        rope[3, b:b + 32] = -np.sin(angD)
        rope[3, b + 32:b + 64] = np.sin(angD)
    slotbias = np.zeros((128, 16), np.float32)
    for r in range(8):
        slotbias[:, r] = 0.0 if r < c else NEG
        slotbias[:, 8 + r] = 0.0 if r > c else NEG
    gm_add = np.zeros((128, NT, NBX), np.float32)
    Pt = np.zeros((128, NT, NBX), np.float32)
    Qt = np.full((128, NT, NBX), NEG, np.float32)
    for i in range(NT):
        hq = (i * 128) // BLK
        tl = i * 128 + np.arange(128)
        gpos = gblk[hq] * BLK + (tl % BLK)
        qm = gpos // 256
        for r in range(8):
            for ab in range(2):
                for m in range(MB):
                    n = r * NBL + ab * MB + m
                    g = (r if ab == 0 else 15 - r) * MB + m
                    past = g < qm
                    gm_add[:, i, n] = np.where(past, 0.0, -1e30)
                    if r == c and ab == hq:
                        continue
                    Pt[:, i, n] = np.where(past, -NEG, 0.0)
        for ab in range(2):
            for m in range(MB):
                n = NMAIN + ab * MB + m
                if ab != hq:
                    continue
                g = gblk[ab] * MB + m
                Pt[:, i, n] = np.where(g < qm, -NEG, 0.0)
                Qt[:, i, n] = np.where(g == qm, 0.0, NEG)
    mobatab = np.stack([gm_add.reshape(128, -1), Pt.reshape(128, -1), Qt.reshape(128, -1)])
    return rope, slotbias, mobatab


def const_mats():
    ident = np.eye(128, dtype=np.float32)
    permM = np.zeros((128, 128), np.float32)
    permD = np.zeros((128, 128), np.float32)
    for m in range(128):
        permM[(m + 64) % 128, m] = 1.0
        permD[(m // 64) * 64 + ((m % 64) + 32) % 64, m] = 1.0
    tri = np.triu(np.ones((128, 128), np.float32))
    return np.stack([ident, permM, permD, tri])


def build_program(S_seq, depth, d_ff, do_final=True, lam_base=0, stop=99):
    BLK = S_seq // 16
    T = 2 * BLK
    NT = T // 128
    KTB = BLK // 128
    MB = BLK // 256
    NBL = 2 * MB
    NMAIN = 8 * NBL
    NBX = NMAIN + NBL
    NFF = d_ff // 128
    FFG = 16
    NG = NFF // FFG
    assert NFF % FFG == 0

    import os as _os
    KSUB = int(_os.environ.get("KSUB", "0"))
    nc = bass.Bass("TRN2", target_bir_lowering=False)

    def din(name, shape, dt=F32):
        return nc.dram_tensor(name, list(shape), dt, kind="ExternalInput").ap()

    x_d = din("x", [T, D])
    ang_d = din("attn_norm_g", [depth, D])
    win_d = din("w_in", [depth, D, IN_W])
    dl_d = din("diff_lambda", [depth, 4, 64])
    dsg_d = din("diff_subln_g", [depth, 128])
    slg_d = din("sgu_ln_g", [depth, 4, 128])
    slb_d = din("sgu_ln_b", [depth, 4, 128])
    sw_d = din("sgu_w", [depth, 4, 128, 128])
    sb_d = din("sgu_b", [depth, 4, 128])
    wout_d = din("w_out", [depth, D, D])
    mng_d = din("mlp_norm_g", [depth, D])
    wmi_d = din("w_mlp_in", [depth, D, d_ff])
    wmo_d = din("w_mlp_out", [depth, d_ff, D])
    fng_d = din("final_norm_g", [D])
    rope_d = din("rope", [4, 128, T])
    cmat_d = din("cmat", [4, 128, 128])
    sbias_d = din("slotbias", [128, 16])
    mtab_d = din("mobatab", [3, 128, NT * NBX])
    out_d = nc.dram_tensor("out", [T, D], F32, kind="ExternalOutput").ap()

    KMR = 6
    AGR = 3072 + KMR
    assert KMR * T // 2 == 128 * 6 * NBL
    ag_loc = nc.dram_tensor("ag_loc", [AGR, T], BF16)
    ag_all = nc.dram_tensor("ag_all", [8 * AGR, T], BF16)
    q_loc = nc.dram_tensor("q_loc", [1536, T], BF16)
    kt_loc = ag_loc
    v_loc3 = ag_loc[1536:3072, :].rearrange("r c -> (r c)").rearrange("(h t d) -> h t d", h=12, t=T)
    km_loc2 = ag_loc[3072:AGR, :].rearrange("r c -> (r c)").bitcast(F32).rearrange("(p f) -> p f", p=128)
    ag_all3 = ag_all.ap().rearrange("(r a) c -> r a c", r=8)

    with contextlib.ExitStack() as st:
        dumt = st.enter_context(nc.sbuf_tensor("dumt", [128, 392], F32))
        dctr = {"act": 0, "dve": 0, "pool": 0}

        def dcol(e):
            dctr[e] += 1
            base = {"act": 0, "dve": 128, "pool": 256}[e]
            k = base + dctr[e] % 128
            return dumt[0:1, k:k + 1]

        S = Sched(nc, st, {
            "act": lambda a: a.activation(out=dcol("act"), in_=dumt[0:1, 388:389], func=AF.Copy),
            "dve": lambda v: v.memset(dcol("dve"), 0.0),
            "pool": lambda g: g.memset(dcol("pool"), 0.0),
        })

        def sb(name, shape, dt):
            return st.enter_context(nc.sbuf_tensor(name, list(shape), dt))

        def ps(name, shape, dt):
            return st.enter_context(nc.psum_tensor(name, list(shape), dt))

        xres = sb("xres", [128, NT, D], F32)
        A32 = sb("A32", [128, KD * T], BF16)
        M32 = sb("M32", [128, NT * D], BF16)
        Wb = [sb("Wb%d" % b, [128, KD * 256], BF16) for b in range(NWB)]
        ropeT = sb("ropeT", [128, 4, T], BF16)
        vn = sb("vn", [128, NT, 512], BF16)
        SCRB = 19456
        SCR = sb("SCR", [128, SCRB // 2], BF16)
        cmat = sb("cmat_sb", [128, 4, 128], BF16)
        sbias = sb("sbias_sb", [128, 16], F32)
        mtab = sb("mtab_sb", [128, 3, NT * NBX], F32)
        gcols = sb("gcols", [128, 2 * depth, KD], F32)
        lam = sb("lam", [128, 8 * depth], F32)
        gsub = sb("gsub", [128, depth, 128], F32)
        lng = sb("lng", [128, 512], F32)
        lnb = sb("lnb", [128, 512], F32)
        wsT = sb("wsT", [128, 4, 128], BF16)
        bs_sb = sb("bs_sb", [128, 4], F32)
        biasT = sb("biasT", [64, T], BF16)
        kmg = sb("kmg", [128, 8, 6 * NBL], F32)
        kmb = sb("kmb", [128, 6, NBX], BF16)
        kmst = sb("kmst", [128, 6, NBL], F32)
        stat = sb("stat", [128, 192], F32)
        zt = sb("zt", [128, 130], BF16)
        ssq = sb("ssq", [128, NT], F32)
        rstd = sb("rstd", [128, NT], F32)

        ident = cmat[:, 0, :]
        permM = cmat[:, 1, :]
        permD = cmat[:, 2, :]
        tri = cmat[:, 3, :]

        PS = [ps("PS%d" % b, [128, 512], F32) for b in range(8)]

        def pk(bank, c0=0, n=512):
            return [("P", bank)]

        def scr(off, shape, dt):
            esz = 4 if dt == F32 else 2
            n = 1
            for s_ in shape[1:]:
                n *= s_
            nb = n * esz
            assert off % 4 == 0 and off + nb <= SCRB, (off, nb)
            flat = SCR[0:shape[0], off // 2:(off + nb) // 2]
            if dt == F32:
                flat = flat.bitcast(F32)
            if len(shape) == 3:
                flat = flat.rearrange("p (a b) -> p a b", a=shape[1])
            keys = [("SCR", k) for k in range(off // 512, (off + nb - 1) // 512 + 1)]
            return flat, keys

        def akeys(kt, t0, n):
            return [("A", kt, i) for i in range(t0 // 128, (t0 + n - 1) // 128 + 1)]

        def mkeys(e0, n):
            return [("M", e) for e in range(e0 // 1024, (e0 + n - 1) // 1024 + 1)]

        def Aap(kt, t0, n):
            return A32[:, kt * T + t0: kt * T + t0 + n]

        wplan = []
        for l in range(depth):
            for c0 in (list(range(OFF_MK, OFF_MK + 768, 256)) + list(range(OFF_DK, OFF_DK + 768, 256))
                       + list(range(OFF_MV, OFF_MV + 768, 256)) + list(range(OFF_DV, OFF_DV + 768, 256))
                       + list(range(OFF_SV, OFF_SV + 512, 256)) + list(range(OFF_SU, OFF_SU + 512, 256))
                       + list(range(OFF_DQ, OFF_DQ + 768, 256)) + list(range(OFF_MQ, OFF_MQ + 768, 256))):
                wplan.append(("in", l, c0))
            for c0 in range(0, D, 256):
                wplan.append(("out", l, c0))
            for g in range(NG):
                for j in range(FFG // 2):
                    wplan.append(("mi", l, (g * FFG + 2 * j) * 128))
                for c0 in range(0, D, 256):
                    wplan.append(("mo", l, g, c0))
        wstate = {"dma": 0, "use": 0}

        def w_src(desc):
            kind = desc[0]
            if kind == "in":
                return win_d[desc[1]].rearrange("(k p) n -> p k n", p=128)[:, :, desc[2]:desc[2] + 256]
            if kind == "out":
                return wout_d[desc[1]].rearrange("(k p) n -> p k n", p=128)[:, :, desc[2]:desc[2] + 256]
            if kind == "mi":
                return wmi_d[desc[1]].rearrange("(k p) n -> p k n", p=128)[:, :, desc[2]:desc[2] + 256]
            g, c0 = desc[2], desc[3]
            return wmo_d[desc[1]].rearrange("(k p) n -> p k n", p=128)[:, g * FFG:(g + 1) * FFG, c0:c0 + 256]

        def w_issue():
            n = wstate["dma"]
            if n >= len(wplan):
                return
            b = n % NWB
            src = w_src(wplan[n])
            dst = Wb[b][:].rearrange("p (k n) -> p k n", k=KD)
            S.add("pool", lambda g, dst=dst, src=src: g.dma_start(out=dst, in_=src),
                  writes=[("W", b)], dma=("W", b))
            wstate["dma"] = n + 1

        def w_acquire(desc):
            n = wstate["use"]
            assert wplan[n] == desc, (wplan[n], desc)
            while wstate["dma"] < min(len(wplan), n + NWB):
                w_issue()
            wstate["use"] = n + 1
            b = n % NWB
            return Wb[b][:].rearrange("p (k n) -> p k n", k=KD), ("W", b)

        lp, lpk = scr(0, [128, depth * 256], F32)
        S.add("dve", lambda v: v.memset(dumt[:], 0.0), writes=["dumt"])
        for i in range(NT):
            S.add("sp", lambda q, i=i: q.dma_start(out=xres[:, i, :], in_=x_d[i * 128:(i + 1) * 128, :]),
                  writes=[("x", i)], dma=("xld", i))
        S.add("pool", lambda g: g.dma_start(out=cmat[:], in_=cmat_d.rearrange("a p n -> p a n")),
              writes=["cmat"], dma="cmat")
        S.add("pool", lambda g: g.dma_start(out=ropeT[:], in_=rope_d.rearrange("a p n -> p a n")),
              writes=["rope"], dma="rope")
        S.add("sp", lambda q: q.dma_start(out=sbias[:], in_=sbias_d), writes=["sbias"], dma="sbias")
        S.add("sp", lambda q: q.dma_start(out=mtab[:], in_=mtab_d.rearrange("a p n -> p a n")),
              writes=["mtab"], dma="mtab")

        def ld_gcols(q):
            with nc.allow_non_contiguous_dma(reason="tiny strided gain load"):
                return q.dma_start(out=gcols[:, 0:depth, :], in_=ang_d.rearrange("l (k p) -> p l k", p=128))

        def ld_gcols2(q):
            with nc.allow_non_contiguous_dma(reason="tiny strided gain load"):
                return q.dma_start(out=gcols[:, depth:2 * depth, :], in_=mng_d.rearrange("l (k p) -> p l k", p=128))

        S.add("sp", ld_gcols, writes=["gcols0"], dma="gcols0")
        S.add("sp", ld_gcols2, writes=["gcols1"], dma="gcols1")
        S.add("sp", lambda q: q.dma_start(out=lp, in_=dl_d.rearrange("l a b -> (l a b)").partition_broadcast(128)),
              writes=lpk, dma="lp")
        S.add("sp", lambda q: q.dma_start(out=gsub[:].rearrange("p l n -> p (l n)"),
                                          in_=dsg_d.rearrange("l n -> (l n)").partition_broadcast(128)),
              writes=["gsub"], dma="gsub")
        for l in range(depth):
            lam_init = 0.8 - 0.6 * math.exp(-0.3 * (l + lam_base))
            for pr in range(2):
                a0 = l * 256 + pr * 128
                S.add("dve", lambda v, a0=a0, l=l, pr=pr: v.scalar_tensor_tensor(
                    out=stat[:, 128:192], in0=lp[:, a0:a0 + 64], scalar=1.0, in1=lp[:, a0 + 64:a0 + 128],
                    op0=ALU.mult, op1=ALU.mult, accum_out=lam[:, 4 * depth + 2 * l + pr:4 * depth + 2 * l + pr + 1]),
                    reads=lpk, writes=["stat", ("lam", l)])
            S.add("act", lambda a, l=l: a.activation(out=lam[:, 6 * depth + 2 * l:6 * depth + 2 * l + 2],
                                                     in_=lam[:, 4 * depth + 2 * l:4 * depth + 2 * l + 2], func=AF.Exp),
                  reads=[("lam", l)], writes=[("lam", l)])
            S.add("dve", lambda v, l=l, li=lam_init: v.scalar_tensor_tensor(
                out=lam[:, l:l + 1], in0=lam[:, 6 * depth + 2 * l + 1:6 * depth + 2 * l + 2], scalar=-li,
                in1=lam[:, 6 * depth + 2 * l:6 * depth + 2 * l + 1], op0=ALU.add, op1=ALU.subtract),
                reads=[("lam", l)], writes=[("lam", l)])
            S.add("dve", lambda v, l=l, li=lam_init: v.tensor_scalar(
                out=gsub[:, l, :], in0=gsub[:, l, :], scalar1=1.0 - li, scalar2=None, op0=ALU.mult),
                reads=["gsub"], writes=["gsub"])

        tp_state = {"g": 0}

        def rmsnorm_to_A(gi):
            for i in range(NT):
                hn, hnk = scr(0, [128, D], BF16)
                S.add("act", lambda a, i=i, hn=hn: a.activation(out=hn, in_=xres[:, i, :], func=AF.Square,
                                                                 accum_out=ssq[:, i:i + 1]),
                      reads=[("x", i)], writes=hnk + [("ssq", i)])
                S.add("act", lambda a, i=i: a.activation(out=rstd[:, i:i + 1], in_=ssq[:, i:i + 1], func=AF.Ln,
                                                         scale=1.0 / D, bias=EPS),
                      reads=[("ssq", i)], writes=[("rstd", i)])
                S.add("act", lambda a, i=i: a.activation(out=rstd[:, i:i + 1], in_=rstd[:, i:i + 1], func=AF.Exp,
                                                         scale=-0.5),
                      reads=[("rstd", i)], writes=[("rstd", i)])
                S.add("dve", lambda v, i=i, hn=hn: v.tensor_scalar(out=hn, in0=xres[:, i, :], scalar1=rstd[:, i:i + 1],
                                                                   scalar2=None, op0=ALU.mult),
                      reads=[("x", i), ("rstd", i)], writes=hnk)
                for k0 in range(0, KD, 4):
                    g = tp_state["g"] % 2
                    tp_state["g"] += 1
                    pv = PS[6 + g][:].bitcast(BF16)
                    for kk in range(4):
                        kt = k0 + kk
                        S.add("pe", lambda t, kt=kt, kk=kk, hn=hn, pv=pv: t.transpose(
                            out=pv[:, kk * 128:(kk + 1) * 128],
                            in_=hn[:, kt * 128:(kt + 1) * 128], identity=ident),
                            reads=hnk + ["cmat"], writes=pk(6 + g))
                    for kk in range(4):
                        kt = k0 + kk
                        src = pv[:, kk * 128:(kk + 1) * 128]
                        dst = Aap(kt, i * 128, 128)
                        if g == 0:
                            S.add("act", lambda a, src=src, dst=dst, kt=kt: a.activation(
                                out=dst, in_=src, func=AF.Copy, scale=gcols[:, gi, kt:kt + 1]),
                                reads=pk(6 + g) + ["gcols0", "gcols1"],
                                writes=akeys(kt, i * 128, 128))
                        else:
                            S.add("dve", lambda v, src=src, dst=dst, kt=kt: v.tensor_scalar(
                                out=dst, in0=src, scalar1=gcols[:, gi, kt:kt + 1], scalar2=None, op0=ALU.mult),
                                reads=pk(6 + g) + ["gcols0", "gcols1"],
                                writes=akeys(kt, i * 128, 128))

        def mix_to_A():
            for i in range(NT):
                for k0 in range(0, KD, 4):
                    g = tp_state["g"] % 2
                    tp_state["g"] += 1
                    pv = PS[6 + g][:].bitcast(BF16)
                    for kk in range(4):
                        kt = k0 + kk
                        e0 = i * D + kt * 128
                        S.add("pe", lambda t, kk=kk, e0=e0, pv=pv: t.transpose(
                            out=pv[:, kk * 128:(kk + 1) * 128],
                            in_=M32[:, e0:e0 + 128], identity=ident),
                            reads=mkeys(e0, 128) + ["cmat"], writes=pk(6 + g))
                    for kk in range(4):
                        kt = k0 + kk
                        src = pv[:, kk * 128:(kk + 1) * 128]
                        dst = Aap(kt, i * 128, 128)
                        eng = "act" if g == 0 else "dve"
                        if eng == "act":
                            S.add("act", lambda a, src=src, dst=dst: a.activation(out=dst, in_=src, func=AF.Copy),
                                  reads=pk(6 + g), writes=akeys(kt, i * 128, 128))
                        else:
                            S.add("dve", lambda v, src=src, dst=dst: v.tensor_copy(out=dst, in_=src),
                                  reads=pk(6 + g), writes=akeys(kt, i * 128, 128))

        fm_state = {"n": 0, "sw": 0, "tm": 0, "stg": 0}

        def fm_matmuls(W, wkey, ch, half):
            bank = fm_state["n"] % 4
            fm_state["n"] += 1
            for kt in range(KD):
                S.add("pe", lambda t, kt=kt, bank=bank: t.matmul(
                    PS[bank][:, 0:BLK], lhsT=W[:, kt, ch * 128:(ch + 1) * 128], rhs=Aap(kt, half * BLK, BLK),
                    start=(kt == 0), stop=(kt == KD - 1)),
                    reads=[wkey] + akeys(kt, half * BLK, BLK), writes=pk(bank, 0, BLK))
            return bank

        def tm_matmuls(W, wkey, i, lhs_fn, lhs_keys_fn, nk=KD):
            r = fm_state["tm"] % 3
            fm_state["tm"] += 1
            bank, c0 = 4 + r, 0
            for kt in range(nk):
                S.add("pe", lambda t, kt=kt, bank=bank, c0=c0: t.matmul(
                    PS[bank][:, c0:c0 + 256], lhsT=lhs_fn(kt), rhs=W[:, kt, :],
                    start=(kt == 0), stop=(kt == nk - 1)),
                    reads=[wkey] + lhs_keys_fn(kt), writes=pk(bank, c0, 256))
            return bank, c0

        def rope_chunk(bank, half, is_moba, want_f32):
            j = fm_state["sw"] % 2
            fm_state["sw"] += 1
            xb, xbk = scr(j * 1024, [128, BLK], BF16)
            t1, t1k = scr(2048 + j * 2048, [128, BLK], F32)
            t2, t2k = scr(6144 + j * 2048, [128, BLK], F32)
            rb, rbk = scr(10240 + j * 1024, [128, BLK], BF16)
            swb = 4 + j
            perm = permM if is_moba else permD
            ci, si = (0, 1) if is_moba else (2, 3)
            t0 = half * BLK
            S.add("dve", lambda v: v.tensor_copy(out=xb, in_=PS[bank][:, 0:BLK]),
                  reads=pk(bank, 0, BLK), writes=xbk)
            S.add("pe", lambda t: t.matmul(PS[swb][:, 0:BLK], lhsT=perm, rhs=xb, start=True, stop=True),
                  reads=xbk + ["cmat"], writes=pk(swb, 0, BLK))
            S.add("dve", lambda v: v.tensor_tensor(out=t1, in0=PS[bank][:, 0:BLK], in1=ropeT[:, ci, t0:t0 + BLK],
                                                   op=ALU.mult),
                  reads=pk(bank, 0, BLK) + ["rope"], writes=t1k)
            S.add("dve", lambda v: v.tensor_tensor(out=t2, in0=PS[swb][:, 0:BLK], in1=ropeT[:, si, t0:t0 + BLK],
                                                   op=ALU.mult),
                  reads=pk(swb, 0, BLK) + ["rope"], writes=t2k)
            if want_f32:
                S.add("pool", lambda g: g.tensor_tensor(out=t1, in0=t1, in1=t2, op=ALU.add),
                      reads=t1k + t2k, writes=t1k)
                S.add("pool", lambda g: g.tensor_copy(out=rb, in_=t1), reads=t1k, writes=rbk)
            else:
                S.add("pool", lambda g: g.tensor_tensor(out=rb, in0=t1, in1=t2, op=ALU.add),
                      reads=t1k + t2k, writes=rbk)
            return rb, rbk, t1, t1k, j

        def layer(l):
            S.add("sp", lambda q: q.dma_start(out=lng[:], in_=slg_d[l].rearrange("g n -> (g n)").partition_broadcast(128)),
                  writes=["lng"], dma="lng")
            S.add("sp", lambda q: q.dma_start(out=lnb[:], in_=slb_d[l].rearrange("g n -> (g n)").partition_broadcast(128)),
                  writes=["lnb"], dma="lnb")

            def ld_bs(q):
                with nc.allow_non_contiguous_dma(reason="tiny"):
                    return q.dma_start(out=bs_sb[:], in_=sb_d[l].rearrange("g t -> t g"))
            S.add("sp", ld_bs, writes=["bs"], dma="bs")
            wst, wstk = scr(16384, [128, 4, 128], BF16)
            S.add("pool", lambda g: g.dma_start(out=wst, in_=sw_d[l].rearrange("g t s -> t g s")),
                  writes=wstk, dma="wst")
            pv = PS[7][:].bitcast(BF16)
            for g4 in range(4):
                S.add("pe", lambda t, g4=g4: t.transpose(out=pv[:, g4 * 128:(g4 + 1) * 128], in_=wst[:, g4, :],
                                                          identity=ident),
                      reads=wstk + ["cmat"], writes=pk(7))
            for g4 in range(4):
                S.add("dve", lambda v, g4=g4: v.tensor_tensor(out=wsT[:, g4, :], in0=pv[:, g4 * 128:(g4 + 1) * 128],
                                                             in1=tri, op=ALU.mult),
                      reads=pk(7) + ["cmat"], writes=["wsT"])

            if stop <= 0:
                return
            rmsnorm_to_A(l)
            if stop <= 1:
                return

            def store_fm(rb, rbk, j, dram, dname, H, half):
                S.add("sp", lambda q: q.dma_start(out=dram[H * 128:(H + 1) * 128, half * BLK:(half + 1) * BLK], in_=rb),
                      reads=rbk, writes=[(dname, H, half)], dma=("rb", j))

            for base, is_moba in ((OFF_MK, True), (OFF_DK, False)):
                for c0 in range(base, base + 768, 256):
                    W, wkey = w_acquire(("in", l, c0))
                    for ch in range(2):
                        hh = (c0 - base) // 128 + ch
                        H = hh if is_moba else 6 + hh
                        for half in range(2):
                            bank = fm_matmuls(W, wkey, ch, half)
                            if KSUB == 1:
                                continue
                            rb, rbk, t1, t1k, j = rope_chunk(bank, half, is_moba, want_f32=is_moba)
                            if KSUB == 2:
                                continue
                            if is_moba:
                                S.add("dve", lambda v, t1=t1, hh=hh, half=half: v.tensor_reduce(
                                    out=kmst[:, hh, half * MB:(half + 1) * MB],
                                    in_=t1.rearrange("p (a b) -> p a b", a=MB), axis=AX.X, op=ALU.add),
                                    reads=t1k, writes=["kmst"])
                            store_fm(rb, rbk, j, kt_loc, "kt_loc", H, half)
                if is_moba and KSUB in (0, 4):
                    S.add("dve", lambda v: v.tensor_scalar(out=kmst[:], in0=kmst[:], scalar1=1.0 / 256, scalar2=None,
                                                           op0=ALU.mult),
                          reads=["kmst"], writes=["kmst"])
                    S.add("sp", lambda q: q.dma_start(out=km_loc2, in_=kmst[:].rearrange("p h b -> p (h b)")),
                          reads=["kmst"], writes=["km_loc"], dma="km_loc")
            if stop <= 2:
                return
            def lhsA(i):
                return (lambda kt: Aap(kt, i * 128, 128)), (lambda kt: akeys(kt, i * 128, 128))

            for base, hoff in ((OFF_MV, 0), (OFF_DV, 6)):
                for c0 in range(base, base + 768, 256):
                    W, wkey = w_acquire(("in", l, c0))
                    h0 = hoff + (c0 - base) // 128
                    for i in range(NT):
                        lf, lk = lhsA(i)
                        bank, pc = tm_matmuls(W, wkey, i, lf, lk)
                        j = fm_state["stg"] % 2
                        fm_state["stg"] += 1
                        vst, vstk = scr(12288 + j * 512, [128, 256], BF16)
                        S.add("act", lambda a, bank=bank, pc=pc, vst=vst: a.activation(
                            out=vst, in_=PS[bank][:, pc:pc + 256], func=AF.Copy),
                            reads=pk(bank, pc, 256), writes=vstk)
                        dst = v_loc3[h0:h0 + 2, i * 128:(i + 1) * 128, :].rearrange("h t d -> t h d")
                        S.add("sp", lambda q, dst=dst, vst=vst: q.dma_start(
                            out=dst, in_=vst.rearrange("p (h d) -> p h d", h=2)),
                            reads=vstk, writes=[("v_loc", h0, i), ("v_loc", h0 + 1, i)], dma=("vst", j))
            S.add("pool", lambda g: g.collective_compute(
                "AllGather", ALU.bypass, replica_groups=[list(range(NCORES))],
                ins=[ag_loc.ap().opt()], outs=[ag_all.ap().opt()]),
                reads=[("v_loc", H, i) for H in range(12) for i in range(NT)] + ["km_loc"]
                + [("kt_loc", H, hf) for H in range(12) for hf in range(2)], writes=["ag_all"], dma="cc_ag", inc=1)
            km_src = ag_all3[:, 3072:AGR, :].rearrange("r a c -> r (a c)").bitcast(F32).rearrange("r (p f) -> p r f", p=128)
            S.add("sp", lambda q: q.dma_start(out=kmg[:], in_=km_src),
                  reads=["ag_all"], writes=["kmg"], dma="kmg")
            for h in range(6):
                S.add("dve", lambda v, h=h: v.tensor_copy(
                    out=kmb[:, h, 0:NMAIN].rearrange("p (r b) -> p r b", b=NBL),
                    in_=kmg[:, :, h * NBL:(h + 1) * NBL]),
                    reads=["kmg"], writes=["kmb"])
            S.add("dve", lambda v: v.tensor_copy(out=kmb[:, :, NMAIN:NBX], in_=kmst[:]),
                  reads=["kmst"], writes=["kmb"])

            if stop <= 3:
                return
            for c0 in range(OFF_SV, OFF_SV + 512, 256):
                W, wkey = w_acquire(("in", l, c0))
                g0 = (c0 - OFF_SV) // 128
                for i in range(NT):
                    lf, lk = lhsA(i)
                    bank, pc = tm_matmuls(W, wkey, i, lf, lk)
                    j = fm_state["stg"] % 2
                    fm_state["stg"] += 1
                    gv, gvk = scr(13312 + j * 1024, [128, 256], F32)
                    S.add("act", lambda a, bank=bank, pc=pc, gv=gv: a.activation(
                        out=gv, in_=PS[bank][:, pc:pc + 256], func=AF.Gelu_apprx_tanh),
                        reads=pk(bank, pc, 256), writes=gvk)
                    sk = ("stat", j)
                    so = j * 32
                    for gg in range(2):
                        S.add("dve", lambda v, gv=gv, gg=gg, so=so: v.bn_stats(
                            out=stat[:, so + gg * 6: so + gg * 6 + 6], in_=gv[:, gg * 128:(gg + 1) * 128]),
                            reads=gvk, writes=[sk])
                        S.add("dve", lambda v, gg=gg, so=so: v.bn_aggr(
                            out=stat[:, so + 12 + gg * 2: so + 14 + gg * 2], in_=stat[:, so + gg * 6: so + gg * 6 + 6]),
                            reads=[sk], writes=[sk])
                    varv = stat[:, so + 12: so + 16].rearrange("p (g t) -> p g t", t=2)[:, :, 1:2]
                    rsv = stat[:, so + 16: so + 18].rearrange("p (g t) -> p g t", t=1)
                    S.add("act", lambda a, varv=varv, rsv=rsv: a.activation(out=rsv, in_=varv, func=AF.Ln, bias=EPS),
                          reads=[sk], writes=[sk])
                    S.add("act", lambda a, rsv=rsv: a.activation(out=rsv, in_=rsv, func=AF.Exp, scale=-0.5),
                          reads=[sk], writes=[sk])
                    for gg in range(2):
                        S.add("dve", lambda v, gv=gv, gg=gg, so=so: v.tensor_scalar(
                            out=gv[:, gg * 128:(gg + 1) * 128], in0=gv[:, gg * 128:(gg + 1) * 128],
                            scalar1=stat[:, so + 12 + gg * 2: so + 13 + gg * 2],
                            scalar2=stat[:, so + 16 + gg: so + 17 + gg], op0=ALU.subtract, op1=ALU.mult),
                            reads=gvk + [sk], writes=gvk)
                    S.add("dve", lambda v, gv=gv, g0=g0: v.tensor_tensor(
                        out=gv, in0=gv, in1=lng[:, g0 * 128: g0 * 128 + 256], op=ALU.mult),
                        reads=gvk + ["lng"], writes=gvk)
                    S.add("pool", lambda g, gv=gv, g0=g0, i=i: g.tensor_tensor(
                        out=vn[:, i, g0 * 128: g0 * 128 + 256], in0=gv, in1=lnb[:, g0 * 128: g0 * 128 + 256], op=ALU.add),
                        reads=gvk + ["lnb"], writes=[("vn", i, g0 // 2)])

            for c0 in range(OFF_SU, OFF_SU + 512, 256):
                W, wkey = w_acquire(("in", l, c0))
                g0 = (c0 - OFF_SU) // 128
                for i in range(NT):
                    lf, lk = lhsA(i)
                    bank, pc = tm_matmuls(W, wkey, i, lf, lk)
                    j = fm_state["stg"] % 2
                    fm_state["stg"] += 1
                    u, uk = scr(15360 + j * 512, [128, 256], BF16)
                    S.add("act", lambda a, bank=bank, pc=pc, u=u: a.activation(
                        out=u, in_=PS[bank][:, pc:pc + 256], func=AF.Gelu_apprx_tanh),
                        reads=pk(bank, pc, 256), writes=uk)
                    for gg in range(2):
                        g4 = g0 + gg
                        qd = gg
                        S.add("pe", lambda t, g4=g4, i=i, qd=qd: t.matmul(
                            PS[7][:, qd * 128:(qd + 1) * 128], lhsT=wsT[:, g4, :], rhs=vn[:, i, g4 * 128:(g4 + 1) * 128],
                            start=True, stop=True),
                            reads=["wsT", ("vn", i, g0 // 2)], writes=pk(7))
                        e0 = i * D + 1536 + g4 * 128
                        S.add("dve", lambda v, g4=g4, gg=gg, qd=qd, e0=e0, u=u: v.scalar_tensor_tensor(
                            out=M32[:, e0:e0 + 128], in0=PS[7][:, qd * 128:(qd + 1) * 128], scalar=bs_sb[:, g4:g4 + 1],
                            in1=u[:, gg * 128:(gg + 1) * 128], op0=ALU.add, op1=ALU.mult),
                            reads=pk(7) + ["bs"] + uk, writes=mkeys(e0, 128))

            if stop <= 4:
                return
            for base, is_moba in ((OFF_DQ, False), (OFF_MQ, True)):
                for c0 in range(base, base + 768, 256):
                    W, wkey = w_acquire(("in", l, c0))
                    for ch in range(2):
                        hh = (c0 - base) // 128 + ch
                        H = hh if is_moba else 6 + hh
                        for half in range(2):
                            bank = fm_matmuls(W, wkey, ch, half)
                            rb, rbk, t1, t1k, j = rope_chunk(bank, half, is_moba, want_f32=False)
                            store_fm(rb, rbk, j, q_loc, "q_loc", H, half)

            if stop <= 5:
                return
            attention(l)
            if stop <= 7:
                return

            mix_to_A()
            for c0 in range(0, D, 256):
                W, wkey = w_acquire(("out", l, c0))
                for i in range(NT):
                    lf, lk = lhsA(i)
                    bank, pc = tm_matmuls(W, wkey, i, lf, lk)
                    S.add("dve", lambda v, bank=bank, pc=pc, i=i, c0=c0: v.tensor_tensor(
                        out=xres[:, i, c0:c0 + 256], in0=PS[bank][:, pc:pc + 256], in1=xres[:, i, c0:c0 + 256], op=ALU.add),
                        reads=pk(bank, pc, 256) + [("x", i)], writes=[("x", i)])

            if stop <= 8:
                return
            rmsnorm_to_A(depth + l)
            rl_state = 0
            for g in range(NG):
                for jj in range(FFG // 2):
                    W, wkey = w_acquire(("mi", l, (g * FFG + 2 * jj) * 128))
                    for ch in range(2):
                        f = 2 * jj + ch
                        for half in range(2):
                            bank = fm_matmuls(W, wkey, ch, half)
                            rj = rl_state % 2
                            rl_state += 1
                            rl, rlk = scr(4096 + rj * 2048, [128, BLK], F32)
                            S.add("act", lambda a, bank=bank, rl=rl: a.activation(out=rl, in_=PS[bank][:, 0:BLK], func=AF.Relu),
                                  reads=pk(bank, 0, BLK), writes=rlk)
                            e0 = f * T + half * BLK
                            S.add("pool", lambda gq, rl=rl, e0=e0: gq.tensor_tensor(
                                out=M32[:, e0:e0 + BLK], in0=rl, in1=rl, op=ALU.mult),
                                reads=rlk, writes=mkeys(e0, BLK))
                for c0 in range(0, D, 256):
                    W, wkey = w_acquire(("mo", l, g, c0))
                    for i in range(NT):
                        lf = lambda f, i=i: M32[:, f * T + i * 128: f * T + (i + 1) * 128]
                        lk = lambda f, i=i: mkeys(f * T + i * 128, 128)
                        bank, pc = tm_matmuls(W, wkey, i, lf, lk, nk=FFG)
                        S.add("dve", lambda v, bank=bank, pc=pc, i=i, c0=c0: v.tensor_tensor(
                            out=xres[:, i, c0:c0 + 256], in0=PS[bank][:, pc:pc + 256], in1=xres[:, i, c0:c0 + 256],
                            op=ALU.add),
                            reads=pk(bank, pc, 256) + [("x", i)], writes=[("x", i)])

        def attention(l):
            kvs = {"n": 0, "q": 0, "pt": 0}
            v_all4 = ag_all3[:, 1536:3072, :].rearrange("r a c -> r (a c)").rearrange("r (h t d) -> r h t d", h=12, t=T)

            def load_kv(H, slot):
                bi = kvs["n"] % 3
                kvs["n"] += 1
                Kb, Kbk = scr(bi * 1024, [128, BLK], BF16)
                Vb, Vbk = scr(3072 + bi * 1040, [128, KTB, 130], BF16)
                kind, r, ab = slot[0], slot[1], slot[2]
                if kind == "all":
                    ksrc = ag_all3[r, H * 128:(H + 1) * 128, ab * BLK:(ab + 1) * BLK]
                    vsrc = v_all4[r, H, ab * BLK:(ab + 1) * BLK, :].rearrange("(j p) d -> p j d", p=128)
                    rk, rv = ["ag_all"], ["ag_all"]
                else:
                    ksrc = kt_loc[H * 128:(H + 1) * 128, ab * BLK:(ab + 1) * BLK]
                    vsrc = v_loc3[H, ab * BLK:(ab + 1) * BLK, :].rearrange("(j p) d -> p j d", p=128)
                    rk = [("kt_loc", H, ab)]
                    rv = [("v_loc", H, i) for i in range(ab * KTB, (ab + 1) * KTB)]
                S.add("sp", lambda q: q.dma_start(out=Kb, in_=ksrc), reads=rk, writes=Kbk, dma=("Kb", bi))
                S.add("sp", lambda q: q.dma_start(out=Vb[:, :, 0:128], in_=vsrc), reads=rv, writes=Vbk, dma=("Vb", bi))
                return Kb, Kbk, Vb, Vbk

            def load_q(H):
                qi = kvs["q"] % 2
                kvs["q"] += 1
                Qb, Qbk = scr(6656 + qi * 2 * T, [128, T], BF16)
                S.add("sp", lambda q: q.dma_start(out=Qb, in_=q_loc[H * 128:(H + 1) * 128, :]),
                      reads=[("q_loc", H, 0), ("q_loc", H, 1)], writes=Qbk, dma=("Qb", qi))
                return Qb, Qbk

            def new_pt():
                pi = kvs["pt"] % 6
                kvs["pt"] += 1
                return scr(6656 + 4 * T + pi * 2 * BLK, [128, BLK], BF16)

            for bi in range(3):
                Vb, Vbk = scr(3072 + bi * 1040, [128, KTB, 130], BF16)
                S.add("dve", lambda v, Vb=Vb: v.memset(Vb[:, :, 128:129], 1.0), writes=Vbk)

            def slots_for(qh):
                if qh == 0:
                    sl = [("all", r, 0, r) for r in range(7)] + [("loc", -1, 0, None)]
                else:
                    sl = ([("all", r, 0, None) for r in range(8)] + [("all", r, 1, 8 + r) for r in range(1, 8)]
                          + [("loc", -1, 1, None)])
                return sl

            def acc_region(idx):
                return idx // 3, (idx % 3) * 129

            for hd in range(6):
                H = 6 + hd
                Qb, Qbk = load_q(H)
                for qh in range(2):
                    sl = slots_for(qh)
                    for ridx in range(2 * KTB):
                        bankZ, offZ = acc_region(ridx)
                        S.add("pe", lambda t, bankZ=bankZ, offZ=offZ: t.matmul(
                            PS[bankZ][:, offZ:offZ + 129], lhsT=zt[:, 0:128], rhs=zt[:, 0:129], start=True, stop=False),
                            reads=["zt"], writes=pk(bankZ))
                    for si, slot in enumerate(sl):
                        Kb, Kbk, Vb, Vbk = load_kv(H, slot)
                        diag = slot[0] == "loc"
                        bcol = slot[3]
                        for j in range(KTB):
                            q0 = j * 128 if diag else 0
                            nq = BLK - q0
                            pts = []
                            for sm in range(2):
                                bank = 3 + (j % 2) * 2 + sm
                                S.add("pe", lambda t, sm=sm, bank=bank, j=j, q0=q0, nq=nq, Kb=Kb, Qb=Qb, qh=qh: t.matmul(
                                    PS[bank][:, q0:q0 + nq], lhsT=Kb[64 * sm:64 * sm + 64, j * 128:(j + 1) * 128],
                                    rhs=Qb[64 * sm:64 * sm + 64, qh * BLK + q0: qh * BLK + BLK],
                                    start=True, stop=True, tile_position=(64 * sm, 0)),
                                    reads=Kbk + Qbk, writes=pk(bank, q0, nq))
                            for sm in range(2):
                                bank = 3 + (j % 2) * 2 + sm
                                PT, PTk = new_pt()
                                pts.append((PT, PTk))
                                if bcol is None:
                                    S.add("act", lambda a, bank=bank, PT=PT, q0=q0, nq=nq: a.activation(
                                        out=PT[:, q0:q0 + nq], in_=PS[bank][:, q0:q0 + nq], func=AF.Exp, scale=0.125),
                                        reads=pk(bank, q0, nq), writes=PTk)
                                else:
                                    S.add("act", lambda a, bank=bank, PT=PT, q0=q0, nq=nq, bcol=bcol: a.activation(
                                        out=PT[:, q0:q0 + nq], in_=PS[bank][:, q0:q0 + nq], func=AF.Exp, scale=0.125,
                                        bias=sbias[:, bcol:bcol + 1]),
                                        reads=pk(bank, q0, nq) + ["sbias"], writes=PTk)
                                if diag:
                                    S.add("pool", lambda g, PT=PT, q0=q0: g.tensor_tensor(
                                        out=PT[:, q0:q0 + 128], in0=PT[:, q0:q0 + 128], in1=tri, op=ALU.mult),
                                        reads=PTk + ["cmat"], writes=PTk)
                            for qi in range(q0 // 128, KTB):
                                for sm in range(2):
                                    PT, PTk = pts[sm]
                                    bank, off = acc_region(qi * 2 + sm)
                                    first = False
                                    last = (diag and j == qi)
                                    S.add("pe", lambda t, PT=PT, bank=bank, off=off, qi=qi, j=j, first=first, last=last, Vb=Vb: t.matmul(
                                        PS[bank][:, off:off + 129], lhsT=PT[:, qi * 128:(qi + 1) * 128], rhs=Vb[:, j, 0:129],
                                        start=first, stop=last),
                                        reads=PTk + Vbk, writes=pk(bank, off, 129))
                    for qi in range(KTB):
                        it = qh * KTB + qi
                        b0, o0 = acc_region(qi * 2)
                        b1, o1 = acc_region(qi * 2 + 1)
                        fj = qi % 2
                        o32, o32k = scr(16896 + fj * 512, [128, 128], F32)
                        fs = 80 + fj * 8
                        fk = ("fstat", fj)
                        S.add("dve", lambda v, b0=b0, o0=o0, fs=fs: v.reciprocal(out=stat[:, fs:fs + 1], in_=PS[b0][:, o0 + 128:o0 + 129]),
                              reads=pk(b0, o0, 129), writes=[fk])
                        S.add("dve", lambda v, b1=b1, o1=o1, fs=fs: v.reciprocal(out=stat[:, fs + 1:fs + 2], in_=PS[b1][:, o1 + 128:o1 + 129]),
                              reads=pk(b1, o1, 129), writes=[fk])
                        S.add("dve", lambda v, fs=fs: v.tensor_tensor(out=stat[:, fs + 1:fs + 2], in0=stat[:, fs + 1:fs + 2],
                                                                      in1=lam[:, l:l + 1], op=ALU.mult),
                              reads=[fk, ("lam", l)], writes=[fk])
                        S.add("dve", lambda v, b0=b0, o0=o0, fs=fs, o32=o32: v.tensor_scalar(
                            out=o32, in0=PS[b0][:, o0:o0 + 128], scalar1=stat[:, fs:fs + 1], scalar2=None, op0=ALU.mult),
                            reads=pk(b0, o0, 129) + [fk], writes=o32k)
                        S.add("dve", lambda v, b1=b1, o1=o1, fs=fs, o32=o32: v.scalar_tensor_tensor(
                            out=o32, in0=PS[b1][:, o1:o1 + 128], scalar=stat[:, fs + 1:fs + 2], in1=o32,
                            op0=ALU.mult, op1=ALU.add),
                            reads=pk(b1, o1, 129) + [fk] + o32k, writes=o32k)
                        jk, jkk = scr(17920 + fj * 512, [128, 128], F32)
                        S.add("dve", lambda v, o32=o32, jk=jk, fs=fs: v.scalar_tensor_tensor(
                            out=jk, in0=o32, scalar=1.0, in1=o32, op0=ALU.mult, op1=ALU.mult,
                            accum_out=stat[:, fs + 2:fs + 3]),
                            reads=o32k, writes=jkk + [fk])
                        S.add("act", lambda a, fs=fs: a.activation(out=stat[:, fs + 3:fs + 4], in_=stat[:, fs + 2:fs + 3],
                                                                   func=AF.Ln, scale=1.0 / 128, bias=EPS),
                              reads=[fk], writes=[fk])
                        S.add("act", lambda a, fs=fs: a.activation(out=stat[:, fs + 3:fs + 4], in_=stat[:, fs + 3:fs + 4],
                                                                   func=AF.Exp, scale=-0.5),
                              reads=[fk], writes=[fk])
                        e0 = it * D + 768 + hd * 128
                        S.add("dve", lambda v, o32=o32, fs=fs, e0=e0: v.scalar_tensor_tensor(
                            out=M32[:, e0:e0 + 128], in0=o32, scalar=stat[:, fs + 3:fs + 4], in1=gsub[:, l, :],
                            op0=ALU.mult, op1=ALU.mult),
                            reads=o32k + [fk, "gsub"], writes=mkeys(e0, 128))

            if stop <= 6:
                return
            ident64 = cmat[0:64, 0, 0:64]
            for hm in range(6):
                H = hm
                Qb, Qbk = load_q(H)
                for i in range(NT):
                    S.add("pe", lambda t, i=i, Qb=Qb, hm=hm: t.matmul(
                        PS[6][:, i * NBX:(i + 1) * NBX], lhsT=Qb[:, i * 128:(i + 1) * 128], rhs=kmb[:, hm, :],
                        start=True, stop=True),
                        reads=Qbk + ["kmb"], writes=pk(6, 0, 512))
                gm, gmk = scr(16896, [128, NT * NBX], F32)
                bq, bqk = scr(16896 + NT * NBX * 4, [128, NT, NBX], BF16)
                S.add("dve", lambda v, gm=gm: v.tensor_tensor(out=gm, in0=PS[6][:, 0:NT * NBX], in1=mtab[:, 0, :], op=ALU.add),
                      reads=pk(6, 0, 512) + ["mtab"], writes=gmk)
                gm3 = gm.rearrange("p (i n) -> p i n", n=NBX)
                for i in range(NT):
                    S.add("dve", lambda v, i=i, gm3=gm3: v.max(out=stat[:, 64:72], in_=gm3[:, i, 0:NMAIN]),
                          reads=gmk, writes=["top8"])
                    S.add("dve", lambda v, i=i, gm3=gm3: v.tensor_scalar(out=gm3[:, i, :], in0=gm3[:, i, :],
                                                                         scalar1=stat[:, 66:67], scalar2=None, op0=ALU.is_ge),
                          reads=gmk + ["top8"], writes=gmk)
                S.add("dve", lambda v, gm=gm: v.tensor_tensor(out=gm, in0=gm, in1=mtab[:, 1, :], op=ALU.mult),
                      reads=gmk + ["mtab"], writes=gmk)
                S.add("dve", lambda v, gm=gm, bq=bq: v.tensor_tensor(out=bq.rearrange("p i n -> p (i n)"), in0=gm,
                                                                       in1=mtab[:, 2, :], op=ALU.add),
                      reads=gmk + ["mtab"], writes=bqk)
                pv = PS[7][:].bitcast(BF16)
                for i in range(NT):
                    S.add("pe", lambda t, i=i, bq=bq: t.transpose(out=pv[0:NBX, i * 128:(i + 1) * 128], in_=bq[:, i, :],
                                                                   identity=ident),
                          reads=bqk + ["cmat"], writes=pk(7, 0, 512))
                S.add("act", lambda a: a.activation(out=biasT[0:NBX, :], in_=pv[0:NBX, 0:T], func=AF.Copy),
                      reads=pk(7, 0, 512), writes=["biasT"])

                for qh in range(2):
                    sl = slots_for(qh)
                    for ridx in range(KTB):
                        bankZ, offZ = acc_region(ridx)
                        S.add("pe", lambda t, bankZ=bankZ, offZ=offZ: t.matmul(
                            PS[bankZ][:, offZ:offZ + 129], lhsT=zt[:, 0:128], rhs=zt[:, 0:129], start=True, stop=False),
                            reads=["zt"], writes=pk(bankZ))
                    for si, slot in enumerate(sl):
                        Kb, Kbk, Vb, Vbk = load_kv(H, slot)
                        diag = slot[0] == "loc"
                        r, ab = slot[1], slot[2]
                        for j in range(KTB):
                            q0 = j * 128 if diag else 0
                            nq = BLK - q0
                            nidx = (NMAIN + ab * MB + j // 2) if diag else (r * NBL + ab * MB + j // 2)
                            bank = 2 + (kvs["pt"] % 4)
                            S.add("pe", lambda t, bank=bank, j=j, q0=q0, nq=nq, Kb=Kb, Qb=Qb, qh=qh: t.matmul(
                                PS[bank][:, q0:q0 + nq], lhsT=Kb[:, j * 128:(j + 1) * 128],
                                rhs=Qb[:, qh * BLK + q0: qh * BLK + BLK], start=True, stop=False),
                                reads=Kbk + Qbk, writes=pk(bank, q0, nq))
                            S.add("pe", lambda t, bank=bank, q0=q0, nq=nq, nidx=nidx, qh=qh: t.matmul(
                                PS[bank][:, q0:q0 + nq], lhsT=ident64[:, nidx:nidx + 1].to_broadcast([64, 128]),
                                rhs=biasT[0:64, qh * BLK + q0: qh * BLK + BLK], start=False, stop=True),
                                reads=["biasT", "cmat"], writes=pk(bank, q0, nq))
                            PT, PTk = new_pt()
                            S.add("act", lambda a, bank=bank, PT=PT, q0=q0, nq=nq: a.activation(
                                out=PT[:, q0:q0 + nq], in_=PS[bank][:, q0:q0 + nq], func=AF.Exp, scale=HD ** -0.5),
                                reads=pk(bank, q0, nq), writes=PTk)
                            if diag:
                                S.add("pool", lambda g, PT=PT, q0=q0: g.tensor_tensor(
                                    out=PT[:, q0:q0 + 128], in0=PT[:, q0:q0 + 128], in1=tri, op=ALU.mult),
                                    reads=PTk + ["cmat"], writes=PTk)
                            for qi in range(q0 // 128, KTB):
                                bankA, off = acc_region(qi)
                                first = False
                                last = (diag and j == qi)
                                S.add("pe", lambda t, PT=PT, bankA=bankA, off=off, qi=qi, j=j, first=first, last=last, Vb=Vb: t.matmul(
                                    PS[bankA][:, off:off + 129], lhsT=PT[:, qi * 128:(qi + 1) * 128], rhs=Vb[:, j, 0:129],
                                    start=first, stop=last),
                                    reads=PTk + Vbk, writes=pk(bankA, off, 129))
                    for qi in range(KTB):
                        it = qh * KTB + qi
                        b0, o0 = acc_region(qi)
                        fj = qi % 2
                        fs = 80 + fj * 8
                        fk = ("fstat", fj)
                        S.add("dve", lambda v, b0=b0, o0=o0, fs=fs: v.reciprocal(out=stat[:, fs:fs + 1], in_=PS[b0][:, o0 + 128:o0 + 129]),
                              reads=pk(b0, o0, 129), writes=[fk])
                        e0 = it * D + hm * 128
                        S.add("dve", lambda v, b0=b0, o0=o0, fs=fs, e0=e0: v.tensor_scalar(
                            out=M32[:, e0:e0 + 128], in0=PS[b0][:, o0:o0 + 128], scalar1=stat[:, fs:fs + 1], scalar2=None,
                            op0=ALU.mult),
                            reads=pk(b0, o0, 129) + [fk], writes=mkeys(e0, 128))

        S.add("dve", lambda v: v.memset(biasT[:], 0.0), writes=["biasT"])
        S.add("dve", lambda v: v.memset(kmst[:], 0.0), writes=["kmst"])
        S.add("dve", lambda v: v.memset(zt[:], 0.0), writes=["zt"])

        for l in range(depth):
            layer(l)

        finals = []
        if do_final:
            gfin = Wb[0][:].bitcast(F32)
            S.add("sp", lambda q: q.dma_start(out=gfin, in_=fng_d.partition_broadcast(128)),
                  writes=[("W", 0)], dma=("W", 0))
        for i in range(NT):
            if do_final:
                hn, hnk = scr(0, [128, D], BF16)
                S.add("act", lambda a, i=i, hn=hn: a.activation(out=hn, in_=xres[:, i, :], func=AF.Square,
                                                                 accum_out=ssq[:, i:i + 1]),
                      reads=[("x", i)], writes=hnk + [("ssq", i)])
                S.add("act", lambda a, i=i: a.activation(out=rstd[:, i:i + 1], in_=ssq[:, i:i + 1], func=AF.Ln,
                                                         scale=1.0 / D, bias=EPS),
                      reads=[("ssq", i)], writes=[("rstd", i)])
                S.add("act", lambda a, i=i: a.activation(out=rstd[:, i:i + 1], in_=rstd[:, i:i + 1], func=AF.Exp,
                                                         scale=-0.5),
                      reads=[("rstd", i)], writes=[("rstd", i)])
                S.add("dve", lambda v, i=i: v.scalar_tensor_tensor(
                    out=xres[:, i, :], in0=xres[:, i, :], scalar=rstd[:, i:i + 1], in1=gfin, op0=ALU.mult, op1=ALU.mult),
                    reads=[("x", i), ("rstd", i), ("W", 0)], writes=[("x", i)])
            finals.append(S.add("sp", lambda q, i=i: q.dma_start(out=out_d[i * 128:(i + 1) * 128, :], in_=xres[:, i, :]),
                                reads=[("x", i)], writes=[("out", i)], dma="out"))
        S.emit(final_waits=finals)
    return nc


_PROG_CACHE = {}


STOP = 99


def _get_prog(S_seq, depth, d_ff, do_final, lam_base):
    key = (S_seq, depth, d_ff, do_final, lam_base, STOP)
    if key not in _PROG_CACHE:
        _PROG_CACHE[key] = build_program(S_seq, depth, d_ff, do_final, lam_base, STOP)
    return _PROG_CACHE[key]


def run_model(inputs, S_seq, depth, d_ff, layers_per_launch=None):
    x = np.asarray(inputs["x"], np.float32)[0]
    BLK = S_seq // 16
    cm = const_mats()
    per_core_tabs = [host_tables(c, S_seq) for c in range(NCORES)]
    xs = [np.ascontiguousarray(np.concatenate([x[c * BLK:(c + 1) * BLK], x[(15 - c) * BLK:(16 - c) * BLK]], 0))
          for c in range(NCORES)]
    names = ["attn_norm_g", "w_in", "diff_lambda", "diff_subln_g", "sgu_ln_g", "sgu_ln_b", "sgu_w", "sgu_b",
             "w_out", "mlp_norm_g", "w_mlp_in", "w_mlp_out"]
    lpl = depth if layers_per_launch is None else layers_per_launch
    l0 = 0
    while l0 < depth:
        nl = min(lpl, depth - l0)
        last = (l0 + nl == depth)
        nc = _get_prog(S_seq, nl, d_ff, last, l0)
        shared = {n: np.ascontiguousarray(np.asarray(inputs[n], np.float32)[l0:l0 + nl]) for n in names}
        shared["final_norm_g"] = np.ascontiguousarray(np.asarray(inputs["final_norm_g"], np.float32))
        shared["cmat"] = cm
        in_maps = []
        for c in range(NCORES):
            rope, sbias, mtab = per_core_tabs[c]
            m = dict(shared)
            m["x"] = xs[c]
            m["rope"] = rope
            m["slotbias"] = sbias
            m["mobatab"] = mtab
            in_maps.append(m)
        import os as _os
        if _os.environ.get("KTRACE"):
            res = run_bass_kernel_spmd(nc, in_maps, core_ids=list(range(NCORES)), trace=True)
            print("EXEC_NS", res.exec_time_ns)
        else:
            res = run_bass_kernel_spmd(nc, in_maps, core_ids=list(range(NCORES)))
        xs = [np.asarray(res.results[c]["out"], np.float32) for c in range(NCORES)]
        l0 += nl
    out = np.zeros((S_seq, D), np.float32)
    for c in range(NCORES):
        out[c * BLK:(c + 1) * BLK] = xs[c][:BLK]
        out[(15 - c) * BLK:(16 - c) * BLK] = xs[c][BLK:]
    return out[None]


def kernel(**inputs):
    return run_model(inputs, 8192, 4, 8192)
```

```python
import contextlib
import math

import numpy as np

import concourse.bass as bass
import concourse.mybir as mybir
from concourse.bass_utils import run_bass_kernel_spmd

F32 = mybir.dt.float32
BF16 = mybir.dt.bfloat16
AF = mybir.ActivationFunctionType
ALU = mybir.AluOpType
AX = mybir.AxisListType

D = 2048
KD = 16
HD = 128
IN_W = 5632
OFF_MQ, OFF_MK, OFF_MV, OFF_DQ, OFF_DK, OFF_DV, OFF_SU, OFF_SV = 0, 768, 1536, 2304, 3072, 3840, 4608, 5120
EPS = 1e-6
NEG = -30000.0
NCORES = 8
NWB = 3

ENGS = ("pe", "act", "dve", "pool", "sp")


class Op:
    __slots__ = ("eng", "fn", "deps", "is_dma", "sem", "val", "inc", "sig", "gidx")

    def __init__(self, eng, fn, is_dma):
        self.eng = eng
        self.fn = fn
        self.deps = []
        self.is_dma = is_dma
        self.sem = None
        self.val = 0
        self.inc = 1
        self.sig = None
        self.gidx = -1


class Sched:
    def __init__(self, nc, stack, dummy_fns):
        self.nc = nc
        self.stack = stack
        self.order = []
        self.last_w = {}
        self.readers = {}
        self.pair_sem = {}
        self.dma_sems = {}
        self.dma_cnt = {}
        self.dummy_fns = dummy_fns

    def _pair(self, p, c):
        if (p, c) not in self.pair_sem:
            self.pair_sem[(p, c)] = self.stack.enter_context(self.nc.semaphore("s_%s_%s" % (p, c)))
        return self.pair_sem[(p, c)]

    def _dma_sem(self, key):
        if key not in self.dma_sems:
            self.dma_sems[key] = self.stack.enter_context(
                self.nc.semaphore("ds_%d" % len(self.dma_sems)))
            self.dma_cnt[key] = 0
        return self.dma_sems[key]

    def add(self, eng, fn, reads=(), writes=(), dma=None, inc=None):
        op = Op(eng, fn, dma is not None)
        op.gidx = len(self.order)
        deps = []
        for r in reads:
            w = self.last_w.get(r)
            if w is not None:
                deps.append(w)
        for k in writes:
            rd = self.readers.get(k)
            if rd and (rd[0] or rd[1]):
                deps.extend(rd[0].values())
                deps.extend(rd[1])
            else:
                w = self.last_w.get(k)
                if w is not None:
                    deps.append(w)
        best = {}
        seen = set()
        for d in deps:
            if id(d) in seen:
                continue
            seen.add(id(d))
            if d.is_dma:
                op.deps.append(d)
                continue
            if d.eng == "pe" and eng == "pe" and dma is None:
                continue
            b = best.get(d.eng)
            if b is None or b.gidx < d.gidx:
                best[d.eng] = d
        op.deps.extend(best.values())
        if dma is not None:
            op.sem = self._dma_sem(dma)
            op.inc = 16 if inc is None else inc
            self.dma_cnt[dma] += op.inc
            op.val = self.dma_cnt[dma]
        for r in reads:
            rd = self.readers.setdefault(r, [{}, []])
            if op.is_dma:
                rd[1].append(op)
            else:
                rd[0][eng] = op
        for k in writes:
            self.last_w[k] = op
            self.readers[k] = [{}, []]
        self.order.append(op)
        return op

    def emit(self, final_waits=()):
        nc = self.nc
        dependents = {}
        for op in self.order:
            for d in op.deps:
                dependents.setdefault(id(d), []).append(op)
        post = {}
        cnt = {}

        def newsig(p, c):
            cnt[(p, c)] = cnt.get((p, c), 0) + 1
            return (c, self._pair(p, c), cnt[(p, c)])

        n_dummy = 0
        for X in self.order:
            ds = dependents.get(id(X))
            if not ds:
                continue
            ds.sort(key=lambda o: o.gidx)
            by_eng = {}
            for Y in ds:
                by_eng.setdefault(Y.eng, []).append(Y)
            engs = list(by_eng)
            c1 = engs[0]
            if not X.is_dma:
                X.sig = newsig(X.eng, c1)
            if len(engs) == 1:
                continue
            if (not X.is_dma) and X.eng in ("act", "dve", "pool"):
                for c2 in engs[1:]:
                    R = Op(X.eng, self.dummy_fns[X.eng], False)
                    R.gidx = X.gidx
                    R.sig = newsig(X.eng, c2)
                    post.setdefault(id(X), []).append(R)
                    n_dummy += 1
                    for Y in by_eng[c2]:
                        Y.deps = [R if d is X else d for d in Y.deps]
            else:
                Y1 = by_eng[c1][0]
                for c2 in engs[1:]:
                    for Y in by_eng[c2]:
                        Y.deps = [Y1 if d is X else d for d in Y.deps]
                        dependents.setdefault(id(Y1), []).append(Y)
        self.n_dummy = n_dummy
        queues = {e: [] for e in ENGS}
        for op in self.order:
            queues[op.eng].append(op)
            queues[op.eng].extend(post.get(id(op), ()))
        names = {"pe": "tensor", "act": "scalar", "dve": "vector", "pool": "gpsimd", "sp": "sync"}
        with nc.Block() as block:
            for e in ENGS:
                ops = queues[e]
                fw = final_waits if e == "sp" else ()

                def body(engine, ops=ops, fw=fw, e=e):
                    waited = {}
                    for op in ops:
                        for d in op.deps:
                            if d.is_dma:
                                sem, val = d.sem, d.val
                            else:
                                assert d.sig is not None and d.sig[0] == e, (d.eng, e, d.sig)
                                sem, val = d.sig[1], d.sig[2]
                            key = id(sem)
                            if waited.get(key, 0) >= val:
                                continue
                            engine.wait_ge(sem, val)
                            waited[key] = val
                        ins = op.fn(engine)
                        if op.is_dma:
                            ins.then_inc(op.sem, op.inc)
                        elif op.sig is not None:
                            ins.then_inc(op.sig[1], 1)
                    if fw:
                        for k, sem in self.dma_sems.items():
                            engine.wait_ge(sem, self.dma_cnt[k])

                getattr(block, names[e])(body)


def host_tables(c, S_seq):
    BLK = S_seq // 16
    T = 2 * BLK
    NT = T // 128
    MB = BLK // 256
    NBL = 2 * MB
    NMAIN = 8 * NBL
    NBX = NMAIN + NBL
    gblk = [c, 15 - c]
    pos = np.concatenate([np.arange(BLK) + gblk[0] * BLK, np.arange(BLK) + gblk[1] * BLK]).astype(np.float32)
    rope = np.zeros((4, 128, T), np.float32)
    inv64 = (1.0 / (10000.0 ** (np.arange(0, 128, 2, dtype=np.float32) / 128))).astype(np.float32)
    inv32 = (1.0 / (10000.0 ** (np.arange(0, 64, 2, dtype=np.float32) / 64))).astype(np.float32)
    angM = pos[None, :] * inv64[:, None]
    angD = pos[None, :] * inv32[:, None]
    rope[0, 0:64] = np.cos(angM)
    rope[0, 64:128] = np.cos(angM)
    rope[1, 0:64] = -np.sin(angM)
    rope[1, 64:128] = np.sin(angM)
    for c2 in range(2):
        b = c2 * 64
        rope[2, b:b + 32] = np.cos(angD)
        rope[2, b + 32:b + 64] = np.cos(angD)
        rope[3, b:b + 32] = -np.sin(angD)
        rope[3, b + 32:b + 64] = np.sin(angD)
    slotbias = np.zeros((128, 16), np.float32)
    for r in range(8):
        slotbias[:, r] = 0.0 if r < c else NEG
        slotbias[:, 8 + r] = 0.0 if r > c else NEG
    gm_add = np.zeros((128, NT, NBX), np.float32)
    Pt = np.zeros((128, NT, NBX), np.float32)
    Qt = np.full((128, NT, NBX), NEG, np.float32)
    for i in range(NT):
        hq = (i * 128) // BLK
        tl = i * 128 + np.arange(128)
        gpos = gblk[hq] * BLK + (tl % BLK)
        qm = gpos // 256
        for r in range(8):
            for ab in range(2):
                for m in range(MB):
                    n = r * NBL + ab * MB + m
                    g = (r if ab == 0 else 15 - r) * MB + m
                    past = g < qm
                    gm_add[:, i, n] = np.where(past, 0.0, -1e30)
                    if r == c and ab == hq:
                        continue
                    Pt[:, i, n] = np.where(past, -NEG, 0.0)
        for ab in range(2):
            for m in range(MB):
                n = NMAIN + ab * MB + m
                if ab != hq:
                    continue
                g = gblk[ab] * MB + m
                Pt[:, i, n] = np.where(g < qm, -NEG, 0.0)
                Qt[:, i, n] = np.where(g == qm, 0.0, NEG)
    mobatab = np.stack([gm_add.reshape(128, -1), Pt.reshape(128, -1), Qt.reshape(128, -1)])
    return rope, slotbias, mobatab


def const_mats():
    ident = np.eye(128, dtype=np.float32)
    permM = np.zeros((128, 128), np.float32)
    permD = np.zeros((128, 128), np.float32)
    for m in range(128):
        permM[(m + 64) % 128, m] = 1.0
        permD[(m // 64) * 64 + ((m % 64) + 32) % 64, m] = 1.0
    tri = np.triu(np.ones((128, 128), np.float32))
    return np.stack([ident, permM, permD, tri])


def build_program(S_seq, depth, d_ff, do_final=True, lam_base=0, stop=99):
    BLK = S_seq // 16
    T = 2 * BLK
    NT = T // 128
    KTB = BLK // 128
    MB = BLK // 256
    NBL = 2 * MB
    NMAIN = 8 * NBL
    NBX = NMAIN + NBL
    NFF = d_ff // 128
    FFG = 16
    NG = NFF // FFG
    assert NFF % FFG == 0

    import os as _os
    KSUB = int(_os.environ.get("KSUB", "0"))
    nc = bass.Bass("TRN2", target_bir_lowering=False)

    def din(name, shape, dt=F32):
        return nc.dram_tensor(name, list(shape), dt, kind="ExternalInput").ap()

    x_d = din("x", [T, D])
    ang_d = din("attn_norm_g", [depth, D])
    win_d = din("w_in", [depth, D, IN_W])
    dl_d = din("diff_lambda", [depth, 4, 64])
    dsg_d = din("diff_subln_g", [depth, 128])
    slg_d = din("sgu_ln_g", [depth, 4, 128])
    slb_d = din("sgu_ln_b", [depth, 4, 128])
    sw_d = din("sgu_w", [depth, 4, 128, 128])
    sb_d = din("sgu_b", [depth, 4, 128])
    wout_d = din("w_out", [depth, D, D])
    mng_d = din("mlp_norm_g", [depth, D])
    wmi_d = din("w_mlp_in", [depth, D, d_ff])
    wmo_d = din("w_mlp_out", [depth, d_ff, D])
    fng_d = din("final_norm_g", [D])
    rope_d = din("rope", [4, 128, T])
    cmat_d = din("cmat", [4, 128, 128])
    sbias_d = din("slotbias", [128, 16])
    mtab_d = din("mobatab", [3, 128, NT * NBX])
    out_d = nc.dram_tensor("out", [T, D], F32, kind="ExternalOutput").ap()

    KMR = 6
    AGR = 3072 + KMR
    assert KMR * T // 2 == 128 * 6 * NBL
    ag_loc = nc.dram_tensor("ag_loc", [AGR, T], BF16)
    ag_all = nc.dram_tensor("ag_all", [8 * AGR, T], BF16)
    q_loc = nc.dram_tensor("q_loc", [1536, T], BF16)
    kt_loc = ag_loc
    v_loc3 = ag_loc[1536:3072, :].rearrange("r c -> (r c)").rearrange("(h t d) -> h t d", h=12, t=T)
    km_loc2 = ag_loc[3072:AGR, :].rearrange("r c -> (r c)").bitcast(F32).rearrange("(p f) -> p f", p=128)
    ag_all3 = ag_all.ap().rearrange("(r a) c -> r a c", r=8)

    with contextlib.ExitStack() as st:
        dumt = st.enter_context(nc.sbuf_tensor("dumt", [128, 392], F32))
        dctr = {"act": 0, "dve": 0, "pool": 0}

        def dcol(e):
            dctr[e] += 1
            base = {"act": 0, "dve": 128, "pool": 256}[e]
            k = base + dctr[e] % 128
            return dumt[0:1, k:k + 1]

        S = Sched(nc, st, {
            "act": lambda a: a.activation(out=dcol("act"), in_=dumt[0:1, 388:389], func=AF.Copy),
            "dve": lambda v: v.memset(dcol("dve"), 0.0),
            "pool": lambda g: g.memset(dcol("pool"), 0.0),
        })

        def sb(name, shape, dt):
            return st.enter_context(nc.sbuf_tensor(name, list(shape), dt))

        def ps(name, shape, dt):
            return st.enter_context(nc.psum_tensor(name, list(shape), dt))

        xres = sb("xres", [128, NT, D], F32)
        A32 = sb("A32", [128, KD * T], BF16)
        M32 = sb("M32", [128, NT * D], BF16)
        Wb = [sb("Wb%d" % b, [128, KD * 256], BF16) for b in range(NWB)]
        ropeT = sb("ropeT", [128, 4, T], BF16)
        vn = sb("vn", [128, NT, 512], BF16)
        SCRB = 19456
        SCR = sb("SCR", [128, SCRB // 2], BF16)
        cmat = sb("cmat_sb", [128, 4, 128], BF16)
        sbias = sb("sbias_sb", [128, 16], F32)
        mtab = sb("mtab_sb", [128, 3, NT * NBX], F32)
        gcols = sb("gcols", [128, 2 * depth, KD], F32)
        lam = sb("lam", [128, 8 * depth], F32)
        gsub = sb("gsub", [128, depth, 128], F32)
        lng = sb("lng", [128, 512], F32)
        lnb = sb("lnb", [128, 512], F32)
        wsT = sb("wsT", [128, 4, 128], BF16)
        bs_sb = sb("bs_sb", [128, 4], F32)
        biasT = sb("biasT", [64, T], BF16)
        kmg = sb("kmg", [128, 8, 6 * NBL], F32)
        kmb = sb("kmb", [128, 6, NBX], BF16)
        kmst = sb("kmst", [128, 6, NBL], F32)
        stat = sb("stat", [128, 192], F32)
        zt = sb("zt", [128, 130], BF16)
        ssq = sb("ssq", [128, NT], F32)
        rstd = sb("rstd", [128, NT], F32)

        ident = cmat[:, 0, :]
        permM = cmat[:, 1, :]
        permD = cmat[:, 2, :]
        tri = cmat[:, 3, :]

        PS = [ps("PS%d" % b, [128, 512], F32) for b in range(8)]

        def pk(bank, c0=0, n=512):
            return [("P", bank)]

        def scr(off, shape, dt):
            esz = 4 if dt == F32 else 2
            n = 1
            for s_ in shape[1:]:
                n *= s_
            nb = n * esz
            assert off % 4 == 0 and off + nb <= SCRB, (off, nb)
            flat = SCR[0:shape[0], off // 2:(off + nb) // 2]
            if dt == F32:
                flat = flat.bitcast(F32)
            if len(shape) == 3:
                flat = flat.rearrange("p (a b) -> p a b", a=shape[1])
            keys = [("SCR", k) for k in range(off // 512, (off + nb - 1) // 512 + 1)]
            return flat, keys

        def akeys(kt, t0, n):
            return [("A", kt, i) for i in range(t0 // 128, (t0 + n - 1) // 128 + 1)]

        def mkeys(e0, n):
            return [("M", e) for e in range(e0 // 1024, (e0 + n - 1) // 1024 + 1)]

        def Aap(kt, t0, n):
            return A32[:, kt * T + t0: kt * T + t0 + n]

        wplan = []
        for l in range(depth):
            for c0 in (list(range(OFF_MK, OFF_MK + 768, 256)) + list(range(OFF_DK, OFF_DK + 768, 256))
                       + list(range(OFF_MV, OFF_MV + 768, 256)) + list(range(OFF_DV, OFF_DV + 768, 256))
                       + list(range(OFF_SV, OFF_SV + 512, 256)) + list(range(OFF_SU, OFF_SU + 512, 256))
                       + list(range(OFF_DQ, OFF_DQ + 768, 256)) + list(range(OFF_MQ, OFF_MQ + 768, 256))):
                wplan.append(("in", l, c0))
            for c0 in range(0, D, 256):
                wplan.append(("out", l, c0))
            for g in range(NG):
                for j in range(FFG // 2):
                    wplan.append(("mi", l, (g * FFG + 2 * j) * 128))
                for c0 in range(0, D, 256):
                    wplan.append(("mo", l, g, c0))
        wstate = {"dma": 0, "use": 0}

        def w_src(desc):
            kind = desc[0]
            if kind == "in":
                return win_d[desc[1]].rearrange("(k p) n -> p k n", p=128)[:, :, desc[2]:desc[2] + 256]
            if kind == "out":
                return wout_d[desc[1]].rearrange("(k p) n -> p k n", p=128)[:, :, desc[2]:desc[2] + 256]
            if kind == "mi":
                return wmi_d[desc[1]].rearrange("(k p) n -> p k n", p=128)[:, :, desc[2]:desc[2] + 256]
            g, c0 = desc[2], desc[3]
            return wmo_d[desc[1]].rearrange("(k p) n -> p k n", p=128)[:, g * FFG:(g + 1) * FFG, c0:c0 + 256]

        def w_issue():
            n = wstate["dma"]
            if n >= len(wplan):
                return
            b = n % NWB
            src = w_src(wplan[n])
            dst = Wb[b][:].rearrange("p (k n) -> p k n", k=KD)
            S.add("pool", lambda g, dst=dst, src=src: g.dma_start(out=dst, in_=src),
                  writes=[("W", b)], dma=("W", b))
            wstate["dma"] = n + 1

        def w_acquire(desc):
            n = wstate["use"]
            assert wplan[n] == desc, (wplan[n], desc)
            while wstate["dma"] < min(len(wplan), n + NWB):
                w_issue()
            wstate["use"] = n + 1
            b = n % NWB
            return Wb[b][:].rearrange("p (k n) -> p k n", k=KD), ("W", b)

        lp, lpk = scr(0, [128, depth * 256], F32)
        S.add("dve", lambda v: v.memset(dumt[:], 0.0), writes=["dumt"])
        for i in range(NT):
            S.add("sp", lambda q, i=i: q.dma_start(out=xres[:, i, :], in_=x_d[i * 128:(i + 1) * 128, :]),
                  writes=[("x", i)], dma=("xld", i))
        S.add("pool", lambda g: g.dma_start(out=cmat[:], in_=cmat_d.rearrange("a p n -> p a n")),
              writes=["cmat"], dma="cmat")
        S.add("pool", lambda g: g.dma_start(out=ropeT[:], in_=rope_d.rearrange("a p n -> p a n")),
              writes=["rope"], dma="rope")
        S.add("sp", lambda q: q.dma_start(out=sbias[:], in_=sbias_d), writes=["sbias"], dma="sbias")
        S.add("sp", lambda q: q.dma_start(out=mtab[:], in_=mtab_d.rearrange("a p n -> p a n")),
              writes=["mtab"], dma="mtab")

        def ld_gcols(q):
            with nc.allow_non_contiguous_dma(reason="tiny strided gain load"):
                return q.dma_start(out=gcols[:, 0:depth, :], in_=ang_d.rearrange("l (k p) -> p l k", p=128))

        def ld_gcols2(q):
            with nc.allow_non_contiguous_dma(reason="tiny strided gain load"):
                return q.dma_start(out=gcols[:, depth:2 * depth, :], in_=mng_d.rearrange("l (k p) -> p l k", p=128))

        S.add("sp", ld_gcols, writes=["gcols0"], dma="gcols0")
        S.add("sp", ld_gcols2, writes=["gcols1"], dma="gcols1")
        S.add("sp", lambda q: q.dma_start(out=lp, in_=dl_d.rearrange("l a b -> (l a b)").partition_broadcast(128)),
              writes=lpk, dma="lp")
        S.add("sp", lambda q: q.dma_start(out=gsub[:].rearrange("p l n -> p (l n)"),
                                          in_=dsg_d.rearrange("l n -> (l n)").partition_broadcast(128)),
              writes=["gsub"], dma="gsub")
        for l in range(depth):
            lam_init = 0.8 - 0.6 * math.exp(-0.3 * (l + lam_base))
            for pr in range(2):
                a0 = l * 256 + pr * 128
                S.add("dve", lambda v, a0=a0, l=l, pr=pr: v.scalar_tensor_tensor(
                    out=stat[:, 128:192], in0=lp[:, a0:a0 + 64], scalar=1.0, in1=lp[:, a0 + 64:a0 + 128],
                    op0=ALU.mult, op1=ALU.mult, accum_out=lam[:, 4 * depth + 2 * l + pr:4 * depth + 2 * l + pr + 1]),
                    reads=lpk, writes=["stat", ("lam", l)])
            S.add("act", lambda a, l=l: a.activation(out=lam[:, 6 * depth + 2 * l:6 * depth + 2 * l + 2],
                                                     in_=lam[:, 4 * depth + 2 * l:4 * depth + 2 * l + 2], func=AF.Exp),
                  reads=[("lam", l)], writes=[("lam", l)])
            S.add("dve", lambda v, l=l, li=lam_init: v.scalar_tensor_tensor(
                out=lam[:, l:l + 1], in0=lam[:, 6 * depth + 2 * l + 1:6 * depth + 2 * l + 2], scalar=-li,
                in1=lam[:, 6 * depth + 2 * l:6 * depth + 2 * l + 1], op0=ALU.add, op1=ALU.subtract),
                reads=[("lam", l)], writes=[("lam", l)])
            S.add("dve", lambda v, l=l, li=lam_init: v.tensor_scalar(
                out=gsub[:, l, :], in0=gsub[:, l, :], scalar1=1.0 - li, scalar2=None, op0=ALU.mult),
                reads=["gsub"], writes=["gsub"])

        tp_state = {"g": 0}

        def rmsnorm_to_A(gi):
            for i in range(NT):
                hn, hnk = scr(0, [128, D], BF16)
                S.add("act", lambda a, i=i, hn=hn: a.activation(out=hn, in_=xres[:, i, :], func=AF.Square,
                                                                 accum_out=ssq[:, i:i + 1]),
                      reads=[("x", i)], writes=hnk + [("ssq", i)])
                S.add("act", lambda a, i=i: a.activation(out=rstd[:, i:i + 1], in_=ssq[:, i:i + 1], func=AF.Ln,
                                                         scale=1.0 / D, bias=EPS),
                      reads=[("ssq", i)], writes=[("rstd", i)])
                S.add("act", lambda a, i=i: a.activation(out=rstd[:, i:i + 1], in_=rstd[:, i:i + 1], func=AF.Exp,
                                                         scale=-0.5),
                      reads=[("rstd", i)], writes=[("rstd", i)])
                S.add("dve", lambda v, i=i, hn=hn: v.tensor_scalar(out=hn, in0=xres[:, i, :], scalar1=rstd[:, i:i + 1],
                                                                   scalar2=None, op0=ALU.mult),
                      reads=[("x", i), ("rstd", i)], writes=hnk)
                for k0 in range(0, KD, 4):
                    g = tp_state["g"] % 2
                    tp_state["g"] += 1
                    pv = PS[6 + g][:].bitcast(BF16)
                    for kk in range(4):
                        kt = k0 + kk
                        S.add("pe", lambda t, kt=kt, kk=kk, hn=hn, pv=pv: t.transpose(
                            out=pv[:, kk * 128:(kk + 1) * 128],
                            in_=hn[:, kt * 128:(kt + 1) * 128], identity=ident),
                            reads=hnk + ["cmat"], writes=pk(6 + g))
                    for kk in range(4):
                        kt = k0 + kk
                        src = pv[:, kk * 128:(kk + 1) * 128]
                        dst = Aap(kt, i * 128, 128)
                        if g == 0:
                            S.add("act", lambda a, src=src, dst=dst, kt=kt: a.activation(
                                out=dst, in_=src, func=AF.Copy, scale=gcols[:, gi, kt:kt + 1]),
                                reads=pk(6 + g) + ["gcols0", "gcols1"],
                                writes=akeys(kt, i * 128, 128))
                        else:
                            S.add("dve", lambda v, src=src, dst=dst, kt=kt: v.tensor_scalar(
                                out=dst, in0=src, scalar1=gcols[:, gi, kt:kt + 1], scalar2=None, op0=ALU.mult),
                                reads=pk(6 + g) + ["gcols0", "gcols1"],
                                writes=akeys(kt, i * 128, 128))

        def mix_to_A():
            for i in range(NT):
                for k0 in range(0, KD, 4):
                    g = tp_state["g"] % 2
                    tp_state["g"] += 1
                    pv = PS[6 + g][:].bitcast(BF16)
                    for kk in range(4):
                        kt = k0 + kk
                        e0 = i * D + kt * 128
                        S.add("pe", lambda t, kk=kk, e0=e0, pv=pv: t.transpose(
                            out=pv[:, kk * 128:(kk + 1) * 128],
                            in_=M32[:, e0:e0 + 128], identity=ident),
                            reads=mkeys(e0, 128) + ["cmat"], writes=pk(6 + g))
                    for kk in range(4):
                        kt = k0 + kk
                        src = pv[:, kk * 128:(kk + 1) * 128]
                        dst = Aap(kt, i * 128, 128)
                        eng = "act" if g == 0 else "dve"
                        if eng == "act":
                            S.add("act", lambda a, src=src, dst=dst: a.activation(out=dst, in_=src, func=AF.Copy),
                                  reads=pk(6 + g), writes=akeys(kt, i * 128, 128))
                        else:
                            S.add("dve", lambda v, src=src, dst=dst: v.tensor_copy(out=dst, in_=src),
                                  reads=pk(6 + g), writes=akeys(kt, i * 128, 128))

        fm_state = {"n": 0, "sw": 0, "tm": 0, "stg": 0}

        def fm_matmuls(W, wkey, ch, half):
            bank = fm_state["n"] % 4
            fm_state["n"] += 1
            for kt in range(KD):
                S.add("pe", lambda t, kt=kt, bank=bank: t.matmul(
                    PS[bank][:, 0:BLK], lhsT=W[:, kt, ch * 128:(ch + 1) * 128], rhs=Aap(kt, half * BLK, BLK),
                    start=(kt == 0), stop=(kt == KD - 1)),
                    reads=[wkey] + akeys(kt, half * BLK, BLK), writes=pk(bank, 0, BLK))
            return bank

        def tm_matmuls(W, wkey, i, lhs_fn, lhs_keys_fn, nk=KD):
            r = fm_state["tm"] % 3
            fm_state["tm"] += 1
            bank, c0 = 4 + r, 0
            for kt in range(nk):
                S.add("pe", lambda t, kt=kt, bank=bank, c0=c0: t.matmul(
                    PS[bank][:, c0:c0 + 256], lhsT=lhs_fn(kt), rhs=W[:, kt, :],
                    start=(kt == 0), stop=(kt == nk - 1)),
                    reads=[wkey] + lhs_keys_fn(kt), writes=pk(bank, c0, 256))
            return bank, c0

        def rope_chunk(bank, half, is_moba, want_f32):
            j = fm_state["sw"] % 2
            fm_state["sw"] += 1
            xb, xbk = scr(j * 1024, [128, BLK], BF16)
            t1, t1k = scr(2048 + j * 2048, [128, BLK], F32)
            t2, t2k = scr(6144 + j * 2048, [128, BLK], F32)
            rb, rbk = scr(10240 + j * 1024, [128, BLK], BF16)
            swb = 4 + j
            perm = permM if is_moba else permD
            ci, si = (0, 1) if is_moba else (2, 3)
            t0 = half * BLK
            S.add("dve", lambda v: v.tensor_copy(out=xb, in_=PS[bank][:, 0:BLK]),
                  reads=pk(bank, 0, BLK), writes=xbk)
            S.add("pe", lambda t: t.matmul(PS[swb][:, 0:BLK], lhsT=perm, rhs=xb, start=True, stop=True),
                  reads=xbk + ["cmat"], writes=pk(swb, 0, BLK))
            S.add("dve", lambda v: v.tensor_tensor(out=t1, in0=PS[bank][:, 0:BLK], in1=ropeT[:, ci, t0:t0 + BLK],
                                                   op=ALU.mult),
                  reads=pk(bank, 0, BLK) + ["rope"], writes=t1k)
            S.add("dve", lambda v: v.tensor_tensor(out=t2, in0=PS[swb][:, 0:BLK], in1=ropeT[:, si, t0:t0 + BLK],
                                                   op=ALU.mult),
                  reads=pk(swb, 0, BLK) + ["rope"], writes=t2k)
            if want_f32:
                S.add("pool", lambda g: g.tensor_tensor(out=t1, in0=t1, in1=t2, op=ALU.add),
                      reads=t1k + t2k, writes=t1k)
                S.add("pool", lambda g: g.tensor_copy(out=rb, in_=t1), reads=t1k, writes=rbk)
            else:
                S.add("pool", lambda g: g.tensor_tensor(out=rb, in0=t1, in1=t2, op=ALU.add),
                      reads=t1k + t2k, writes=rbk)
            return rb, rbk, t1, t1k, j

        def layer(l):
            S.add("sp", lambda q: q.dma_start(out=lng[:], in_=slg_d[l].rearrange("g n -> (g n)").partition_broadcast(128)),
                  writes=["lng"], dma="lng")
            S.add("sp", lambda q: q.dma_start(out=lnb[:], in_=slb_d[l].rearrange("g n -> (g n)").partition_broadcast(128)),
                  writes=["lnb"], dma="lnb")

            def ld_bs(q):
                with nc.allow_non_contiguous_dma(reason="tiny"):
                    return q.dma_start(out=bs_sb[:], in_=sb_d[l].rearrange("g t -> t g"))
            S.add("sp", ld_bs, writes=["bs"], dma="bs")
            wst, wstk = scr(16384, [128, 4, 128], BF16)
            S.add("pool", lambda g: g.dma_start(out=wst, in_=sw_d[l].rearrange("g t s -> t g s")),
                  writes=wstk, dma="wst")
            pv = PS[7][:].bitcast(BF16)
            for g4 in range(4):
                S.add("pe", lambda t, g4=g4: t.transpose(out=pv[:, g4 * 128:(g4 + 1) * 128], in_=wst[:, g4, :],
                                                          identity=ident),
                      reads=wstk + ["cmat"], writes=pk(7))
            for g4 in range(4):
                S.add("dve", lambda v, g4=g4: v.tensor_tensor(out=wsT[:, g4, :], in0=pv[:, g4 * 128:(g4 + 1) * 128],
                                                             in1=tri, op=ALU.mult),
                      reads=pk(7) + ["cmat"], writes=["wsT"])

            if stop <= 0:
                return
            rmsnorm_to_A(l)
            if stop <= 1:
                return

            def store_fm(rb, rbk, j, dram, dname, H, half):
                S.add("sp", lambda q: q.dma_start(out=dram[H * 128:(H + 1) * 128, half * BLK:(half + 1) * BLK], in_=rb),
                      reads=rbk, writes=[(dname, H, half)], dma=("rb", j))

            for base, is_moba in ((OFF_MK, True), (OFF_DK, False)):
                for c0 in range(base, base + 768, 256):
                    W, wkey = w_acquire(("in", l, c0))
                    for ch in range(2):
                        hh = (c0 - base) // 128 + ch
                        H = hh if is_moba else 6 + hh
                        for half in range(2):
                            bank = fm_matmuls(W, wkey, ch, half)
                            if KSUB == 1:
                                continue
                            rb, rbk, t1, t1k, j = rope_chunk(bank, half, is_moba, want_f32=is_moba)
                            if KSUB == 2:
                                continue
                            if is_moba:
                                S.add("dve", lambda v, t1=t1, hh=hh, half=half: v.tensor_reduce(
                                    out=kmst[:, hh, half * MB:(half + 1) * MB],
                                    in_=t1.rearrange("p (a b) -> p a b", a=MB), axis=AX.X, op=ALU.add),
                                    reads=t1k, writes=["kmst"])
                            store_fm(rb, rbk, j, kt_loc, "kt_loc", H, half)
                if is_moba and KSUB in (0, 4):
                    S.add("dve", lambda v: v.tensor_scalar(out=kmst[:], in0=kmst[:], scalar1=1.0 / 256, scalar2=None,
                                                           op0=ALU.mult),
                          reads=["kmst"], writes=["kmst"])
                    S.add("sp", lambda q: q.dma_start(out=km_loc2, in_=kmst[:].rearrange("p h b -> p (h b)")),
                          reads=["kmst"], writes=["km_loc"], dma="km_loc")
            if stop <= 2:
                return
            def lhsA(i):
                return (lambda kt: Aap(kt, i * 128, 128)), (lambda kt: akeys(kt, i * 128, 128))

            for base, hoff in ((OFF_MV, 0), (OFF_DV, 6)):
                for c0 in range(base, base + 768, 256):
                    W, wkey = w_acquire(("in", l, c0))
                    h0 = hoff + (c0 - base) // 128
                    for i in range(NT):
                        lf, lk = lhsA(i)
                        bank, pc = tm_matmuls(W, wkey, i, lf, lk)
                        j = fm_state["stg"] % 2
                        fm_state["stg"] += 1
                        vst, vstk = scr(12288 + j * 512, [128, 256], BF16)
                        S.add("act", lambda a, bank=bank, pc=pc, vst=vst: a.activation(
                            out=vst, in_=PS[bank][:, pc:pc + 256], func=AF.Copy),
                            reads=pk(bank, pc, 256), writes=vstk)
                        dst = v_loc3[h0:h0 + 2, i * 128:(i + 1) * 128, :].rearrange("h t d -> t h d")
                        S.add("sp", lambda q, dst=dst, vst=vst: q.dma_start(
                            out=dst, in_=vst.rearrange("p (h d) -> p h d", h=2)),
                            reads=vstk, writes=[("v_loc", h0, i), ("v_loc", h0 + 1, i)], dma=("vst", j))
            S.add("pool", lambda g: g.collective_compute(
                "AllGather", ALU.bypass, replica_groups=[list(range(NCORES))],
                ins=[ag_loc.ap().opt()], outs=[ag_all.ap().opt()]),
                reads=[("v_loc", H, i) for H in range(12) for i in range(NT)] + ["km_loc"]
                + [("kt_loc", H, hf) for H in range(12) for hf in range(2)], writes=["ag_all"], dma="cc_ag", inc=1)
            km_src = ag_all3[:, 3072:AGR, :].rearrange("r a c -> r (a c)").bitcast(F32).rearrange("r (p f) -> p r f", p=128)
            S.add("sp", lambda q: q.dma_start(out=kmg[:], in_=km_src),
                  reads=["ag_all"], writes=["kmg"], dma="kmg")
            for h in range(6):
                S.add("dve", lambda v, h=h: v.tensor_copy(
                    out=kmb[:, h, 0:NMAIN].rearrange("p (r b) -> p r b", b=NBL),
                    in_=kmg[:, :, h * NBL:(h + 1) * NBL]),
                    reads=["kmg"], writes=["kmb"])
            S.add("dve", lambda v: v.tensor_copy(out=kmb[:, :, NMAIN:NBX], in_=kmst[:]),
                  reads=["kmst"], writes=["kmb"])

            if stop <= 3:
                return
            for c0 in range(OFF_SV, OFF_SV + 512, 256):
                W, wkey = w_acquire(("in", l, c0))
                g0 = (c0 - OFF_SV) // 128
                for i in range(NT):
                    lf, lk = lhsA(i)
                    bank, pc = tm_matmuls(W, wkey, i, lf, lk)
                    j = fm_state["stg"] % 2
                    fm_state["stg"] += 1
                    gv, gvk = scr(13312 + j * 1024, [128, 256], F32)
                    S.add("act", lambda a, bank=bank, pc=pc, gv=gv: a.activation(
                        out=gv, in_=PS[bank][:, pc:pc + 256], func=AF.Gelu_apprx_tanh),
                        reads=pk(bank, pc, 256), writes=gvk)
                    sk = ("stat", j)
                    so = j * 32
                    for gg in range(2):
                        S.add("dve", lambda v, gv=gv, gg=gg, so=so: v.bn_stats(
                            out=stat[:, so + gg * 6: so + gg * 6 + 6], in_=gv[:, gg * 128:(gg + 1) * 128]),
                            reads=gvk, writes=[sk])
                        S.add("dve", lambda v, gg=gg, so=so: v.bn_aggr(
                            out=stat[:, so + 12 + gg * 2: so + 14 + gg * 2], in_=stat[:, so + gg * 6: so + gg * 6 + 6]),
                            reads=[sk], writes=[sk])
                    varv = stat[:, so + 12: so + 16].rearrange("p (g t) -> p g t", t=2)[:, :, 1:2]
                    rsv = stat[:, so + 16: so + 18].rearrange("p (g t) -> p g t", t=1)
                    S.add("act", lambda a, varv=varv, rsv=rsv: a.activation(out=rsv, in_=varv, func=AF.Ln, bias=EPS),
                          reads=[sk], writes=[sk])
                    S.add("act", lambda a, rsv=rsv: a.activation(out=rsv, in_=rsv, func=AF.Exp, scale=-0.5),
                          reads=[sk], writes=[sk])
                    for gg in range(2):
                        S.add("dve", lambda v, gv=gv, gg=gg, so=so: v.tensor_scalar(
                            out=gv[:, gg * 128:(gg + 1) * 128], in0=gv[:, gg * 128:(gg + 1) * 128],
                            scalar1=stat[:, so + 12 + gg * 2: so + 13 + gg * 2],
                            scalar2=stat[:, so + 16 + gg: so + 17 + gg], op0=ALU.subtract, op1=ALU.mult),
                            reads=gvk + [sk], writes=gvk)
                    S.add("dve", lambda v, gv=gv, g0=g0: v.tensor_tensor(
                        out=gv, in0=gv, in1=lng[:, g0 * 128: g0 * 128 + 256], op=ALU.mult),
                        reads=gvk + ["lng"], writes=gvk)
                    S.add("pool", lambda g, gv=gv, g0=g0, i=i: g.tensor_tensor(
                        out=vn[:, i, g0 * 128: g0 * 128 + 256], in0=gv, in1=lnb[:, g0 * 128: g0 * 128 + 256], op=ALU.add),
                        reads=gvk + ["lnb"], writes=[("vn", i, g0 // 2)])

            for c0 in range(OFF_SU, OFF_SU + 512, 256):
                W, wkey = w_acquire(("in", l, c0))
                g0 = (c0 - OFF_SU) // 128
                for i in range(NT):
                    lf, lk = lhsA(i)
                    bank, pc = tm_matmuls(W, wkey, i, lf, lk)
                    j = fm_state["stg"] % 2
                    fm_state["stg"] += 1
                    u, uk = scr(15360 + j * 512, [128, 256], BF16)
                    S.add("act", lambda a, bank=bank, pc=pc, u=u: a.activation(
                        out=u, in_=PS[bank][:, pc:pc + 256], func=AF.Gelu_apprx_tanh),
                        reads=pk(bank, pc, 256), writes=uk)
                    for gg in range(2):
                        g4 = g0 + gg
                        qd = gg
                        S.add("pe", lambda t, g4=g4, i=i, qd=qd: t.matmul(
                            PS[7][:, qd * 128:(qd + 1) * 128], lhsT=wsT[:, g4, :], rhs=vn[:, i, g4 * 128:(g4 + 1) * 128],
                            start=True, stop=True),
                            reads=["wsT", ("vn", i, g0 // 2)], writes=pk(7))
                        e0 = i * D + 1536 + g4 * 128
                        S.add("dve", lambda v, g4=g4, gg=gg, qd=qd, e0=e0, u=u: v.scalar_tensor_tensor(
                            out=M32[:, e0:e0 + 128], in0=PS[7][:, qd * 128:(qd + 1) * 128], scalar=bs_sb[:, g4:g4 + 1],
                            in1=u[:, gg * 128:(gg + 1) * 128], op0=ALU.add, op1=ALU.mult),
                            reads=pk(7) + ["bs"] + uk, writes=mkeys(e0, 128))

            if stop <= 4:
                return
            for base, is_moba in ((OFF_DQ, False), (OFF_MQ, True)):
                for c0 in range(base, base + 768, 256):
                    W, wkey = w_acquire(("in", l, c0))
                    for ch in range(2):
                        hh = (c0 - base) // 128 + ch
                        H = hh if is_moba else 6 + hh
                        for half in range(2):
                            bank = fm_matmuls(W, wkey, ch, half)
                            rb, rbk, t1, t1k, j = rope_chunk(bank, half, is_moba, want_f32=False)
                            store_fm(rb, rbk, j, q_loc, "q_loc", H, half)

            if stop <= 5:
                return
            attention(l)
            if stop <= 7:
                return

            mix_to_A()
            for c0 in range(0, D, 256):
                W, wkey = w_acquire(("out", l, c0))
                for i in range(NT):
                    lf, lk = lhsA(i)
                    bank, pc = tm_matmuls(W, wkey, i, lf, lk)
                    S.add("dve", lambda v, bank=bank, pc=pc, i=i, c0=c0: v.tensor_tensor(
                        out=xres[:, i, c0:c0 + 256], in0=PS[bank][:, pc:pc + 256], in1=xres[:, i, c0:c0 + 256], op=ALU.add),
                        reads=pk(bank, pc, 256) + [("x", i)], writes=[("x", i)])

            if stop <= 8:
                return
            rmsnorm_to_A(depth + l)
            rl_state = 0
            for g in range(NG):
                for jj in range(FFG // 2):
                    W, wkey = w_acquire(("mi", l, (g * FFG + 2 * jj) * 128))
                    for ch in range(2):
                        f = 2 * jj + ch
                        for half in range(2):
                            bank = fm_matmuls(W, wkey, ch, half)
                            rj = rl_state % 2
                            rl_state += 1
                            rl, rlk = scr(4096 + rj * 2048, [128, BLK], F32)
                            S.add("act", lambda a, bank=bank, rl=rl: a.activation(out=rl, in_=PS[bank][:, 0:BLK], func=AF.Relu),
                                  reads=pk(bank, 0, BLK), writes=rlk)
                            e0 = f * T + half * BLK
                            S.add("pool", lambda gq, rl=rl, e0=e0: gq.tensor_tensor(
                                out=M32[:, e0:e0 + BLK], in0=rl, in1=rl, op=ALU.mult),
                                reads=rlk, writes=mkeys(e0, BLK))
                for c0 in range(0, D, 256):
                    W, wkey = w_acquire(("mo", l, g, c0))
                    for i in range(NT):
                        lf = lambda f, i=i: M32[:, f * T + i * 128: f * T + (i + 1) * 128]
                        lk = lambda f, i=i: mkeys(f * T + i * 128, 128)
                        bank, pc = tm_matmuls(W, wkey, i, lf, lk, nk=FFG)
                        S.add("dve", lambda v, bank=bank, pc=pc, i=i, c0=c0: v.tensor_tensor(
                            out=xres[:, i, c0:c0 + 256], in0=PS[bank][:, pc:pc + 256], in1=xres[:, i, c0:c0 + 256],
                            op=ALU.add),
                            reads=pk(bank, pc, 256) + [("x", i)], writes=[("x", i)])

        def attention(l):
            kvs = {"n": 0, "q": 0, "pt": 0}
            v_all4 = ag_all3[:, 1536:3072, :].rearrange("r a c -> r (a c)").rearrange("r (h t d) -> r h t d", h=12, t=T)

            def load_kv(H, slot):
                bi = kvs["n"] % 3
                kvs["n"] += 1
                Kb, Kbk = scr(bi * 1024, [128, BLK], BF16)
                Vb, Vbk = scr(3072 + bi * 1040, [128, KTB, 130], BF16)
                kind, r, ab = slot[0], slot[1], slot[2]
                if kind == "all":
                    ksrc = ag_all3[r, H * 128:(H + 1) * 128, ab * BLK:(ab + 1) * BLK]
                    vsrc = v_all4[r, H, ab * BLK:(ab + 1) * BLK, :].rearrange("(j p) d -> p j d", p=128)
                    rk, rv = ["ag_all"], ["ag_all"]
                else:
                    ksrc = kt_loc[H * 128:(H + 1) * 128, ab * BLK:(ab + 1) * BLK]
                    vsrc = v_loc3[H, ab * BLK:(ab + 1) * BLK, :].rearrange("(j p) d -> p j d", p=128)
                    rk = [("kt_loc", H, ab)]
                    rv = [("v_loc", H, i) for i in range(ab * KTB, (ab + 1) * KTB)]
                S.add("sp", lambda q: q.dma_start(out=Kb, in_=ksrc), reads=rk, writes=Kbk, dma=("Kb", bi))
                S.add("sp", lambda q: q.dma_start(out=Vb[:, :, 0:128], in_=vsrc), reads=rv, writes=Vbk, dma=("Vb", bi))
                return Kb, Kbk, Vb, Vbk

            def load_q(H):
                qi = kvs["q"] % 2
                kvs["q"] += 1
                Qb, Qbk = scr(6656 + qi * 2 * T, [128, T], BF16)
                S.add("sp", lambda q: q.dma_start(out=Qb, in_=q_loc[H * 128:(H + 1) * 128, :]),
                      reads=[("q_loc", H, 0), ("q_loc", H, 1)], writes=Qbk, dma=("Qb", qi))
                return Qb, Qbk

            def new_pt():
                pi = kvs["pt"] % 6
                kvs["pt"] += 1
                return scr(6656 + 4 * T + pi * 2 * BLK, [128, BLK], BF16)

            for bi in range(3):
                Vb, Vbk = scr(3072 + bi * 1040, [128, KTB, 130], BF16)
                S.add("dve", lambda v, Vb=Vb: v.memset(Vb[:, :, 128:129], 1.0), writes=Vbk)

            def slots_for(qh):
                if qh == 0:
                    sl = [("all", r, 0, r) for r in range(7)] + [("loc", -1, 0, None)]
                else:
                    sl = ([("all", r, 0, None) for r in range(8)] + [("all", r, 1, 8 + r) for r in range(1, 8)]
                          + [("loc", -1, 1, None)])
                return sl

            def acc_region(idx):
                return idx // 3, (idx % 3) * 129

            for hd in range(6):
                H = 6 + hd
                Qb, Qbk = load_q(H)
                for qh in range(2):
                    sl = slots_for(qh)
                    for ridx in range(2 * KTB):
                        bankZ, offZ = acc_region(ridx)
                        S.add("pe", lambda t, bankZ=bankZ, offZ=offZ: t.matmul(
                            PS[bankZ][:, offZ:offZ + 129], lhsT=zt[:, 0:128], rhs=zt[:, 0:129], start=True, stop=False),
                            reads=["zt"], writes=pk(bankZ))
                    for si, slot in enumerate(sl):
                        Kb, Kbk, Vb, Vbk = load_kv(H, slot)
                        diag = slot[0] == "loc"
                        bcol = slot[3]
                        for j in range(KTB):
                            q0 = j * 128 if diag else 0
                            nq = BLK - q0
                            pts = []
                            for sm in range(2):
                                bank = 3 + (j % 2) * 2 + sm
                                S.add("pe", lambda t, sm=sm, bank=bank, j=j, q0=q0, nq=nq, Kb=Kb, Qb=Qb, qh=qh: t.matmul(
                                    PS[bank][:, q0:q0 + nq], lhsT=Kb[64 * sm:64 * sm + 64, j * 128:(j + 1) * 128],
                                    rhs=Qb[64 * sm:64 * sm + 64, qh * BLK + q0: qh * BLK + BLK],
                                    start=True, stop=True, tile_position=(64 * sm, 0)),
                                    reads=Kbk + Qbk, writes=pk(bank, q0, nq))
                            for sm in range(2):
                                bank = 3 + (j % 2) * 2 + sm
                                PT, PTk = new_pt()
                                pts.append((PT, PTk))
                                if (not diag) and BLK == 512:
                                    for hf in range(2):
                                        h0 = hf * 256
                                        if bcol is None:
                                            S.add("act", lambda a, bank=bank, PT=PT, h0=h0: a.activation(
                                                out=PT[:, h0:h0 + 256], in_=PS[bank][:, h0:h0 + 256], func=AF.Exp, scale=0.125),
                                                reads=pk(bank), writes=[PTk[hf]])
                                        else:
                                            S.add("act", lambda a, bank=bank, PT=PT, h0=h0, bcol=bcol: a.activation(
                                                out=PT[:, h0:h0 + 256], in_=PS[bank][:, h0:h0 + 256], func=AF.Exp, scale=0.125,
                                                bias=sbias[:, bcol:bcol + 1]),
                                                reads=pk(bank) + ["sbias"], writes=[PTk[hf]])
                                elif bcol is None:
                                    S.add("act", lambda a, bank=bank, PT=PT, q0=q0, nq=nq: a.activation(
                                        out=PT[:, q0:q0 + nq], in_=PS[bank][:, q0:q0 + nq], func=AF.Exp, scale=0.125),
                                        reads=pk(bank, q0, nq), writes=PTk)
                                else:
                                    S.add("act", lambda a, bank=bank, PT=PT, q0=q0, nq=nq, bcol=bcol: a.activation(
                                        out=PT[:, q0:q0 + nq], in_=PS[bank][:, q0:q0 + nq], func=AF.Exp, scale=0.125,
                                        bias=sbias[:, bcol:bcol + 1]),
                                        reads=pk(bank, q0, nq) + ["sbias"], writes=PTk)
                                if diag:
                                    S.add("pool", lambda g, PT=PT, q0=q0: g.tensor_tensor(
                                        out=PT[:, q0:q0 + 128], in0=PT[:, q0:q0 + 128], in1=tri, op=ALU.mult),
                                        reads=PTk + ["cmat"], writes=PTk)
                            for sm in range(2):
                                for qi in range(q0 // 128, KTB):
                                    PT, PTk = pts[sm]
                                    bank, off = acc_region(qi * 2 + sm)
                                    first = False
                                    last = (diag and j == qi)
                                    S.add("pe", lambda t, PT=PT, bank=bank, off=off, qi=qi, j=j, first=first, last=last, Vb=Vb: t.matmul(
                                        PS[bank][:, off:off + 129], lhsT=PT[:, qi * 128:(qi + 1) * 128], rhs=Vb[:, j, 0:129],
                                        start=first, stop=last),
                                        reads=(PTk if (diag or BLK != 512) else [PTk[qi // 2]]) + Vbk, writes=pk(bank, off, 129))
                    for qi in range(KTB):
                        it = qh * KTB + qi
                        b0, o0 = acc_region(qi * 2)
                        b1, o1 = acc_region(qi * 2 + 1)
                        fj = qi % 2
                        o32, o32k = scr(16896 + fj * 512, [128, 128], F32)
                        fs = 80 + fj * 8
                        fk = ("fstat", fj)
                        S.add("dve", lambda v, b0=b0, o0=o0, fs=fs: v.reciprocal(out=stat[:, fs:fs + 1], in_=PS[b0][:, o0 + 128:o0 + 129]),
                              reads=pk(b0, o0, 129), writes=[fk])
                        S.add("dve", lambda v, b1=b1, o1=o1, fs=fs: v.reciprocal(out=stat[:, fs + 1:fs + 2], in_=PS[b1][:, o1 + 128:o1 + 129]),
                              reads=pk(b1, o1, 129), writes=[fk])
                        S.add("dve", lambda v, fs=fs: v.tensor_tensor(out=stat[:, fs + 1:fs + 2], in0=stat[:, fs + 1:fs + 2],
                                                                      in1=lam[:, l:l + 1], op=ALU.mult),
                              reads=[fk, ("lam", l)], writes=[fk])
                        S.add("dve", lambda v, b0=b0, o0=o0, fs=fs, o32=o32: v.tensor_scalar(
                            out=o32, in0=PS[b0][:, o0:o0 + 128], scalar1=stat[:, fs:fs + 1], scalar2=None, op0=ALU.mult),
                            reads=pk(b0, o0, 129) + [fk], writes=o32k)
                        S.add("dve", lambda v, b1=b1, o1=o1, fs=fs, o32=o32: v.scalar_tensor_tensor(
                            out=o32, in0=PS[b1][:, o1:o1 + 128], scalar=stat[:, fs + 1:fs + 2], in1=o32,
                            op0=ALU.mult, op1=ALU.add),
                            reads=pk(b1, o1, 129) + [fk] + o32k, writes=o32k)
                        jk, jkk = scr(17920 + fj * 512, [128, 128], F32)
                        S.add("dve", lambda v, o32=o32, jk=jk, fs=fs: v.scalar_tensor_tensor(
                            out=jk, in0=o32, scalar=1.0, in1=o32, op0=ALU.mult, op1=ALU.mult,
                            accum_out=stat[:, fs + 2:fs + 3]),
                            reads=o32k, writes=jkk + [fk])
                        S.add("act", lambda a, fs=fs: a.activation(out=stat[:, fs + 3:fs + 4], in_=stat[:, fs + 2:fs + 3],
                                                                   func=AF.Ln, scale=1.0 / 128, bias=EPS),
                              reads=[fk], writes=[fk])
                        S.add("act", lambda a, fs=fs: a.activation(out=stat[:, fs + 3:fs + 4], in_=stat[:, fs + 3:fs + 4],
                                                                   func=AF.Exp, scale=-0.5),
                              reads=[fk], writes=[fk])
                        e0 = it * D + 768 + hd * 128
                        S.add("dve", lambda v, o32=o32, fs=fs, e0=e0: v.scalar_tensor_tensor(
                            out=M32[:, e0:e0 + 128], in0=o32, scalar=stat[:, fs + 3:fs + 4], in1=gsub[:, l, :],
                            op0=ALU.mult, op1=ALU.mult),
                            reads=o32k + [fk, "gsub"], writes=mkeys(e0, 128))

            if stop <= 6:
                return
            ident64 = cmat[0:64, 0, 0:64]
            for hm in range(6):
                H = hm
                Qb, Qbk = load_q(H)
                for i in range(NT):
                    S.add("pe", lambda t, i=i, Qb=Qb, hm=hm: t.matmul(
                        PS[6][:, i * NBX:(i + 1) * NBX], lhsT=Qb[:, i * 128:(i + 1) * 128], rhs=kmb[:, hm, :],
                        start=True, stop=True),
                        reads=Qbk + ["kmb"], writes=pk(6, 0, 512))
                gm, gmk = scr(16896, [128, NT * NBX], F32)
                bq, bqk = scr(16896 + NT * NBX * 4, [128, NT, NBX], BF16)
                S.add("dve", lambda v, gm=gm: v.tensor_tensor(out=gm, in0=PS[6][:, 0:NT * NBX], in1=mtab[:, 0, :], op=ALU.add),
                      reads=pk(6, 0, 512) + ["mtab"], writes=gmk)
                gm3 = gm.rearrange("p (i n) -> p i n", n=NBX)
                for i in range(NT):
                    S.add("dve", lambda v, i=i, gm3=gm3: v.max(out=stat[:, 64:72], in_=gm3[:, i, 0:NMAIN]),
                          reads=gmk, writes=["top8"])
                    S.add("dve", lambda v, i=i, gm3=gm3: v.tensor_scalar(out=gm3[:, i, :], in0=gm3[:, i, :],
                                                                         scalar1=stat[:, 66:67], scalar2=None, op0=ALU.is_ge),
                          reads=gmk + ["top8"], writes=gmk)
                S.add("dve", lambda v, gm=gm: v.tensor_tensor(out=gm, in0=gm, in1=mtab[:, 1, :], op=ALU.mult),
                      reads=gmk + ["mtab"], writes=gmk)
                S.add("dve", lambda v, gm=gm, bq=bq: v.tensor_tensor(out=bq.rearrange("p i n -> p (i n)"), in0=gm,
                                                                       in1=mtab[:, 2, :], op=ALU.add),
                      reads=gmk + ["mtab"], writes=bqk)
                pv = PS[7][:].bitcast(BF16)
                for i in range(NT):
                    S.add("pe", lambda t, i=i, bq=bq: t.transpose(out=pv[0:NBX, i * 128:(i + 1) * 128], in_=bq[:, i, :],
                                                                   identity=ident),
                          reads=bqk + ["cmat"], writes=pk(7, 0, 512))
                S.add("act", lambda a: a.activation(out=biasT[0:NBX, :], in_=pv[0:NBX, 0:T], func=AF.Copy),
                      reads=pk(7, 0, 512), writes=["biasT"])

                for qh in range(2):
                    sl = slots_for(qh)
                    for ridx in range(KTB):
                        bankZ, offZ = acc_region(ridx)
                        S.add("pe", lambda t, bankZ=bankZ, offZ=offZ: t.matmul(
                            PS[bankZ][:, offZ:offZ + 129], lhsT=zt[:, 0:128], rhs=zt[:, 0:129], start=True, stop=False),
                            reads=["zt"], writes=pk(bankZ))
                    for si, slot in enumerate(sl):
                        Kb, Kbk, Vb, Vbk = load_kv(H, slot)
                        diag = slot[0] == "loc"
                        r, ab = slot[1], slot[2]
                        for j in range(KTB):
                            q0 = j * 128 if diag else 0
                            nq = BLK - q0
                            nidx = (NMAIN + ab * MB + j // 2) if diag else (r * NBL + ab * MB + j // 2)
                            bank = 2 + (kvs["pt"] % 4)
                            S.add("pe", lambda t, bank=bank, j=j, q0=q0, nq=nq, Kb=Kb, Qb=Qb, qh=qh: t.matmul(
                                PS[bank][:, q0:q0 + nq], lhsT=Kb[:, j * 128:(j + 1) * 128],
                                rhs=Qb[:, qh * BLK + q0: qh * BLK + BLK], start=True, stop=False),
                                reads=Kbk + Qbk, writes=pk(bank, q0, nq))
                            S.add("pe", lambda t, bank=bank, q0=q0, nq=nq, nidx=nidx, qh=qh: t.matmul(
                                PS[bank][:, q0:q0 + nq], lhsT=ident64[:, nidx:nidx + 1].to_broadcast([64, 128]),
                                rhs=biasT[0:64, qh * BLK + q0: qh * BLK + BLK], start=False, stop=True),
                                reads=["biasT", "cmat"], writes=pk(bank, q0, nq))
                            PT, PTk = new_pt()
                            if (not diag) and BLK == 512:
                                for hf in range(2):
                                    h0 = hf * 256
                                    S.add("act", lambda a, bank=bank, PT=PT, h0=h0: a.activation(
                                        out=PT[:, h0:h0 + 256], in_=PS[bank][:, h0:h0 + 256], func=AF.Exp, scale=HD ** -0.5),
                                        reads=pk(bank), writes=[PTk[hf]])
                            else:
                                S.add("act", lambda a, bank=bank, PT=PT, q0=q0, nq=nq: a.activation(
                                    out=PT[:, q0:q0 + nq], in_=PS[bank][:, q0:q0 + nq], func=AF.Exp, scale=HD ** -0.5),
                                    reads=pk(bank, q0, nq), writes=PTk)
                            if diag:
                                S.add("pool", lambda g, PT=PT, q0=q0: g.tensor_tensor(
                                    out=PT[:, q0:q0 + 128], in0=PT[:, q0:q0 + 128], in1=tri, op=ALU.mult),
                                    reads=PTk + ["cmat"], writes=PTk)
                            for qi in range(q0 // 128, KTB):
                                bankA, off = acc_region(qi)
                                first = False
                                last = (diag and j == qi)
                                S.add("pe", lambda t, PT=PT, bankA=bankA, off=off, qi=qi, j=j, first=first, last=last, Vb=Vb: t.matmul(
                                    PS[bankA][:, off:off + 129], lhsT=PT[:, qi * 128:(qi + 1) * 128], rhs=Vb[:, j, 0:129],
                                    start=first, stop=last),
                                    reads=(PTk if (diag or BLK != 512) else [PTk[qi // 2]]) + Vbk, writes=pk(bankA, off, 129))
                    for qi in range(KTB):
                        it = qh * KTB + qi
                        b0, o0 = acc_region(qi)
                        fj = qi % 2
                        fs = 80 + fj * 8
                        fk = ("fstat", fj)
                        S.add("dve", lambda v, b0=b0, o0=o0, fs=fs: v.reciprocal(out=stat[:, fs:fs + 1], in_=PS[b0][:, o0 + 128:o0 + 129]),
                              reads=pk(b0, o0, 129), writes=[fk])
                        e0 = it * D + hm * 128
                        S.add("dve", lambda v, b0=b0, o0=o0, fs=fs, e0=e0: v.tensor_scalar(
                            out=M32[:, e0:e0 + 128], in0=PS[b0][:, o0:o0 + 128], scalar1=stat[:, fs:fs + 1], scalar2=None,
                            op0=ALU.mult),
                            reads=pk(b0, o0, 129) + [fk], writes=mkeys(e0, 128))

        S.add("dve", lambda v: v.memset(biasT[:], 0.0), writes=["biasT"])
        S.add("dve", lambda v: v.memset(kmst[:], 0.0), writes=["kmst"])
        S.add("dve", lambda v: v.memset(zt[:], 0.0), writes=["zt"])

        for l in range(depth):
            layer(l)

        finals = []
        if do_final:
            gfin = Wb[0][:].bitcast(F32)
            S.add("sp", lambda q: q.dma_start(out=gfin, in_=fng_d.partition_broadcast(128)),
                  writes=[("W", 0)], dma=("W", 0))
        for i in range(NT):
            if do_final:
                hn, hnk = scr(0, [128, D], BF16)
                S.add("act", lambda a, i=i, hn=hn: a.activation(out=hn, in_=xres[:, i, :], func=AF.Square,
                                                                 accum_out=ssq[:, i:i + 1]),
                      reads=[("x", i)], writes=hnk + [("ssq", i)])
                S.add("act", lambda a, i=i: a.activation(out=rstd[:, i:i + 1], in_=ssq[:, i:i + 1], func=AF.Ln,
                                                         scale=1.0 / D, bias=EPS),
                      reads=[("ssq", i)], writes=[("rstd", i)])
                S.add("act", lambda a, i=i: a.activation(out=rstd[:, i:i + 1], in_=rstd[:, i:i + 1], func=AF.Exp,
                                                         scale=-0.5),
                      reads=[("rstd", i)], writes=[("rstd", i)])
                S.add("dve", lambda v, i=i: v.scalar_tensor_tensor(
                    out=xres[:, i, :], in0=xres[:, i, :], scalar=rstd[:, i:i + 1], in1=gfin, op0=ALU.mult, op1=ALU.mult),
                    reads=[("x", i), ("rstd", i), ("W", 0)], writes=[("x", i)])
            finals.append(S.add("sp", lambda q, i=i: q.dma_start(out=out_d[i * 128:(i + 1) * 128, :], in_=xres[:, i, :]),
                                reads=[("x", i)], writes=[("out", i)], dma="out"))
        S.emit(final_waits=finals)
    return nc


_PROG_CACHE = {}


STOP = 99


def _get_prog(S_seq, depth, d_ff, do_final, lam_base):
    key = (S_seq, depth, d_ff, do_final, lam_base, STOP)
    if key not in _PROG_CACHE:
        _PROG_CACHE[key] = build_program(S_seq, depth, d_ff, do_final, lam_base, STOP)
    return _PROG_CACHE[key]


def run_model(inputs, S_seq, depth, d_ff, layers_per_launch=None):
    x = np.asarray(inputs["x"], np.float32)[0]
    BLK = S_seq // 16
    cm = const_mats()
    per_core_tabs = [host_tables(c, S_seq) for c in range(NCORES)]
    xs = [np.ascontiguousarray(np.concatenate([x[c * BLK:(c + 1) * BLK], x[(15 - c) * BLK:(16 - c) * BLK]], 0))
          for c in range(NCORES)]
    names = ["attn_norm_g", "w_in", "diff_lambda", "diff_subln_g", "sgu_ln_g", "sgu_ln_b", "sgu_w", "sgu_b",
             "w_out", "mlp_norm_g", "w_mlp_in", "w_mlp_out"]
    lpl = depth if layers_per_launch is None else layers_per_launch
    l0 = 0
    while l0 < depth:
        nl = min(lpl, depth - l0)
        last = (l0 + nl == depth)
        nc = _get_prog(S_seq, nl, d_ff, last, l0)
        shared = {n: np.ascontiguousarray(np.asarray(inputs[n], np.float32)[l0:l0 + nl]) for n in names}
        shared["final_norm_g"] = np.ascontiguousarray(np.asarray(inputs["final_norm_g"], np.float32))
        shared["cmat"] = cm
        in_maps = []
        for c in range(NCORES):
            rope, sbias, mtab = per_core_tabs[c]
            m = dict(shared)
            m["x"] = xs[c]
            m["rope"] = rope
            m["slotbias"] = sbias
            m["mobatab"] = mtab
            in_maps.append(m)
        import os as _os
        if _os.environ.get("KTRACE"):
            res = run_bass_kernel_spmd(nc, in_maps, core_ids=list(range(NCORES)), trace=True)
            print("EXEC_NS", res.exec_time_ns)
        else:
            res = run_bass_kernel_spmd(nc, in_maps, core_ids=list(range(NCORES)))
        xs = [np.asarray(res.results[c]["out"], np.float32) for c in range(NCORES)]
        l0 += nl
    out = np.zeros((S_seq, D), np.float32)
    for c in range(NCORES):
        out[c * BLK:(c + 1) * BLK] = xs[c][:BLK]
        out[(15 - c) * BLK:(16 - c) * BLK] = xs[c][BLK:]
    return out[None]


def kernel(**inputs):
    return run_model(inputs, 8192, 4, 8192)
```
